# Optimizing a Trainium2 kernel written in Bass

```python
import math
import jax, jax.numpy as jnp
from jax import lax
import numpy as np

D_MODEL = 1024
BATCH = 4
SEQ = 4096
DEPTH = 1
DEC_BATCH = 128
DEC_SEQ = 8
PAST_LEN = 2048
PAGE_SIZE = 128

HEAD_DIM = 64
N_DIFF_HEADS = 4
N_NSA_HEADS = 8
N_NSA_KV_HEADS = 2
NSA_GROUP = N_NSA_HEADS // N_NSA_KV_HEADS
CMP_BLOCK = 32
CMP_STRIDE = 16
CMP_HIDDEN = 128
SEL_BLOCK = 64
SEL_TOP_N = 16
WINDOW = 512
D_FF = 4 * D_MODEL
ROPE_THETA = 10000.0
NORM_EPS = 1e-6
DIFF_WIDTH = N_DIFF_HEADS * 2 * HEAD_DIM
NSA_WIDTH = N_NSA_HEADS * HEAD_DIM
MIX_WIDTH = DIFF_WIDTH + NSA_WIDTH
SPLIT_SIZES = (DIFF_WIDTH, DIFF_WIDTH, DIFF_WIDTH, NSA_WIDTH,
               6 * N_NSA_KV_HEADS * HEAD_DIM, 3 * N_NSA_HEADS)
D_IN = sum(SPLIT_SIZES)
NEG = -1e9
BIG = 1e9

kernel_name = 'hymba_diffattn_nsa_decode_step'


def rms_norm(x, g):
    xf = x.astype(jnp.float32)
    y = xf * lax.rsqrt(jnp.mean(xf * xf, axis=-1, keepdims=True) + NORM_EPS)
    return (y * g.astype(jnp.float32)).astype(x.dtype)


def rope(x, pos):
    half = x.shape[-1] // 2
    inv = ROPE_THETA ** (-jnp.arange(half, dtype=jnp.float32) / half)
    ang = pos.astype(jnp.float32)[:, None] * inv[None, :]
    shp = (1, pos.shape[0]) + (1,) * (x.ndim - 3) + (half,)
    cos, sin = jnp.cos(ang).reshape(shp), jnp.sin(ang).reshape(shp)
    x1 = x[..., :half].astype(jnp.float32)
    x2 = x[..., half:].astype(jnp.float32)
    return jnp.concatenate([x1 * cos - x2 * sin, x1 * sin + x2 * cos], axis=-1).astype(x.dtype)


def masked_softmax(s, mask):
    s = jnp.where(mask, s.astype(jnp.float32), NEG)
    m = jnp.max(s, axis=-1, keepdims=True)
    p = jnp.where(mask, jnp.exp(s - m), 0.0)
    return p / jnp.maximum(jnp.sum(p, axis=-1, keepdims=True), 1e-30)


def to_qblocks(a, qb):
    b, s = a.shape[:2]
    return a.reshape((b, s // qb, qb) + a.shape[2:]).swapaxes(0, 1)


def from_qblocks(o):
    nb, b, qb = o.shape[:3]
    return o.swapaxes(0, 1).reshape((b, nb * qb) + o.shape[3:])


def diff_attention(q, k, v, lam, lam_init, subln_g, q0):
    b, sq = q.shape[:2]
    t = k.shape[1]
    qb = math.gcd(sq, 128)
    kpos = jnp.arange(t)
    scale = HEAD_DIM ** -0.5

    def block(args):
        qblk, j = args
        tpos = q0 + j * qb + jnp.arange(qb)
        s = jnp.einsum('bqhcd,bkhcd->bhcqk', qblk, k) * scale
        p = masked_softmax(s, kpos[None, :] <= tpos[:, None])
        a = p[:, :, 0] - lam * p[:, :, 1]
        return jnp.einsum('bhqk,bkhe->bqhe', a.astype(v.dtype), v)

    o = from_qblocks(lax.map(block, (to_qblocks(q, qb), jnp.arange(sq // qb))))
    o = rms_norm(o, subln_g) * (1.0 - lam_init)
    return o.reshape(b, sq, DIFF_WIDTH)


def compress(rows, pos_emb, w1, w2):
    t = rows.shape[1]
    nc = (t - CMP_BLOCK) // CMP_STRIDE + 1
    idx = jnp.arange(nc)[:, None] * CMP_STRIDE + jnp.arange(CMP_BLOCK)[None, :]
    blk = rows[:, idx] + pos_emb[None, None, :, None, :]
    h = jax.nn.gelu(jnp.einsum('bnlhd,ldf->bnhf', blk, w1.reshape(CMP_BLOCK, HEAD_DIM, CMP_HIDDEN)))
    return jnp.einsum('bnhf,fd->bnhd', h, w2)


def nsa_attention(q, gates, k_cmp, v_cmp, k_slc, v_slc, k_win, v_win, q0):
    b, sq = q.shape[:2]
    t = k_slc.shape[1]
    nc = k_cmp.shape[1]
    ns = -(-t // SEL_BLOCK)
    n_sel = min(SEL_TOP_N, ns)
    scale = HEAD_DIM ** -0.5
    q = q.reshape(b, sq, N_NSA_KV_HEADS, NSA_GROUP, HEAD_DIM)
    gates = gates.reshape(b, sq, N_NSA_KV_HEADS, NSA_GROUP, 3)
    cs = np.arange(nc) * CMP_STRIDE
    ss = np.arange(ns) * SEL_BLOCK
    overlap = jnp.asarray(((cs[:, None] < ss[None, :] + SEL_BLOCK)
                           & (cs[:, None] + CMP_BLOCK > ss[None, :])).astype(np.float32))
    cmp_end = jnp.arange(nc) * CMP_STRIDE + CMP_BLOCK - 1
    pad = ns * SEL_BLOCK - t

    def sel_blocks(a):
        a = jnp.pad(a, ((0, 0), (0, pad), (0, 0), (0, 0)))
        return a.reshape(b, ns, SEL_BLOCK, N_NSA_KV_HEADS, HEAD_DIM).transpose(0, 3, 1, 2, 4)

    ks_b, vs_b = sel_blocks(k_slc), sel_blocks(v_slc)
    bi = jnp.arange(b)[:, None, None, None]
    hi = jnp.arange(N_NSA_KV_HEADS)[None, :, None, None]
    blk_ids = jnp.arange(ns)
    qb = math.gcd(sq, max(1, 256 // b))

    def block(args):
        qblk, gblk, j = args
        tpos = q0 + j * qb + jnp.arange(qb)
        s_c = jnp.einsum('bqhgd,bnhd->bhgqn', qblk, k_cmp) * scale
        p_c = masked_softmax(s_c, cmp_end[None, :] <= tpos[:, None])
        o_c = jnp.einsum('bhgqn,bnhd->bqhgd', p_c.astype(v_cmp.dtype), v_cmp)
        p_sel = jnp.einsum('bhgqn,nm->bhqm', p_c, overlap)
        cur = tpos // SEL_BLOCK
        future = blk_ids[None, :] > cur[:, None]
        forced = (blk_ids[None, :] == 0) | (blk_ids[None, :] == cur[:, None]) | (blk_ids[None, :] == cur[:, None] - 1)
        score = jnp.where(future, NEG, jnp.where(forced, BIG, p_sel))
        _, idx = lax.top_k(score, n_sel)
        kg = ks_b[bi, hi, idx]
        vg = vs_b[bi, hi, idx]
        kpos = idx[..., None] * SEL_BLOCK + jnp.arange(SEL_BLOCK)
        smask = (kpos <= tpos[None, None, :, None, None]).reshape(b, N_NSA_KV_HEADS, 1, qb, n_sel * SEL_BLOCK)
        s_s = jnp.einsum('bqhgd,bhqnld->bhgqnl', qblk, kg) * scale
        p_s = masked_softmax(s_s.reshape(b, N_NSA_KV_HEADS, NSA_GROUP, qb, n_sel * SEL_BLOCK), smask)
        p_s = p_s.reshape(b, N_NSA_KV_HEADS, NSA_GROUP, qb, n_sel, SEL_BLOCK)
        o_s = jnp.einsum('bhgqnl,bhqnld->bqhgd', p_s.astype(vg.dtype), vg)
        kw = lax.dynamic_slice_in_dim(k_win, j * qb, WINDOW + qb, axis=1)
        vw = lax.dynamic_slice_in_dim(v_win, j * qb, WINDOW + qb, axis=1)
        wpos = q0 - WINDOW + j * qb + jnp.arange(WINDOW + qb)
        dist = tpos[:, None] - wpos[None, :]
        wmask = (wpos[None, :] >= 0) & (dist >= 0) & (dist < WINDOW)
        s_w = jnp.einsum('bqhgd,bkhd->bhgqk', qblk, kw) * scale
        p_w = masked_softmax(s_w, wmask)
        o_w = jnp.einsum('bhgqk,bkhd->bqhgd', p_w.astype(vw.dtype), vw)
        return gblk[..., 0:1] * o_c + gblk[..., 1:2] * o_s + gblk[..., 2:3] * o_w

    o = lax.map(block, (to_qblocks(q, qb), to_qblocks(gates, qb), jnp.arange(sq // qb)))
    return from_qblocks(o).reshape(b, sq, NSA_WIDTH)


def decoder_layer(x, past_diff, past_nsa, win_buf, layer, w_in, w_out, w_up, w_down,
                  g_pre_mix, g_post_mix, g_pre_mlp, g_post_mlp,
                  lam_q1, lam_k1, lam_q2, lam_k2, diff_subln, cmp_pos, cmp_w1, cmp_w2):
    b, s, _ = x.shape
    q0 = past_diff.shape[1]
    pos = q0 + jnp.arange(s)
    h = rms_norm(x, g_pre_mix)
    proj = h @ w_in
    offs = [int(o) for o in np.cumsum(SPLIT_SIZES)[:-1]]
    q_d, k_d, v_d, q_n, kv_n, g_n = jnp.split(proj, offs, axis=-1)

    q_d = rope(q_d.reshape(b, s, N_DIFF_HEADS, 2, HEAD_DIM), pos)
    k_d = rope(k_d.reshape(b, s, N_DIFF_HEADS, 2, HEAD_DIM), pos)
    v_d = v_d.reshape(b, s, N_DIFF_HEADS, 2 * HEAD_DIM)
    new_diff = jnp.stack([k_d.reshape(b, s, N_DIFF_HEADS, 2 * HEAD_DIM), v_d], axis=2)
    diff_kv = jnp.concatenate([past_diff, new_diff], axis=1)
    t = diff_kv.shape[1]
    lam_init = 0.8 - 0.6 * math.exp(-0.3 * layer)
    lam = (jnp.exp(jnp.sum(lam_q1.astype(jnp.float32) * lam_k1.astype(jnp.float32)))
           - jnp.exp(jnp.sum(lam_q2.astype(jnp.float32) * lam_k2.astype(jnp.float32))) + lam_init)
    o_diff = diff_attention(q_d, diff_kv[:, :, 0].reshape(b, t, N_DIFF_HEADS, 2, HEAD_DIM),
                            diff_kv[:, :, 1], lam, lam_init, diff_subln, q0)

    q_n = rope(q_n.reshape(b, s, N_NSA_HEADS, HEAD_DIM), pos)
    kv_n = kv_n.reshape(b, s, 6, N_NSA_KV_HEADS, HEAD_DIM)
    k_rows = rope(kv_n[:, :, 0::2], pos)
    kv_n = jnp.stack([k_rows, kv_n[:, :, 1::2]], axis=3).reshape(b, s, 6, N_NSA_KV_HEADS, HEAD_DIM)
    new_nsa = kv_n[:, :, :4]
    nsa_all = jnp.concatenate([past_nsa, new_nsa], axis=1)
    k_cmp = compress(nsa_all[:, :, 0], cmp_pos[0], cmp_w1[0], cmp_w2[0])
    v_cmp = compress(nsa_all[:, :, 1], cmp_pos[1], cmp_w1[1], cmp_w2[1])
    wb = win_buf.shape[1]
    win_all = jnp.concatenate([jnp.zeros((b, WINDOW - wb) + win_buf.shape[2:], win_buf.dtype),
                               win_buf, kv_n[:, :, 4:]], axis=1)
    new_win = win_all[:, win_all.shape[1] - min(WINDOW, q0 + s):]
    gates = jax.nn.sigmoid(g_n.astype(jnp.float32)).astype(x.dtype).reshape(b, s, N_NSA_HEADS, 3)
    o_nsa = nsa_attention(q_n, gates, k_cmp, v_cmp, nsa_all[:, :, 2], nsa_all[:, :, 3],
                          win_all[:, :, 0], win_all[:, :, 1], q0)

    mix = jnp.concatenate([o_diff, o_nsa], axis=-1) @ w_out
    x = x + rms_norm(mix, g_post_mix)
    hm = rms_norm(x, g_pre_mlp)
    ff = jnp.square(jax.nn.relu(hm @ w_up)) @ w_down
    x = x + rms_norm(ff, g_post_mlp)
    return x, new_diff, new_nsa, new_win


def setup_inputs(seed: int = 0) -> dict:
    key = jax.random.key(seed)
    ks = jax.random.split(key, 24)
    n_pages = PAST_LEN // PAGE_SIZE
    n_used = DEC_BATCH * n_pages
    n_phys = (5 * n_used + 3) // 4
    win_rows = min(WINDOW, PAST_LEN)
    f32 = jnp.float32
    nrm = lambda k, shp, sc: sc * jax.random.normal(k, shp, f32)
    gain = lambda k, shp: 1.0 + 0.02 * jax.random.normal(k, shp, f32)
    return {
        'x_prompt': nrm(ks[0], (BATCH, SEQ, D_MODEL), 1.0),
        'x_sample': nrm(ks[1], (DEC_BATCH, DEC_SEQ, D_MODEL), 1.0),
        'cache_diff_kv': nrm(ks[2], (DEPTH, n_phys, PAGE_SIZE, 2, N_DIFF_HEADS, 2 * HEAD_DIM), 1.0),
        'cache_nsa_kv': nrm(ks[3], (DEPTH, n_phys, PAGE_SIZE, 4, N_NSA_KV_HEADS, HEAD_DIM), 1.0),
        'state_nsa_win_kv': nrm(ks[4], (DEPTH, DEC_BATCH, win_rows, 2, N_NSA_KV_HEADS, HEAD_DIM), 1.0),
        'page_table': jax.random.permutation(ks[5], n_phys)[:n_used].reshape(DEC_BATCH, n_pages).astype(jnp.int32),
        'w_in': nrm(ks[6], (DEPTH, D_MODEL, D_IN), D_MODEL ** -0.5),
        'w_out': nrm(ks[7], (DEPTH, MIX_WIDTH, D_MODEL), MIX_WIDTH ** -0.5),
        'w_up': nrm(ks[8], (DEPTH, D_MODEL, D_FF), D_MODEL ** -0.5),
        'w_down': nrm(ks[9], (DEPTH, D_FF, D_MODEL), D_FF ** -0.5),
        'g_pre_mix': gain(ks[10], (DEPTH, D_MODEL)),
        'g_post_mix': gain(ks[11], (DEPTH, D_MODEL)),
        'g_pre_mlp': gain(ks[12], (DEPTH, D_MODEL)),
        'g_post_mlp': gain(ks[13], (DEPTH, D_MODEL)),
        'lam_q1': nrm(ks[14], (DEPTH, HEAD_DIM), 0.1),
        'lam_k1': nrm(ks[15], (DEPTH, HEAD_DIM), 0.1),
        'lam_q2': nrm(ks[16], (DEPTH, HEAD_DIM), 0.1),
        'lam_k2': nrm(ks[17], (DEPTH, HEAD_DIM), 0.1),
        'diff_subln': gain(ks[18], (DEPTH, 2 * HEAD_DIM)),
        'cmp_pos': nrm(ks[19], (DEPTH, 2, CMP_BLOCK, HEAD_DIM), 0.1),
        'cmp_w1': nrm(ks[20], (DEPTH, 2, CMP_BLOCK * HEAD_DIM, CMP_HIDDEN), (CMP_BLOCK * HEAD_DIM) ** -0.5),
        'cmp_w2': nrm(ks[21], (DEPTH, 2, CMP_HIDDEN, HEAD_DIM), CMP_HIDDEN ** -0.5),
    }


def reference(x_prompt, x_sample, cache_diff_kv, cache_nsa_kv, state_nsa_win_kv, page_table,
              w_in, w_out, w_up, w_down, g_pre_mix, g_post_mix, g_pre_mlp, g_post_mlp,
              lam_q1, lam_k1, lam_q2, lam_k2, diff_subln, cmp_pos, cmp_w1, cmp_w2):
    b = x_prompt.shape[0]
    db = x_sample.shape[0]
    past_len = page_table.shape[1] * PAGE_SIZE
    yp, ys = x_prompt, x_sample
    p_diff, p_nsa, p_win, s_diff, s_nsa, s_win = [], [], [], [], [], []
    for l in range(DEPTH):
        prm = (w_in[l], w_out[l], w_up[l], w_down[l], g_pre_mix[l], g_post_mix[l], g_pre_mlp[l], g_post_mlp[l],
               lam_q1[l], lam_k1[l], lam_q2[l], lam_k2[l], diff_subln[l], cmp_pos[l], cmp_w1[l], cmp_w2[l])
        yp, pd, pn, pw = decoder_layer(
            yp, jnp.zeros((b, 0) + cache_diff_kv.shape[3:], cache_diff_kv.dtype),
            jnp.zeros((b, 0) + cache_nsa_kv.shape[3:], cache_nsa_kv.dtype),
            jnp.zeros((b, 0) + state_nsa_win_kv.shape[3:], state_nsa_win_kv.dtype), l, *prm)
        past_diff = cache_diff_kv[l, page_table].reshape((db, past_len) + cache_diff_kv.shape[3:])
        past_nsa = cache_nsa_kv[l, page_table].reshape((db, past_len) + cache_nsa_kv.shape[3:])
        ys, sd, sn, sw = decoder_layer(ys, past_diff, past_nsa, state_nsa_win_kv[l], l, *prm)
        p_diff.append(pd); p_nsa.append(pn); p_win.append(pw)
        s_diff.append(sd); s_nsa.append(sn); s_win.append(sw)
    return (yp, ys, jnp.stack(p_diff), jnp.stack(p_nsa), jnp.stack(p_win),
            jnp.stack(s_diff), jnp.stack(s_nsa), jnp.stack(s_win))
```

```python
import contextlib
import math

import numpy as np
import ml_dtypes

import concourse.bass as bass
import concourse.mybir as mybir
from concourse.bass_utils import run_bass_kernel_spmd

F32 = mybir.dt.float32
BF16 = mybir.dt.bfloat16
I32 = mybir.dt.int32
ALU = mybir.AluOpType
AF = mybir.ActivationFunctionType

D_MODEL = 1024
SEQ = 4096
NT_ALL = SEQ // 128
NT_OWN = NT_ALL // 2
D_IN = 2840
D_FF = 4096
EPS = 1e-6
LAM_INIT = 0.8 - 0.6 * math.exp(-0.3 * 0)
N_CMP_P = 255
N_CMP_S = 127
NB_S = 16
PAST = 2048


class T:
    __slots__ = ("ap", "name", "w", "r")

    def __init__(self, ap, name):
        self.ap = ap
        self.name = name
        self.w = None
        self.r = {}

    def __getitem__(self, k):
        return self.ap[k]


class Prog:
    def __init__(self, nc, n_dma_sems=16):
        self.nc = nc
        self.es = contextlib.ExitStack()
        self.eng = {"pe": nc.tensor, "dve": nc.vector, "act": nc.scalar, "pool": nc.gpsimd, "sp": nc.sync}
        self.sem = {}
        for e in ("pe", "dve", "act", "pool"):
            self.sem[("e", e)] = self.es.enter_context(nc.semaphore("s_" + e))
        self.nd = n_dma_sems
        self.dq = {"sp": 0, "pool": 1}
        self.dcount = {"sp": 0, "pool": 0}
        for qi in range(2):
            for k in range(n_dma_sems):
                self.sem[("d", qi * n_dma_sems + k)] = self.es.enter_context(nc.semaphore("s_d%d_%d" % (qi, k)))
        self.cnt = {e: 0 for e in ("pe", "dve", "act", "pool")}
        self.dma_i = 0
        self.seen = {q: {} for q in self.eng}
        self.out_deps = {}
        self.n_inst = 0
        self.n_wait = 0
        self.pe_pending = False
        self.names = {}

    def sbuf(self, name, shape, dtype, stack=None):
        self.names[name] = self.names.get(name, 0) + 1
        if self.names[name] > 1:
            name = "%s__%d" % (name, self.names[name])
        t = (stack or self.es).enter_context(self.nc.sbuf_tensor(name, list(shape), dtype))
        return T(t[tuple(slice(None) for _ in shape)], name)

    def psum(self, name, shape, dtype):
        t = self.es.enter_context(self.nc.psum_tensor(name, list(shape), dtype))
        return t

    def _wait(self, q, key, val):
        if key == ("e", "pe") and q == "pe":
            return
        if self.seen[q].get(key, 0) >= val:
            return
        self.eng[q].wait_ge(self.sem[key], val)
        self.seen[q][key] = val
        self.n_wait += 1

    def _deps(self, q, reads, writes):
        for t in reads:
            if t.w is not None:
                self._wait(q, *t.w)
        for t in writes:
            if t.w is not None:
                self._wait(q, *t.w)
            for k, v in t.r.items():
                self._wait(q, k, v)

    def _mark(self, reads, writes, key, val):
        for t in reads:
            if t.r.get(key, 0) < val:
                t.r[key] = val
        for t in writes:
            t.w = (key, val)
            t.r = {}

    def op(self, e, reads, writes, fn, inc=True):
        self._deps(e, reads, writes)
        ins = fn()
        key = ("e", e)
        if inc:
            self.cnt[e] += 1
            ins.then_inc(self.sem[key], 1)
            val = self.cnt[e]
            if e == "pe":
                self.pe_pending = False
        else:
            assert e == "pe"
            val = self.cnt[e] + 1
            self.pe_pending = True
        self._mark(reads, writes, key, val)
        self.n_inst += 1
        return ins

    def dma(self, q, out, in_, reads=(), writes=(), is_output=False, indirect=None):
        qi = self.dq[q]
        ci = self.dcount[q]
        self.dcount[q] += 1
        k = qi * self.nd + ci % self.nd
        val = 16 * (ci // self.nd + 1)
        self.dma_i += 1
        key = ("d", k)
        if val > 16:
            self._wait(q, key, val - 16)
        self._deps(q, reads, writes)
        if indirect is not None:
            ins = self.eng[q].indirect_dma_start(out=out, out_offset=None, in_=in_, in_offset=indirect)
        else:
            ins = self.eng[q].dma_start(out=out, in_=in_)
        ins.then_inc(self.sem[key], 16)
        self._mark(reads, writes, key, val)
        if is_output:
            self.out_deps[key] = val
        self.n_inst += 1

    def barrier(self):
        assert not self.pe_pending
        for q in self.eng:
            for e in ("pe", "dve", "act", "pool"):
                if self.cnt[e] > 0:
                    self._wait(q, ("e", e), self.cnt[e])
            for qn, qi in self.dq.items():
                for k in range(self.nd):
                    n_used = (self.dcount[qn] - k + self.nd - 1) // self.nd
                    if n_used > 0:
                        self._wait(q, ("d", qi * self.nd + k), 16 * n_used)

    def finish(self):
        for key, val in self.out_deps.items():
            self._wait("sp", key, val)
        for e in ("pe", "dve", "act", "pool"):
            if self.cnt[e] > 0:
                self._wait("sp", ("e", e), self.cnt[e])


def _rope_tab(pos):
    half = 32
    inv = (10000.0 ** (-np.arange(half, dtype=np.float32) / half)).astype(np.float32)
    ang = pos.astype(np.float32)[:, None] * inv[None, :]
    return np.cos(ang).astype(np.float32), np.sin(ang).astype(np.float32)


def own_tiles(r):
    return [2 * j + (r if j % 2 == 0 else 1 - r) for j in range(NT_OWN)]


def _bf(a):
    return np.ascontiguousarray(a.astype(ml_dtypes.bfloat16))


def make_consts(r):
    c = {}
    c["ident_f"] = np.eye(128, dtype=np.float32)
    c["ident_b"] = _bf(np.eye(128, dtype=np.float32))
    cos, sin = _rope_tab(np.arange(SEQ))
    c["rope_all"] = np.ascontiguousarray(
        np.stack([cos.reshape(NT_ALL, 128, 32), sin.reshape(NT_ALL, 128, 32)], 0).transpose(2, 0, 1, 3))
    G = own_tiles(r)
    c["rope_own"] = np.ascontiguousarray(c["rope_all"][:, :, G, :])
    cs, ss = _rope_tab(PAST + (np.arange(128) % 8))
    c["rope_smp"] = np.ascontiguousarray(np.stack([cs, ss], 1))
    kk = np.arange(128)[:, None]
    qq = np.arange(128)[None, :]
    mc = np.zeros((128, 2, 2, 128), np.float32)
    mw = np.zeros((128, 2, 6, 128), np.float32)
    for par in range(2):
        delta = r if par == 0 else 1 - r
        for rr in range(2):
            mc[:, par, rr, :] = ((128 * rr + kk) <= (128 * delta + qq))
        for ri, rr in enumerate(range(-4, 2)):
            d = (128 * delta + qq) - (128 * rr + kk)
            mw[:, par, ri, :] = (d >= 0) & (d < 512)
    c["mask_c"] = _bf(mc)
    c["mask_w"] = _bf(mw)
    mcmp = np.zeros((128, NT_OWN, 2, 128), np.float32)
    bonus = np.zeros((128, NT_OWN, 64), np.float32)
    for j, g in enumerate(G):
        tpos = 128 * g + np.arange(128)
        for nt in range(2):
            n = 128 * nt + np.arange(128)
            mcmp[:, j, nt, :] = ((16 * n + 31)[:, None] <= tpos[None, :]) & (n < N_CMP_P)[:, None]
        cur = tpos // 64
        m = np.arange(64)[None, :]
        bonus[:, j, :] = 10.0 * (m == 0) + 20.0 * (m == cur[:, None]) + 30.0 * (m == cur[:, None] - 1)
    c["mask_cmp"] = _bf(mcmp)
    c["bonus"] = bonus
    n = np.arange(256)
    cs_ = n * 16
    ss_ = np.arange(64) * 64
    ov = ((cs_[:, None] < ss_[None, :] + 64) & (cs_[:, None] + 32 > ss_[None, :]) & (n < N_CMP_P)[:, None]).astype(np.float32)
    c["overlap"] = _bf(ov.reshape(2, 128, 64).transpose(1, 0, 2))
    E = np.zeros((64, NT_ALL, 128), np.float32)
    for kt in range(NT_ALL):
        for k in range(128):
            E[(128 * kt + k) // 64, kt, k] = 1.0
    c["expand"] = _bf(np.concatenate([E, E], 0))
    c["iota_p"] = np.arange(128, dtype=np.float32).reshape(128, 1)
    k = np.arange(128)
    mN = np.zeros((128, NB_S, 8), np.float32)
    for b in range(NB_S):
        mN[:, b, :] = ((k // 8) == b)[:, None] & ((k % 8)[:, None] <= np.arange(8)[None, :])
    c["maskN"] = _bf(mN)
    c["maskW0"] = _bf((k[:, None] > np.arange(8)[None, :]).astype(np.float32))
    Es = np.zeros((64, 17, 128), np.float32)
    for kt in range(16):
        for kk_ in range(128):
            Es[2 * kt + (kk_ // 64), kt, kk_] = 1.0
    Es[32, 16, :] = 1.0
    c["expand_s"] = _bf(Es)
    n_ = np.arange(128)[:, None]
    m_ = np.arange(64)[None, :]
    c["overlap_s"] = _bf(((n_ < N_CMP_S) & (16 * n_ < 64 * m_ + 64) & (16 * n_ + 32 > 64 * m_) & (m_ < 33)).astype(np.float32))
    bs = np.zeros((16, 64), np.float32)
    bs[:, 0] = 10.0
    bs[:, 31] = 30.0
    bs[:, 32] = 20.0
    bs[:, 33:] = -1e9
    c["bonus_s"] = bs
    c["mask127"] = (np.arange(128) < N_CMP_S).astype(np.float32).reshape(128, 1)
    c["ones127"] = _bf(np.repeat(c["mask127"], 128, axis=1))
    return c


CONST_DT = {"ident_f": F32, "ident_b": BF16, "rope_all": F32, "rope_own": F32, "rope_smp": F32,
            "mask_c": BF16, "mask_w": BF16, "mask_cmp": BF16, "bonus": F32, "overlap": BF16,
            "expand": BF16, "iota_p": F32, "maskN": BF16, "maskW0": BF16, "expand_s": BF16, "overlap_s": BF16,
            "bonus_s": F32, "mask127": F32, "ones127": BF16}


def build_program(do_prompt=True, do_sample=True, do_mlp=True, n_kpass=NT_ALL, n_qtiles=NT_OWN, n_phys=2560, debug=False, n_sbatch=NB_S, sstop=99):
    nc = bass.Bass("TRN2", target_bir_lowering=False)
    P = Prog(nc)
    es = P.es
    with es:
        _emit(nc, P, do_prompt, do_sample, do_mlp, n_kpass, n_qtiles, n_phys, debug, n_sbatch, sstop)
    return nc, P


def _emit(nc, P, do_prompt, do_sample, do_mlp, n_kpass, n_qtiles, n_phys, debug=False, n_sbatch=NB_S, sstop=99):
    es = P.es

    def din(name, shape, dt=F32):
        return nc.dram_tensor(name, list(shape), dt, kind="ExternalInput").ap()

    def dout(name, shape, dt=F32):
        return nc.dram_tensor(name, list(shape), dt, kind="ExternalOutput").ap()

    x_all = din("x_all", [SEQ, D_MODEL])
    x_own = din("x_own", [SEQ // 2, D_MODEL])
    x_smp = din("x_smp", [128, D_MODEL])
    w_in = din("w_in", [D_MODEL, D_IN])
    w_out = din("w_out", [D_MODEL, D_MODEL])
    w_up = din("w_up", [D_MODEL, D_FF])
    w_down = din("w_down", [D_FF, D_MODEL])
    gains = {n: din(n, [1, D_MODEL]) for n in ("g_pre_mix", "g_post_mix", "g_pre_mlp", "g_post_mlp")}
    lam_in = {n: din(n, [1, 64]) for n in ("lam_q1", "lam_k1", "lam_q2", "lam_k2")}
    subln = din("diff_subln", [1, 128])
    cmp_pos = din("cmp_pos", [2, 32, 64])
    cmp_w1 = din("cmp_w1", [2, 2048, 128])
    cmp_w2 = din("cmp_w2", [2, 128, 64])
    cache_d = din("cache_d", [n_phys * 128, 1024])
    cache_n = din("cache_n", [n_phys * 128, 512])
    win_st = din("win_st", [NB_S, 512, 256])
    ptab = din("ptab", [1, NB_S * 16], I32)
    cshape = {"ident_f": [128, 128], "ident_b": [128, 128], "rope_all": [128, 2, NT_ALL, 32],
              "rope_own": [128, 2, NT_OWN, 32], "rope_smp": [128, 2, 32], "mask_c": [128, 2, 2, 128],
              "mask_w": [128, 2, 6, 128], "mask_cmp": [128, NT_OWN, 2, 128], "bonus": [128, NT_OWN, 64],
              "overlap": [128, 2, 64], "expand": [128, NT_ALL, 128], "iota_p": [128, 1],
              "maskN": [128, NB_S, 8], "maskW0": [128, 8], "expand_s": [64, 17, 128], "overlap_s": [128, 64],
              "bonus_s": [16, 64], "mask127": [128, 1], "ones127": [128, 128]}
    cdram = {n: din("c_" + n, s, CONST_DT[n]) for n, s in cshape.items()}

    o_yp = dout("o_yp", [SEQ // 2, D_MODEL])
    o_ys = dout("o_ys", [128, D_MODEL])
    o_dkv_p = dout("o_dkv_p", [SEQ, 1024])
    o_nkv_p = dout("o_nkv_p", [SEQ, 512])
    o_wkv_p = dout("o_wkv_p", [512, 256])
    o_dkv_s = dout("o_dkv_s", [128, 1024])
    o_nkv_s = dout("o_nkv_s", [128, 512])
    o_wkv_s = dout("o_wkv_s", [NB_S, 512, 256])
    if debug:
        dbg_omix = dout("dbg_omix", [NT_OWN * 128, 1024], BF16)
        dbg_nsa = dout("dbg_nsa", [3, NT_OWN * 128, 512])
        dbg_psel = dout("dbg_psel", [NT_OWN * 128, 128])
        dbg_bm = dout("dbg_bm", [NT_OWN * 128, 128], BF16)

    psb = [P.psum("psb%d" % i, [128, 512], F32) for i in range(7)]
    psT_t = P.psum("psT", [128, 1024], BF16)

    C = {}
    for n in ("ident_f", "ident_b", "iota_p"):
        C[n] = P.sbuf("sb_" + n, cshape[n], CONST_DT[n])
        P.dma("sp", C[n][:], cdram[n], writes=[C[n]])
    eps_t = P.sbuf("eps_t", [128, 1], F32)
    P.op("dve", [], [eps_t], lambda: nc.vector.memset(eps_t[:], EPS))
    gb = {}

    def load_gains(names, stack):
        for n in names:
            gb[n] = P.sbuf("gb_" + n, [128, D_MODEL], F32, stack)
            P.dma("sp", gb[n][:], gains[n][0:1, :].to_broadcast([128, D_MODEL]), writes=[gb[n]])

    B = [T(psb[i][:, :], "bank%d" % i) for i in range(7)]

    junk = P.sbuf("junk", [128, D_MODEL], BF16)
    st_ss = [P.sbuf("st_ss%d" % i, [128, 1], F32) for i in range(2)]
    st_ln = [P.sbuf("st_ln%d" % i, [128, 1], F32) for i in range(2)]
    st_rs = [P.sbuf("st_rs%d" % i, [128, 1], F32) for i in range(2)]

    def rstd_of(src_t, src_ap, slot, n_feat=D_MODEL):
        ss, ln, rs = st_ss[slot], st_ln[slot], st_rs[slot]
        P.op("act", [src_t], [junk, ss],
             lambda: nc.scalar.activation(out=junk[:, 0:n_feat], in_=src_ap, func=AF.Square, accum_out=ss[:]))
        P.op("act", [ss, eps_t], [ln],
             lambda: nc.scalar.activation(out=ln[:], in_=ss[:], func=AF.Ln, bias=eps_t[:], scale=1.0 / n_feat))
        P.op("act", [ln], [rs],
             lambda: nc.scalar.activation(out=rs[:], in_=ln[:], func=AF.Exp, scale=-0.5))
        return rs

    psT = T(psT_t[:, :], "psT")

    def transpose_to(hb, hT, nchunks=8, evac="act"):
        for kc in range(nchunks):
            P.op("pe", [hb, C["ident_b"]], [psT],
                 lambda kc=kc: nc.tensor.transpose(out=psT[:, kc * 128:(kc + 1) * 128],
                                                   in_=hb[:, kc * 128:(kc + 1) * 128], identity=C["ident_b"][:]),
                 inc=(kc == nchunks - 1))
        if evac == "act":
            P.op("act", [psT], [hT], lambda: nc.scalar.copy(out=hT[:].rearrange("p a b -> p (a b)")[:, 0:nchunks * 128],
                                                             in_=psT[:, 0:nchunks * 128]))
        else:
            P.op("dve", [psT], [hT], lambda: nc.vector.tensor_copy(out=hT[:].rearrange("p a b -> p (a b)")[:, 0:nchunks * 128],
                                                                    in_=psT[:, 0:nchunks * 128]))

    def rope(eng, src_t, src3, dst_t, dst3, cos_ap, sin_ap, nh, tmp, tab_t):
        s4 = src3.rearrange("p h (two x) -> p h two x", two=2)
        d4 = dst3.rearrange("p h (two x) -> p h two x", two=2)
        cb = cos_ap.unsqueeze(1).to_broadcast([128, nh, 32])
        sb = sin_ap.unsqueeze(1).to_broadcast([128, nh, 32])
        x1, x2 = s4[:, :, 0, :], s4[:, :, 1, :]
        t1 = tmp[0][:, 0:nh * 32].rearrange("p (h x) -> p h x", x=32)
        t2 = tmp[1][:, 0:nh * 32].rearrange("p (h x) -> p h x", x=32)
        E = P.eng[eng]
        P.op(eng, [src_t, tab_t], [tmp[0]], lambda: E.tensor_tensor(out=t1, in0=x1, in1=cb, op=ALU.mult))
        P.op(eng, [src_t, tab_t], [tmp[1]], lambda: E.tensor_tensor(out=t2, in0=x2, in1=sb, op=ALU.mult))
        P.op(eng, [tmp[0], tmp[1]], [dst_t], lambda: E.tensor_tensor(out=d4[:, :, 0, :], in0=t1, in1=t2, op=ALU.subtract))
        P.op(eng, [src_t, tab_t], [tmp[0]], lambda: E.tensor_tensor(out=t1, in0=x1, in1=sb, op=ALU.mult))
        P.op(eng, [src_t, tab_t], [tmp[1]], lambda: E.tensor_tensor(out=t2, in0=x2, in1=cb, op=ALU.mult))
        P.op(eng, [tmp[0], tmp[1]], [dst_t], lambda: E.tensor_tensor(out=d4[:, :, 1, :], in0=t1, in1=t2, op=ALU.add))

    rtmp = [P.sbuf("rtmp%d" % i, [128, 256], F32) for i in range(2)]

    def load_w(name, dst, src3, eng="pool"):
        for kc in range(src3.shape[1]):
            P.dma(eng, dst[:, kc, :], src3[:, kc, :], writes=[dst])

    w_in3 = w_in.rearrange("(kc p) n -> p kc n", p=128)
    w_out3 = w_out.rearrange("(kc p) n -> p kc n", p=128)
    w_up3 = w_up.rearrange("(kc p) n -> p kc n", p=128)
    w_dn3 = w_down.rearrange("(kc p) n -> p kc n", p=128)

    def pipeline(tasks, depth=2):
        pend = []
        for t in tasks:
            if t is None:
                for p_ in pend:
                    p_[1]()
                pend = []
                continue
            t[0]()
            pend.append(t)
            if len(pend) > depth:
                pend.pop(0)[1]()
        for p_ in pend:
            p_[1]()

    noop = lambda: None

    lam_sb = {}
    for n in lam_in:
        lam_sb[n] = P.sbuf("sb_" + n, [128, 64], F32)
        P.dma("sp", lam_sb[n][:], lam_in[n][0:1, :].to_broadcast([128, 64]), writes=[lam_sb[n]])
    lam_s = [P.sbuf("lam_s%d" % i, [128, 1], F32) for i in range(2)]
    lam_e = [P.sbuf("lam_e%d" % i, [128, 1], F32) for i in range(2)]
    neg_lam = P.sbuf("neg_lam", [128, 1], F32)
    junk64 = P.sbuf("junk64", [128, 64], F32)
    for i, (a, b) in enumerate((("lam_q1", "lam_k1"), ("lam_q2", "lam_k2"))):
        P.op("dve", [lam_sb[a], lam_sb[b]], [junk64],
             lambda a=a, b=b: nc.vector.tensor_tensor(out=junk64[:], in0=lam_sb[a][:], in1=lam_sb[b][:], op=ALU.mult))
        P.op("act", [junk64], [junk64, lam_s[i]],
             lambda i=i: nc.scalar.activation(out=junk64[:], in_=junk64[:], func=AF.Copy, accum_out=lam_s[i][:]))
        P.op("act", [lam_s[i]], [lam_e[i]],
             lambda i=i: nc.scalar.activation(out=lam_e[i][:], in_=lam_s[i][:], func=AF.Exp))
    P.op("dve", [lam_e[0], lam_e[1]], [neg_lam],
         lambda: nc.vector.tensor_tensor(out=neg_lam[:], in0=lam_e[1][:], in1=lam_e[0][:], op=ALU.subtract))
    P.op("dve", [neg_lam], [neg_lam],
         lambda: nc.vector.tensor_scalar(out=neg_lam[:], in0=neg_lam[:], scalar1=-LAM_INIT, scalar2=None, op0=ALU.add))
    sg = P.sbuf("sg", [128, 128], F32)
    P.dma("sp", sg[:], subln[0:1, :].to_broadcast([128, 128]), writes=[sg])
    P.op("dve", [sg], [sg],
         lambda: nc.vector.tensor_scalar(out=sg[:], in0=sg[:], scalar1=1.0 - LAM_INIT, scalar2=None, op0=ALU.mult))

    col_ring = [P.sbuf("col%d" % i, [128, 1], F32) for i in range(12)]
    col_i = [0]

    def col():
        c_ = col_ring[col_i[0] % len(col_ring)]
        col_i[0] += 1
        return c_

    D1s = P.sbuf("D1s", [128, D_MODEL], BF16)
    P.op("pool", [], [D1s], lambda: nc.gpsimd.memset(D1s[:], 0.0))

    def front(xt_t, hb_t, hT_t, slot, gname):
        rs = rstd_of(xt_t, xt_t[:], slot)
        P.op("dve", [xt_t, rs, gb[gname]], [hb_t],
             lambda: nc.vector.scalar_tensor_tensor(out=hb_t[:], in0=xt_t[:], scalar=rs[:, 0:1],
                                                    in1=gb[gname][:], op0=ALU.mult, op1=ALU.mult))
        transpose_to(hb_t, hT_t)

    def proj(ps, n, hT_t, w_t, c0):
        for kc in range(8):
            P.op("pe", [hT_t, w_t], [ps],
                 lambda kc=kc: nc.tensor.matmul(ps[:, 0:n], lhsT=hT_t[:, kc, :], rhs=w_t[:, kc, c0:c0 + n],
                                                start=(kc == 0), stop=(kc == 7)),
                 inc=(kc == 7))

    def out_proj_and_delta(omix_b, omixT, wo, psY, d1_ap_fn, slot, skip_transpose=False, d1_t=None):
        if not skip_transpose:
            transpose_to(omix_b, omixT)
        for half in range(2):
            proj(psY[half], 512, omixT, wo, half * 512)
        ssa, ssb = col(), col()
        P.op("act", [psY[0]], [junk, ssa],
             lambda: nc.scalar.activation(out=junk[:, 0:512], in_=psY[0][:, 0:512], func=AF.Square, accum_out=ssa[:]))
        P.op("act", [psY[1]], [junk, ssb],
             lambda: nc.scalar.activation(out=junk[:, 0:512], in_=psY[1][:, 0:512], func=AF.Square, accum_out=ssb[:]))
        P.op("dve", [ssa, ssb], [ssa], lambda: nc.vector.tensor_tensor(out=ssa[:], in0=ssa[:], in1=ssb[:], op=ALU.add))
        ln, rs = st_ln[slot], st_rs[slot]
        P.op("act", [ssa, eps_t], [ln],
             lambda: nc.scalar.activation(out=ln[:], in_=ssa[:], func=AF.Ln, bias=eps_t[:], scale=1.0 / D_MODEL))
        P.op("act", [ln], [rs], lambda: nc.scalar.activation(out=rs[:], in_=ln[:], func=AF.Exp, scale=-0.5))
        for half in range(2):
            P.op("dve", [psY[half], rs, gb["g_post_mix"]], [d1_t],
                 lambda half=half: nc.vector.scalar_tensor_tensor(
                     out=d1_ap_fn(half), in0=psY[half][:, 0:512], scalar=rs[:, 0:1],
                     in1=gb["g_post_mix"][:, half * 512:(half + 1) * 512], op0=ALU.mult, op1=ALU.mult))

    if do_sample:
        sst = contextlib.ExitStack()
        sst.__enter__()
        load_gains(("g_pre_mix", "g_post_mix"), sst)
        sc = {}
        for n in ("rope_smp", "maskN", "maskW0", "expand_s", "overlap_s", "bonus_s", "ones127"):
            sc[n] = P.sbuf("sb_" + n, cshape[n], CONST_DT[n], sst)
            P.dma("sp", sc[n][:], cdram[n], writes=[sc[n]])
        ones_bf = P.sbuf("ones_bf", [128, 128], BF16, sst)
        P.op("dve", [], [ones_bf], lambda: nc.vector.memset(ones_bf[:], 1.0))
        ones_f = P.sbuf("ones_f", [128, 128], F32, sst)
        P.op("dve", [], [ones_f], lambda: nc.vector.memset(ones_f[:], 1.0))
        sgT = P.sbuf("sgT", [128, 1], F32, sst)
        P.dma("sp", sgT[:], subln.rearrange("o e -> e o"), writes=[sgT])
        P.op("dve", [sgT], [sgT], lambda: nc.vector.tensor_scalar(
            out=sgT[:], in0=sgT[:], scalar1=1.0 - LAM_INIT, scalar2=None, op0=ALU.mult))
        pti = P.sbuf("pti", [128, NB_S * 16], I32, sst)
        ptf = P.sbuf("ptf", [128, NB_S * 16], F32, sst)
        idx = P.sbuf("idx", [128, NB_S * 16], I32, sst)
        P.dma("sp", pti[:], ptab[0:1, :].to_broadcast([128, NB_S * 16]), writes=[pti])
        P.op("dve", [pti], [ptf], lambda: nc.vector.tensor_copy(out=ptf[:], in_=pti[:]))
        P.op("dve", [ptf, C["iota_p"]], [ptf], lambda: nc.vector.tensor_scalar(
            out=ptf[:], in0=ptf[:], scalar1=128.0, scalar2=C["iota_p"][:, 0:1], op0=ALU.mult, op1=ALU.add))
        P.op("dve", [ptf], [idx], lambda: nc.vector.tensor_copy(out=idx[:], in_=ptf[:]))
        wo_s = P.sbuf("wo_s", [128, 8, 1024], BF16, sst)
        for kc in range(4):
            P.dma("pool", wo_s[:, kc, :], w_out3[:, kc, :], writes=[wo_s])
        for g in range(4):
            for kvh in range(2):
                r0 = 512 + (kvh * 4 + g) * 64
                P.dma("pool", wo_s[64 * kvh:64 * kvh + 64, 4 + g, :], w_out[r0:r0 + 64, :], writes=[wo_s])
        W1 = P.sbuf("W1s", [128, 2, 32, 128], BF16, sst)
        W2d = P.sbuf("W2ds", [128, 2, 128], BF16, sst)
        for w in range(2):
            for dup in range(2):
                P.dma("pool", W1[64 * dup:64 * dup + 64, w, :, :],
                      cmp_w1[w].rearrange("(l d) f -> d l f", d=64), writes=[W1])
                P.dma("pool", W2d[:, w, 64 * dup:64 * dup + 64], cmp_w2[w], writes=[W2d])
        pos_sb = P.sbuf("pos_sbs", [32, 2, 64], F32, sst)
        for w in range(2):
            P.dma("sp", pos_sb[:, w, :], cmp_pos[w], writes=[pos_sb])
        posT = P.sbuf("posTs", [64, 2, 32], BF16, sst)
        cbias = P.sbuf("cbiass", [128, 2], F32, sst)
        for w in range(2):
            P.op("pe", [pos_sb, C["ident_f"]], [B[3]], lambda w=w: nc.tensor.transpose(
                out=B[3][0:64, w * 32:(w + 1) * 32], in_=pos_sb[0:32, w, :], identity=C["ident_f"][0:32, 0:32]))
        P.op("dve", [B[3]], [posT], lambda: nc.vector.tensor_copy(
            out=posT[:].rearrange("p a b -> p (a b)"), in_=B[3][0:64, 0:64]))
        for w in range(2):
            for l in range(32):
                P.op("pe", [W1, posT], [B[4]], lambda w=w, l=l: nc.tensor.matmul(
                    B[4][:, w:w + 1], lhsT=W1[0:64, w, l, :], rhs=posT[0:64, w, l:l + 1],
                    start=(l == 0), stop=(l == 31)), inc=(l == 31))
        P.op("dve", [B[4]], [cbias], lambda: nc.vector.tensor_copy(out=cbias[:], in_=B[4][:, 0:2]))

        QdTblk = P.sbuf("QdTblk", [128, 4, NB_S, 2, 8], BF16, sst)
        QnT_s = P.sbuf("QnT_s", [128, 4, 128], BF16, sst)
        QnTblk = P.sbuf("QnTblk", [128, NB_S, 2, 4, 8], BF16, sst)
        KdTn = P.sbuf("KdTn", [128, 4, 128], BF16, sst)
        NTn = P.sbuf("NTn", [128, 2, 128], BF16, sst)
        vall_s = P.sbuf("vall_s", [128, 768], BF16, sst)
        GB = P.sbuf("GB", [128, 3, 8, 128], F32, sst)
        omixT_s = P.sbuf("omixT_s", [128, 8, 128], BF16, sst)
        P.op("dve", [], [QdTblk], lambda: nc.vector.memset(QdTblk[:], 0.0))
        P.op("dve", [], [omixT_s], lambda: nc.vector.memset(omixT_s[:], 0.0))

        s0 = contextlib.ExitStack()
        s0.__enter__()
        wS = P.sbuf("wS", [128, 8, D_IN], BF16, s0)
        for kc in range(8):
            P.dma("pool", wS[:, kc, :], w_in3[:, kc, :], writes=[wS])
        xs_t = P.sbuf("xs_t", [128, D_MODEL], F32, s0)
        hbs = P.sbuf("hbs", [128, D_MODEL], BF16, s0)
        hTs = P.sbuf("hTs", [128, 8, 128], BF16, s0)
        qd_s = P.sbuf("qd_s", [128, 512], BF16, s0)
        qn_s = P.sbuf("qn_s", [128, 512], BF16, s0)
        QdT_s = P.sbuf("QdT_s", [128, 4, 128], BF16, s0)
        kvd_s = P.sbuf("kvd_s", [128, 1024], F32, s0)
        kvn_s = P.sbuf("kvn_s", [128, 768], F32, s0)
        gate_s = P.sbuf("gate_s", [128, 24], F32, s0)
        tmpG = P.sbuf("tmpG", [128, 8, 128], F32, s0)
        P.dma("sp", xs_t[:], x_smp[:, :], writes=[xs_t])
        P.dma("sp", o_wkv_s[:, 0:504, :], win_st[:, 8:512, :], is_output=True)
        front(xs_t, hbs, hTs, 0, "g_pre_mix")
        for (bk, c0, n) in ((0, 0, 512), (1, 512, 512), (2, 1024, 512), (3, 1536, 512), (4, 2048, 512), (5, 2560, 256), (6, 2816, 24)):
            proj(B[bk], n, hTs, wS, c0)
        cos, sin = sc["rope_smp"][:, 0, :], sc["rope_smp"][:, 1, :]
        v64 = lambda ap: ap.rearrange("p (h x) -> p h x", x=64)
        rope("dve", B[0], v64(B[0][:, 0:512]), qd_s, v64(qd_s[:, :]), cos, sin, 8, rtmp, sc["rope_smp"])
        rope("dve", B[1], v64(B[1][:, 0:512]), kvd_s, v64(kvd_s[:, 0:512]), cos, sin, 8, rtmp, sc["rope_smp"])
        P.op("act", [B[2]], [kvd_s], lambda: nc.scalar.copy(out=kvd_s[:, 512:1024], in_=B[2][:, 0:512]))
        for kvh in range(2):
            rope("dve", B[3], v64(B[3][:, kvh * 256:(kvh + 1) * 256]), qn_s,
                 qn_s[:, :].rearrange("p (g k x) -> p k g x", g=4, k=2)[:, kvh, :, :], cos, sin, 4, rtmp, sc["rope_smp"])
        for slot in (0, 2):
            rope("dve", B[4], v64(B[4][:, slot * 128:(slot + 1) * 128]), kvn_s, v64(kvn_s[:, slot * 128:(slot + 1) * 128]),
                 cos, sin, 2, rtmp, sc["rope_smp"])
        rope("dve", B[5], v64(B[5][:, 0:128]), kvn_s, v64(kvn_s[:, 512:640]), cos, sin, 2, rtmp, sc["rope_smp"])
        for slot in (1, 3):
            P.op("act", [B[4]], [kvn_s], lambda slot=slot: nc.scalar.copy(
                out=kvn_s[:, slot * 128:(slot + 1) * 128], in_=B[4][:, slot * 128:(slot + 1) * 128]))
        P.op("act", [B[5]], [kvn_s], lambda: nc.scalar.copy(out=kvn_s[:, 640:768], in_=B[5][:, 128:256]))
        P.op("act", [B[6]], [gate_s], lambda: nc.scalar.activation(out=gate_s[:], in_=B[6][:, 0:24], func=AF.Exp, scale=-1.0))
        P.op("dve", [gate_s], [gate_s], lambda: nc.vector.tensor_scalar(
            out=gate_s[:], in0=gate_s[:], scalar1=1.0, scalar2=None, op0=ALU.add))
        P.op("dve", [gate_s], [gate_s], lambda: nc.vector.reciprocal(out=gate_s[:], in_=gate_s[:]))
        P.dma("sp", o_dkv_s[:, :], kvd_s[:], reads=[kvd_s], is_output=True)
        P.dma("sp", o_nkv_s[:, :], kvn_s[:, 0:512], reads=[kvn_s], is_output=True)
        for b in range(NB_S):
            P.dma("sp", o_wkv_s[b, 504:512, :], kvn_s[b * 8:(b + 1) * 8, 512:768], reads=[kvn_s], is_output=True)
        transpose_to(qd_s, QdT_s, nchunks=4, evac="dve")
        for c in range(2):
            lo = 64 * c
            P.op("dve", [QdT_s], [QdTblk], lambda c=c, lo=lo: nc.vector.tensor_copy(
                out=QdTblk[lo:lo + 64, :, :, c, :], in_=QdT_s[lo:lo + 64, :, :].rearrange("p h (b q) -> p h b q", q=8)))
        transpose_to(qn_s, QnT_s, nchunks=4, evac="dve")
        P.op("dve", [], [QnTblk], lambda: nc.vector.memset(QnTblk[:], 0.0))
        for kvh in range(2):
            lo = 64 * kvh
            P.op("dve", [QnT_s], [QnTblk], lambda kvh=kvh, lo=lo: nc.vector.tensor_copy(
                out=QnTblk[lo:lo + 64, :, kvh, :, :].rearrange("p b g q -> p g b q"),
                in_=QnT_s[lo:lo + 64, :, :].rearrange("p g (b q) -> p g b q", q=8)))
        for h in range(4):
            P.op("pe", [kvd_s, C["ident_f"]], [B[0]], lambda h=h: nc.tensor.transpose(
                out=B[0][:, h * 128:(h + 1) * 128], in_=kvd_s[:, h * 128:(h + 1) * 128], identity=C["ident_f"][:]),
                inc=(h == 3))
        P.op("act", [B[0]], [KdTn], lambda: nc.scalar.copy(out=KdTn[:].rearrange("p a b -> p (a b)"), in_=B[0][:, 0:512]))
        for wi, slot in enumerate((2, 4)):
            P.op("pe", [kvn_s, C["ident_f"]], [B[1]], lambda wi=wi, slot=slot: nc.tensor.transpose(
                out=B[1][:, wi * 128:(wi + 1) * 128], in_=kvn_s[:, slot * 128:(slot + 1) * 128], identity=C["ident_f"][:]),
                inc=(wi == 1))
        P.op("act", [B[1]], [NTn], lambda: nc.scalar.copy(out=NTn[:].rearrange("p a b -> p (a b)"), in_=B[1][:, 0:256]))
        P.op("dve", [kvd_s], [vall_s], lambda: nc.vector.tensor_copy(out=vall_s[:, 0:512], in_=kvd_s[:, 512:1024]))
        P.op("dve", [kvn_s], [vall_s], lambda: nc.vector.tensor_copy(out=vall_s[:, 512:640], in_=kvn_s[:, 384:512]))
        P.op("dve", [kvn_s], [vall_s], lambda: nc.vector.tensor_copy(out=vall_s[:, 640:768], in_=kvn_s[:, 640:768]))
        g3 = gate_s[:, :].rearrange("p (h b) -> p h b", b=3)
        for bi in range(3):
            P.op("dve", [gate_s, C["ident_f"]], [tmpG], lambda bi=bi: nc.vector.tensor_tensor(
                out=tmpG[:], in0=g3[:, :, bi].unsqueeze(2).to_broadcast([128, 8, 128]),
                in1=C["ident_f"][:, :].unsqueeze(1).to_broadcast([128, 8, 128]), op=ALU.mult))
            for half in range(2):
                P.op("pe", [ones_f, tmpG], [B[2 + half]], lambda half=half: nc.tensor.matmul(
                    B[2 + half][:, 0:512], lhsT=ones_f[:, :],
                    rhs=tmpG[:, half * 4:(half + 1) * 4, :].rearrange("p a b -> p (a b)"), start=True, stop=True))
                P.op("act", [B[2 + half]], [GB], lambda bi=bi, half=half: nc.scalar.copy(
                    out=GB[:, bi, half * 4:(half + 1) * 4, :].rearrange("p a b -> p (a b)"), in_=B[2 + half][:, 0:512]))
        P.barrier()
        s0.__exit__(None, None, None)

        pgd = [P.sbuf("pgd%d" % i, [128, 16, 1024], BF16, sst) for i in range(2)]
        pgn = P.sbuf("pgn", [128, 16, 512], BF16, sst)
        wst = P.sbuf("wst", [128, 4, 256], BF16, sst)
        KdT_b = P.sbuf("KdT_b", [128, 4, PAST], BF16, sst)
        CS_b = P.sbuf("CS_b", [128, 3, PAST], BF16, sst)
        WkT_b = P.sbuf("WkT_b", [128, 512], BF16, sst)
        PTd = P.sbuf("PTd", [128, 17, 64], BF16, sst)
        PsT = P.sbuf("PsT", [128, 17, 64], BF16, sst)
        PcT = P.sbuf("PcT", [128, 64], BF16, sst)
        msk_sb = P.sbuf("msk_sb", [128, 17, 16], BF16, sst)
        KCT_b = P.sbuf("KCT_b", [128, 128], BF16, sst)
        VC_b = P.sbuf("VC_b", [128, 128], BF16, sst)
        hcb = P.sbuf("hcbs", [128, 128], BF16, sst)
        P.op("dve", [], [hcb], lambda: nc.vector.memset(hcb[:], 0.0))
        gtmp = [P.sbuf("gtmps%d" % i, [128, 128], F32, sst) for i in range(2)]
        rl_sb = P.sbuf("rl_sbs", [128, 64], F32, sst)
        w_sb = P.sbuf("w_sbs", [128, 64], F32, sst)
        t_sb = [P.sbuf("t_sbs%d" % i, [128, 32], F32, sst) for i in range(3)]
        od_s = P.sbuf("od_s", [128, 32], F32, sst)
        accT = P.sbuf("accT", [128, 32], F32, sst)
        tmpP = P.sbuf("tmpP", [64, 64], F32, sst)
        pselT = P.sbuf("pselT", [64, 16], F32, sst)
        score = P.sbuf("score_s", [16, 64], F32, sst)
        swork = P.sbuf("swork_s", [16, 64], F32, sst)
        m8 = [P.sbuf("m8s_%d" % i, [16, 8], F32, sst) for i in range(2)]
        bm_s = P.sbuf("bm_s", [16, 64], BF16, sst)
        bmT2 = P.sbuf("bmT2", [64, 16], BF16, sst)

        def gather_d(b):
            for s_ in range(16):
                col_ = b * 16 + s_
                P.dma("pool", pgd[b % 2][:, s_, :], cache_d[:, :], reads=[idx], writes=[pgd[b % 2]],
                      indirect=bass.IndirectOffsetOnAxis(ap=idx[:, col_:col_ + 1], axis=0))

        def gather_n(b):
            for s_ in range(16):
                col_ = b * 16 + s_
                P.dma("pool", pgn[:, s_, :], cache_n[:, :], reads=[idx], writes=[pgn],
                      indirect=bass.IndirectOffsetOnAxis(ap=idx[:, col_:col_ + 1], axis=0))

        def gather_w(b):
            P.dma("pool", wst[:], win_st[b].rearrange("(t p) c -> p t c", p=128), writes=[wst])

        def gelu_s(src_ps, n, bias_col, dst_bf, tmpa, tmpb):
            P.op("dve", [src_ps, cbias], [tmpa], lambda: nc.vector.tensor_scalar(
                out=tmpa[:, 0:n], in0=src_ps[:, 0:n], scalar1=bias_col, scalar2=None, op0=ALU.add))
            P.op("dve", [tmpa], [tmpb], lambda: nc.vector.tensor_tensor(
                out=tmpb[:, 0:n], in0=tmpa[:, 0:n], in1=tmpa[:, 0:n], op=ALU.mult))
            P.op("dve", [tmpb], [tmpb], lambda: nc.vector.tensor_scalar(
                out=tmpb[:, 0:n], in0=tmpb[:, 0:n], scalar1=0.044715, scalar2=1.0, op0=ALU.mult, op1=ALU.add))
            P.op("dve", [tmpb, tmpa], [tmpb], lambda: nc.vector.tensor_tensor(
                out=tmpb[:, 0:n], in0=tmpb[:, 0:n], in1=tmpa[:, 0:n], op=ALU.mult))
            P.op("act", [tmpb], [tmpb], lambda: nc.scalar.activation(
                out=tmpb[:, 0:n], in_=tmpb[:, 0:n], func=AF.Exp, scale=-1.5957691216057308))
            P.op("dve", [tmpb], [tmpb], lambda: nc.vector.tensor_scalar(
                out=tmpb[:, 0:n], in0=tmpb[:, 0:n], scalar1=1.0, scalar2=None, op0=ALU.add))
            P.op("dve", [tmpb], [tmpb], lambda: nc.vector.reciprocal(out=tmpb[:, 0:n], in_=tmpb[:, 0:n]))
            P.op("dve", [tmpb, tmpa], [dst_bf], lambda: nc.vector.tensor_tensor(
                out=dst_bf[:, 0:n], in0=tmpb[:, 0:n], in1=tmpa[:, 0:n], op=ALU.mult))

        evi = [0]

        def evac_copy(src_t, src_ap, dst_t, dst_ap):
            if evi[0] % 2 == 0:
                P.op("act", [src_t], [dst_t], lambda: nc.scalar.copy(out=dst_ap, in_=src_ap))
            else:
                P.op("dve", [src_t], [dst_t], lambda: nc.vector.tensor_copy(out=dst_ap, in_=src_ap))
            evi[0] += 1

        def nsa_branch_evac(Y, Z, b, bi, first):
            P.op("dve", [Z], [rl_sb], lambda: nc.vector.tensor_scalar(
                out=rl_sb[:], in0=Z[:, 0:64], scalar1=1e-30, scalar2=None, op0=ALU.max))
            P.op("dve", [rl_sb], [rl_sb], lambda: nc.vector.reciprocal(out=rl_sb[:], in_=rl_sb[:]))
            P.op("dve", [rl_sb, GB], [w_sb], lambda: nc.vector.tensor_tensor(
                out=w_sb[:].rearrange("p (h q) -> p h q", q=8), in0=rl_sb[:].rearrange("p (h q) -> p h q", q=8),
                in1=GB[:, bi, :, b * 8:(b + 1) * 8], op=ALU.mult))
            dst = accT if first else t_sb[2]
            for kvh in range(2):
                lo = 64 * kvh
                P.op("dve", [Y, w_sb], [dst], lambda kvh=kvh, lo=lo: nc.vector.tensor_tensor(
                    out=dst[lo:lo + 64, :], in0=Y[lo:lo + 64, kvh * 32:(kvh + 1) * 32],
                    in1=w_sb[lo:lo + 64, kvh * 32:(kvh + 1) * 32], op=ALU.mult))
            if not first:
                P.op("dve", [accT, t_sb[2]], [accT], lambda: nc.vector.tensor_tensor(
                    out=accT[:], in0=accT[:], in1=t_sb[2][:], op=ALU.add))

        if n_sbatch > 0:
            gather_d(0)
            gather_n(0)
            gather_w(0)
        for b in range(n_sbatch):
            pg = pgd[b % 2]
            if sstop <= 1:
                continue
            for sp_ in range(8):
                for s2 in range(2):
                    for h in range(4):
                        i = s2 * 4 + h
                        P.op("pe", [pg, C["ident_b"]], [psT], lambda sp_=sp_, s2=s2, h=h, i=i: nc.tensor.transpose(
                            out=psT[:, i * 128:(i + 1) * 128], in_=pg[:, sp_ * 2 + s2, h * 128:(h + 1) * 128],
                            identity=C["ident_b"][:]), inc=(i == 7))
                evac_copy(psT, psT[:, :].rearrange("p (s h t) -> p s h t", s=2, h=4), KdT_b,
                          KdT_b[:, :, sp_ * 256:(sp_ + 1) * 256].rearrange("p h (s t) -> p s h t", s=2))
            if b + 1 < n_sbatch:
                pass
            if sstop <= 2:
                continue
            for kt in range(17):
                bank = B[kt // 8]
                for h in range(4):
                    lhs = KdT_b[:, h, kt * 128:(kt + 1) * 128] if kt < 16 else KdTn[:, h, :]
                    c0 = (kt % 8) * 64 + h * 16
                    P.op("pe", [KdT_b, KdTn, QdTblk], [bank], lambda bank=bank, lhs=lhs, c0=c0, h=h: nc.tensor.matmul(
                        bank[:, c0:c0 + 16], lhsT=lhs, rhs=QdTblk[:, h, b, :, :].rearrange("p a b -> p (a b)"),
                        start=True, stop=True), inc=(h == 3 and (kt % 8 == 7 or kt == 16)))
            for bk, k0, nk in ((0, 0, 8), (1, 8, 8), (2, 16, 1)):
                P.op("act", [B[bk]], [PTd], lambda bk=bk, k0=k0, nk=nk: nc.scalar.activation(
                    out=PTd[:, k0:k0 + nk, :].rearrange("p a b -> p (a b)"), in_=B[bk][:, 0:nk * 64], func=AF.Exp, scale=0.125))
            P.op("dve", [PTd, sc["maskN"]], [PTd], lambda: nc.vector.tensor_tensor(
                out=PTd[:, 16, :].rearrange("p (a q) -> p a q", q=8), in0=PTd[:, 16, :].rearrange("p (a q) -> p a q", q=8),
                in1=sc["maskN"][:, b, :].unsqueeze(1).to_broadcast([128, 8, 8]), op=ALU.mult))
            for h in range(4):
                for kt in range(17):
                    v = pg[:, kt, 512 + h * 128:512 + (h + 1) * 128] if kt < 16 else vall_s[:, h * 128:(h + 1) * 128]
                    P.op("pe", [pg, vall_s, PTd], [B[3]], lambda v=v, kt=kt, h=h: nc.tensor.matmul(
                        B[3][:, h * 16:(h + 1) * 16], lhsT=v, rhs=PTd[:, kt, h * 16:(h + 1) * 16],
                        start=(kt == 0), stop=(kt == 16)), inc=(kt == 16))
            for kt in range(17):
                P.op("pe", [ones_bf, PTd], [B[4]], lambda kt=kt: nc.tensor.matmul(
                    B[4][:, 0:64], lhsT=ones_bf[:, :], rhs=PTd[:, kt, :], start=(kt == 0), stop=(kt == 16)), inc=(kt == 16))
            if b + 1 < n_sbatch:
                gather_d(b + 1)
            P.op("dve", [B[4]], [rl_sb], lambda: nc.vector.reciprocal(out=rl_sb[:], in_=B[4][:, 0:64]))
            Yv = B[3][:, 0:64].rearrange("p (h c q) -> p h c q", h=4, c=2)
            rv = rl_sb[:, :].rearrange("p (h c q) -> p h c q", h=4, c=2)
            t1 = t_sb[0][:, :].rearrange("p (h q) -> p h q", q=8)
            t2 = t_sb[1][:, :].rearrange("p (h q) -> p h q", q=8)
            P.op("dve", [B[3], rl_sb], [t_sb[0]], lambda: nc.vector.tensor_tensor(out=t1, in0=Yv[:, :, 0, :], in1=rv[:, :, 0, :], op=ALU.mult))
            P.op("dve", [B[3], rl_sb], [t_sb[1]], lambda: nc.vector.tensor_tensor(out=t2, in0=Yv[:, :, 1, :], in1=rv[:, :, 1, :], op=ALU.mult))
            P.op("dve", [t_sb[0], t_sb[1], neg_lam], [od_s], lambda: nc.vector.scalar_tensor_tensor(
                out=od_s[:], in0=t_sb[1][:], scalar=neg_lam[:, 0:1], in1=t_sb[0][:], op0=ALU.mult, op1=ALU.add))
            P.op("dve", [od_s], [t_sb[0]], lambda: nc.vector.tensor_tensor(out=t_sb[0][:], in0=od_s[:], in1=od_s[:], op=ALU.mult))
            P.op("pe", [ones_f, t_sb[0]], [B[5]], lambda: nc.tensor.matmul(
                B[5][:, 0:32], lhsT=ones_f[:, :], rhs=t_sb[0][:, :], start=True, stop=True))
            P.op("act", [B[5], eps_t], [t_sb[1]], lambda: nc.scalar.activation(
                out=t_sb[1][:], in_=B[5][:, 0:32], func=AF.Ln, bias=eps_t[:], scale=1.0 / 128))
            P.op("act", [t_sb[1]], [t_sb[1]], lambda: nc.scalar.activation(out=t_sb[1][:], in_=t_sb[1][:], func=AF.Exp, scale=-0.5))
            P.op("dve", [od_s, sgT, t_sb[1]], [omixT_s], lambda: nc.vector.scalar_tensor_tensor(
                out=omixT_s[:, 0:4, b * 8:(b + 1) * 8], in0=od_s[:, :].rearrange("p (h q) -> p h q", q=8),
                scalar=sgT[:, 0:1], in1=t_sb[1][:, :].rearrange("p (h q) -> p h q", q=8), op0=ALU.mult, op1=ALU.mult))

            if sstop <= 3:
                continue
            for sp_ in range(8):
                for s2 in range(2):
                    for w in range(3):
                        i = s2 * 3 + w
                        P.op("pe", [pgn, C["ident_b"]], [psT], lambda sp_=sp_, s2=s2, w=w, i=i: nc.tensor.transpose(
                            out=psT[:, i * 128:(i + 1) * 128], in_=pgn[:, sp_ * 2 + s2, w * 128:(w + 1) * 128],
                            identity=C["ident_b"][:]), inc=(i == 5))
                evac_copy(psT, psT[:, 0:768].rearrange("p (s w t) -> p s w t", s=2, w=3), CS_b,
                          CS_b[:, :, sp_ * 256:(sp_ + 1) * 256].rearrange("p w (s t) -> p s w t", s=2))
            for wt in range(4):
                P.op("pe", [wst, C["ident_b"]], [psT], lambda wt=wt: nc.tensor.transpose(
                    out=psT[:, wt * 128:(wt + 1) * 128], in_=wst[:, wt, 0:128], identity=C["ident_b"][:]), inc=(wt == 3))
            evac_copy(psT, psT[:, 0:512], WkT_b, WkT_b[:, :])
            if sstop <= 4:
                continue
            for w in range(2):
                for kvh in range(2):
                    lo = 64 * kvh
                    for l in range(32):
                        P.op("pe", [W1, CS_b], [B[5]], lambda w=w, l=l, lo=lo: nc.tensor.matmul(
                            B[5][:, 0:N_CMP_S], lhsT=W1[lo:lo + 64, w, l, :],
                            rhs=CS_b[lo:lo + 64, w, l:l + 16 * (N_CMP_S - 1) + 1:16],
                            start=(l == 0), stop=(l == 31)), inc=(l == 31))
                    gelu_s(B[5], N_CMP_S, cbias[:, w:w + 1], hcb, gtmp[0], gtmp[1])
                    if w == 0:
                        P.op("pe", [W2d, hcb], [B[6]], lambda: nc.tensor.matmul(
                            B[6][:, 0:128], lhsT=W2d[:, 0, :], rhs=hcb[:, 0:128], start=True, stop=True))
                        P.op("dve", [B[6]], [KCT_b], lambda lo=lo: nc.vector.tensor_copy(
                            out=KCT_b[lo:lo + 64, :], in_=B[6][lo:lo + 64, 0:128]))
                    else:
                        P.op("pe", [W2d, hcb], [B[6]], lambda: nc.tensor.matmul(
                            B[6][:, 0:64], lhsT=hcb[:, 0:128], rhs=W2d[:, 1, 0:64], start=True, stop=True))
                        P.op("dve", [B[6]], [VC_b], lambda lo=lo: nc.vector.tensor_copy(out=VC_b[:, lo:lo + 64], in_=B[6][:, 0:64]))
            if sstop <= 5:
                continue
            P.op("pe", [KCT_b, QnTblk], [B[5]], lambda: nc.tensor.matmul(
                B[5][:, 0:64], lhsT=KCT_b[:, :], rhs=QnTblk[:, b, :, :, :].rearrange("p k g q -> p (k g q)"),
                start=True, stop=True))
            if sstop <= 5.2:
                continue
            P.op("act", [B[5]], [PcT], lambda: nc.scalar.activation(out=PcT[:], in_=B[5][:, 0:64], func=AF.Exp, scale=0.125))
            if sstop <= 5.4:
                continue
            P.op("pe", [VC_b, PcT], [B[3]], lambda: nc.tensor.matmul(B[3][:, 0:64], lhsT=VC_b[:, :], rhs=PcT[:, :], start=True, stop=True))
            P.op("pe", [sc["ones127"], PcT], [B[4]], lambda: nc.tensor.matmul(
                B[4][:, 0:64], lhsT=sc["ones127"][:, :], rhs=PcT[:, :], start=True, stop=True))
            P.op("pe", [sc["overlap_s"], PcT], [B[6]], lambda: nc.tensor.matmul(
                B[6][0:64, 0:64], lhsT=sc["overlap_s"][:, :], rhs=PcT[:, :], start=True, stop=True))
            if sstop <= 5.6:
                continue
            nsa_branch_evac(B[3], B[4], b, 0, True)
            if sstop <= 6:
                continue
            P.op("dve", [B[6], rl_sb], [tmpP], lambda: nc.vector.tensor_tensor(
                out=tmpP[:], in0=B[6][0:64, 0:64], in1=rl_sb[0:64, :], op=ALU.mult))
            tp = tmpP[:, :].rearrange("p (k g q) -> p k g q", k=2, g=4)
            pv = pselT[:, :].rearrange("p (k q) -> p k q", k=2)
            P.op("dve", [tmpP], [pselT], lambda: nc.vector.tensor_tensor(out=pv, in0=tp[:, :, 0, :], in1=tp[:, :, 1, :], op=ALU.add))
            P.op("dve", [tmpP, pselT], [pselT], lambda: nc.vector.tensor_tensor(out=pv, in0=pv, in1=tp[:, :, 2, :], op=ALU.add))
            P.op("dve", [tmpP, pselT], [pselT], lambda: nc.vector.tensor_tensor(out=pv, in0=pv, in1=tp[:, :, 3, :], op=ALU.add))
            P.op("pe", [pselT, C["ident_f"]], [B[6]], lambda: nc.tensor.transpose(
                out=B[6][0:16, 64:128], in_=pselT[0:64, :], identity=C["ident_f"][0:64, 0:64]))
            P.op("dve", [B[6], sc["bonus_s"]], [score], lambda: nc.vector.tensor_tensor(
                out=score[:], in0=B[6][0:16, 64:128], in1=sc["bonus_s"][:], op=ALU.add))
            P.op("dve", [score], [m8[0]], lambda: nc.vector.max(out=m8[0][:], in_=score[:]))
            P.op("dve", [score, m8[0]], [swork], lambda: nc.vector.match_replace(
                out=swork[:], in_to_replace=m8[0][:], in_values=score[:], imm_value=-2e9))
            P.op("dve", [swork], [m8[1]], lambda: nc.vector.max(out=m8[1][:], in_=swork[:]))
            P.op("dve", [score, m8[1]], [bm_s], lambda: nc.vector.tensor_scalar(
                out=bm_s[:], in0=score[:], scalar1=m8[1][:, 7:8], scalar2=None, op0=ALU.is_ge))
            P.op("pe", [bm_s, C["ident_b"]], [psT], lambda: nc.tensor.transpose(
                out=psT[0:64, 0:16], in_=bm_s[0:16, :], identity=C["ident_b"][0:16, 0:16]))
            P.op("dve", [psT], [bmT2], lambda: nc.vector.tensor_copy(out=bmT2[:], in_=psT[0:64, 0:16]))
            if sstop <= 7:
                continue
            for kt in range(17):
                bank = B[kt // 8]
                lhs = CS_b[:, 2, kt * 128:(kt + 1) * 128] if kt < 16 else NTn[:, 0, :]
                c0 = (kt % 8) * 64
                P.op("pe", [CS_b, NTn, QnTblk], [bank], lambda bank=bank, lhs=lhs, c0=c0: nc.tensor.matmul(
                    bank[:, c0:c0 + 64], lhsT=lhs, rhs=QnTblk[:, b, :, :, :].rearrange("p k g q -> p (k g q)"),
                    start=True, stop=True), inc=(kt % 8 == 7 or kt == 16))
            for kt in range(17):
                P.op("pe", [sc["expand_s"], bmT2], [B[5]], lambda kt=kt: nc.tensor.matmul(
                    B[5][:, kt * 16:(kt + 1) * 16], lhsT=sc["expand_s"][:, kt, :], rhs=bmT2[:, :], start=True, stop=True),
                    inc=(kt == 16))
            for bk, k0, nk in ((0, 0, 8), (1, 8, 8), (2, 16, 1)):
                P.op("act", [B[bk]], [PsT], lambda bk=bk, k0=k0, nk=nk: nc.scalar.activation(
                    out=PsT[:, k0:k0 + nk, :].rearrange("p a b -> p (a b)"), in_=B[bk][:, 0:nk * 64], func=AF.Exp, scale=0.125))
            P.op("dve", [B[5]], [msk_sb], lambda: nc.vector.tensor_copy(
                out=msk_sb[:].rearrange("p a b -> p (a b)"), in_=B[5][:, 0:272]))
            P.op("dve", [msk_sb, sc["maskN"]], [msk_sb], lambda: nc.vector.tensor_tensor(
                out=msk_sb[:, 16, :].rearrange("p (k q) -> p k q", k=2), in0=msk_sb[:, 16, :].rearrange("p (k q) -> p k q", k=2),
                in1=sc["maskN"][:, b, :].unsqueeze(1).to_broadcast([128, 2, 8]), op=ALU.mult))
            P.op("dve", [PsT, msk_sb], [PsT], lambda: nc.vector.tensor_tensor(
                out=PsT[:].rearrange("p t (k g q) -> p (t k) g q", k=2, g=4),
                in0=PsT[:].rearrange("p t (k g q) -> p (t k) g q", k=2, g=4),
                in1=msk_sb[:].rearrange("p t (k q) -> p (t k) q", k=2).unsqueeze(2).to_broadcast([128, 34, 4, 8]), op=ALU.mult))
            for kt in range(17):
                v = pgn[:, kt, 384:512] if kt < 16 else vall_s[:, 512:640]
                P.op("pe", [pgn, vall_s, PsT], [B[3]], lambda v=v, kt=kt: nc.tensor.matmul(
                    B[3][:, 0:64], lhsT=v, rhs=PsT[:, kt, :], start=(kt == 0), stop=(kt == 16)), inc=(kt == 16))
            for kt in range(17):
                P.op("pe", [ones_bf, PsT], [B[4]], lambda kt=kt: nc.tensor.matmul(
                    B[4][:, 0:64], lhsT=ones_bf[:, :], rhs=PsT[:, kt, :], start=(kt == 0), stop=(kt == 16)), inc=(kt == 16))
            if b + 1 < n_sbatch:
                gather_n(b + 1)
            nsa_branch_evac(B[3], B[4], b, 1, False)
            if sstop <= 8:
                continue
            for kt in range(5):
                lhs = WkT_b[:, kt * 128:(kt + 1) * 128] if kt < 4 else NTn[:, 1, :]
                c0 = kt * 64
                P.op("pe", [WkT_b, NTn, QnTblk], [B[0]], lambda lhs=lhs, c0=c0: nc.tensor.matmul(
                    B[0][:, c0:c0 + 64], lhsT=lhs, rhs=QnTblk[:, b, :, :, :].rearrange("p k g q -> p (k g q)"),
                    start=True, stop=True), inc=(kt == 4))
            P.op("act", [B[0]], [PsT], lambda: nc.scalar.activation(
                out=PsT[:, 0:5, :].rearrange("p a b -> p (a b)"), in_=B[0][:, 0:320], func=AF.Exp, scale=0.125))
            P.op("dve", [PsT, sc["maskW0"]], [PsT], lambda: nc.vector.tensor_tensor(
                out=PsT[:, 0, :].rearrange("p (a q) -> p a q", q=8), in0=PsT[:, 0, :].rearrange("p (a q) -> p a q", q=8),
                in1=sc["maskW0"][:, :].unsqueeze(1).to_broadcast([128, 8, 8]), op=ALU.mult))
            P.op("dve", [PsT, sc["maskN"]], [PsT], lambda: nc.vector.tensor_tensor(
                out=PsT[:, 4, :].rearrange("p (a q) -> p a q", q=8), in0=PsT[:, 4, :].rearrange("p (a q) -> p a q", q=8),
                in1=sc["maskN"][:, b, :].unsqueeze(1).to_broadcast([128, 8, 8]), op=ALU.mult))
            for kt in range(5):
                v = wst[:, kt, 128:256] if kt < 4 else vall_s[:, 640:768]
                P.op("pe", [wst, vall_s, PsT], [B[3]], lambda v=v, kt=kt: nc.tensor.matmul(
                    B[3][:, 0:64], lhsT=v, rhs=PsT[:, kt, :], start=(kt == 0), stop=(kt == 4)), inc=(kt == 4))
            for kt in range(5):
                P.op("pe", [ones_bf, PsT], [B[4]], lambda kt=kt: nc.tensor.matmul(
                    B[4][:, 0:64], lhsT=ones_bf[:, :], rhs=PsT[:, kt, :], start=(kt == 0), stop=(kt == 4)), inc=(kt == 4))
            if b + 1 < n_sbatch:
                gather_w(b + 1)
            nsa_branch_evac(B[3], B[4], b, 2, False)
            P.op("act", [accT], [omixT_s], lambda: nc.scalar.copy(
                out=omixT_s[:, 4:8, b * 8:(b + 1) * 8], in_=accT[:, :].rearrange("p (g q) -> p g q", q=8)))

        if n_sbatch < NB_S:
            pass
        out_proj_and_delta(None, omixT_s, wo_s, [B[0], B[1]], lambda half: D1s[:, half * 512:(half + 1) * 512], 0,
                           skip_transpose=True, d1_t=D1s)
        P.barrier()
        sst.__exit__(None, None, None)

    D1 = P.sbuf("D1", [128, NT_OWN, D_MODEL], BF16)
    P.op("pool", [], [D1], lambda: nc.gpsimd.memset(D1[:], 0.0))
    g1st = contextlib.ExitStack()
    g1st.__enter__()
    load_gains(("g_pre_mix", "g_post_mix"), g1st)

    if do_prompt:
        pst = contextlib.ExitStack()
        pst.__enter__()
        odiff = P.sbuf("odiff", [128, NT_OWN, 512], BF16, pst)
        P.op("pool", [], [odiff], lambda: nc.gpsimd.memset(odiff[:], 0.0))
        rope_own = P.sbuf("sb_rope_own", cshape["rope_own"], F32, pst)
        P.dma("sp", rope_own[:], cdram["rope_own"], writes=[rope_own])
        mask_c = P.sbuf("sb_mask_c", cshape["mask_c"], BF16, pst)
        P.dma("sp", mask_c[:], cdram["mask_c"], writes=[mask_c])
        xt = [P.sbuf("xt%d" % i, [128, D_MODEL], F32, pst) for i in range(2)]
        hb = [P.sbuf("hb%d" % i, [128, D_MODEL], BF16, pst) for i in range(2)]
        hT = [P.sbuf("hT%d" % i, [128, 8, 128], BF16, pst) for i in range(2)]
        PT = [P.sbuf("PT%d" % i, [128, 4, 128], BF16, pst) for i in range(6)]
        pt_i = [0]

        def load_x(src, g, slot):
            P.dma("sp", xt[slot][:], src[g * 128:(g + 1) * 128, :], writes=[xt[slot]])

        ast = contextlib.ExitStack()
        ast.__enter__()
        KdT = P.sbuf("KdT", [128, 4, SEQ], BF16, ast)
        Vd = P.sbuf("Vd", [128, NT_ALL, 4, 129], BF16, ast)
        P.op("pool", [], [Vd], lambda: nc.gpsimd.memset(Vd[:], 1.0))
        rope_all = P.sbuf("sb_rope_all", cshape["rope_all"], F32, ast)
        P.dma("sp", rope_all[:], cdram["rope_all"], writes=[rope_all])
        wA = P.sbuf("wA", [128, 8, 1536], BF16, ast)
        for kc in range(8):
            P.dma("pool", wA[:, kc, 0:1024], w_in3[:, kc, 512:1536], writes=[wA])
        for kc in range(8):
            P.dma("pool", wA[:, kc, 1024:1536], w_in3[:, kc, 0:512], writes=[wA])
        kvd = [P.sbuf("kvd%d" % i, [128, 1024], F32, ast) for i in range(2)]
        psA, psB, psKT = B[0], B[1], B[2]

        psA2 = [B[0], B[3]]
        psB2 = [B[1], B[4]]

        def a1_post(g):
            s = g % 2
            pA, pB = psA2[s], psB2[s]
            cos = rope_all[:, 0, g, :]
            sin = rope_all[:, 1, g, :]
            rope("dve", pA, pA[:, 0:512].rearrange("p (h x) -> p h x", x=64), kvd[s],
                 kvd[s][:, 0:512].rearrange("p (h x) -> p h x", x=64), cos, sin, 8, rtmp, rope_all)
            P.op("act", [pB], [kvd[s]], lambda: nc.scalar.copy(out=kvd[s][:, 512:1024], in_=pB[:, 0:512]))
            P.dma("sp", o_dkv_p[g * 128:(g + 1) * 128, :], kvd[s][:], reads=[kvd[s]], is_output=True)
            for h in range(4):
                P.op("pe", [kvd[s], C["ident_f"]], [psKT],
                     lambda h=h: nc.tensor.transpose(out=psKT[:, h * 128:(h + 1) * 128],
                                                     in_=kvd[s][:, h * 128:(h + 1) * 128], identity=C["ident_f"][:]),
                     inc=(h == 3))
            P.op("act", [psKT], [KdT], lambda: nc.scalar.copy(
                out=KdT[:, :, g * 128:(g + 1) * 128], in_=psKT[:, :].rearrange("p (h t) -> p h t", h=4)))
            P.op("dve", [kvd[s]], [Vd], lambda: nc.vector.tensor_copy(
                out=Vd[:, g, :, 0:128], in_=kvd[s][:, 512:1024].rearrange("p (h e) -> p h e", h=4)))

        if n_kpass > 0:
            load_x(x_all, 0, 0)
            front(xt[0], hb[0], hT[0], 0, "g_pre_mix")
        for g in range(n_kpass):
            s = g % 2
            if g + 1 < n_kpass:
                load_x(x_all, g + 1, (g + 1) % 2)
            proj(psA2[s], 512, hT[s], wA, 0)
            proj(psB2[s], 512, hT[s], wA, 512)
            if g + 1 < n_kpass:
                front(xt[(g + 1) % 2], hb[(g + 1) % 2], hT[(g + 1) % 2], (g + 1) % 2, "g_pre_mix")
            a1_post(g)

        qd_b = [P.sbuf("qd_b%d" % i, [128, 512], BF16, ast) for i in range(2)]
        QdT = [P.sbuf("QdT%d" % i, [128, 4, 128], BF16, ast) for i in range(2)]
        o1_sb = P.sbuf("o1_sb", [128, 128], F32, ast)
        od_sb = P.sbuf("od_sb", [128, 128], F32, ast)
        psQ = B[0]
        psS = [B[1], B[2]]
        psO = [B[3], B[4]]
        gi = [0]
        oi = [0]

        def a2_pre(j):
            s = j % 2
            load_x(x_own, j, s)
            front(xt[s], hb[s], hT[s], s, "g_pre_mix")
            proj(psQ, 512, hT[s], wA, 1024)
            rope("dve", psQ, psQ[:, 0:512].rearrange("p (h x) -> p h x", x=64), qd_b[s],
                 qd_b[s][:, :].rearrange("p (h x) -> p h x", x=64), rope_own[:, 0, j, :], rope_own[:, 1, j, :], 8, rtmp, rope_own)
            transpose_to(qd_b[s], QdT[s], nchunks=4, evac="dve")

        def a2_tasks(j):
            s = j % 2
            par = j % 2
            nkt = 2 * j + 2
            tasks = []
            for h in range(4):
                acc = {}
                for c in range(2):
                    po = psO[oi[0] % 2]
                    oi[0] += 1
                    acc[c] = po
                    groups = [list(range(a, min(a + 4, nkt))) for a in range(0, nkt, 4)]
                    for gidx, grp in enumerate(groups):
                        ps = psS[gi[0] % 2]
                        pt = PT[pt_i[0] % 6]
                        gi[0] += 1
                        pt_i[0] += 1

                        def s1(h=h, c=c, grp=grp, ps=ps, pt=pt):
                            for i, kt in enumerate(grp):
                                P.op("pe", [KdT, QdT[s]], [ps],
                                     lambda i=i, kt=kt: nc.tensor.matmul(
                                         ps[:, i * 128:(i + 1) * 128],
                                         lhsT=KdT[64 * c:64 * c + 64, h, kt * 128:(kt + 1) * 128],
                                         rhs=QdT[s][64 * c:64 * c + 64, h, :], start=True, stop=True),
                                     inc=(i == len(grp) - 1))
                            n = len(grp) * 128
                            P.op("act", [ps], [pt], lambda: nc.scalar.activation(
                                out=pt[:].rearrange("p a b -> p (a b)")[:, 0:n], in_=ps[:, 0:n], func=AF.Exp, scale=0.125))
                            if grp[-1] == nkt - 1:
                                i0 = len(grp) - 2
                                P.op("dve", [pt, mask_c], [pt], lambda: nc.vector.tensor_tensor(
                                    out=pt[:, i0:i0 + 2, :], in0=pt[:, i0:i0 + 2, :], in1=mask_c[:, par, :, :], op=ALU.mult))

                        def s2(h=h, c=c, grp=grp, pt=pt, po=po, acc=acc):
                            for i, kt in enumerate(grp):
                                P.op("pe", [pt, Vd], [po],
                                     lambda i=i, kt=kt: nc.tensor.matmul(
                                         po[:, 0:129], lhsT=pt[:, i, :], rhs=Vd[:, kt, h, :],
                                         start=(kt == 0), stop=(kt == nkt - 1)),
                                     inc=(kt == nkt - 1))
                            if grp[-1] != nkt - 1:
                                return
                            rl = col()
                            P.op("dve", [po], [rl], lambda: nc.vector.reciprocal(out=rl[:], in_=po[:, 128:129]))
                            if c == 0:
                                P.op("dve", [po, rl], [o1_sb], lambda: nc.vector.tensor_scalar(
                                    out=o1_sb[:], in0=po[:, 0:128], scalar1=rl[:, 0:1], scalar2=None, op0=ALU.mult))
                                return
                            P.op("dve", [rl, neg_lam], [rl], lambda: nc.vector.tensor_tensor(
                                out=rl[:], in0=rl[:], in1=neg_lam[:], op=ALU.mult))
                            P.op("dve", [po, rl, o1_sb], [od_sb], lambda: nc.vector.scalar_tensor_tensor(
                                out=od_sb[:], in0=po[:, 0:128], scalar=rl[:, 0:1], in1=o1_sb[:], op0=ALU.mult, op1=ALU.add))
                            ss, ln, rs = col(), col(), col()
                            P.op("act", [od_sb], [junk, ss], lambda: nc.scalar.activation(
                                out=junk[:, 0:128], in_=od_sb[:], func=AF.Square, accum_out=ss[:]))
                            P.op("act", [ss, eps_t], [ln], lambda: nc.scalar.activation(
                                out=ln[:], in_=ss[:], func=AF.Ln, bias=eps_t[:], scale=1.0 / 128))
                            P.op("act", [ln], [rs], lambda: nc.scalar.activation(out=rs[:], in_=ln[:], func=AF.Exp, scale=-0.5))
                            P.op("dve", [od_sb, rs, sg], [odiff], lambda: nc.vector.scalar_tensor_tensor(
                                out=odiff[:, j, h * 128:(h + 1) * 128], in0=od_sb[:], scalar=rs[:, 0:1], in1=sg[:],
                                op0=ALU.mult, op1=ALU.mult))

                        tasks.append((s1, s2))
            return tasks

        if n_qtiles > 0:
            a2_pre(0)
        for j in range(n_qtiles):
            tasks = a2_tasks(j)
            if j + 1 < n_qtiles:
                tasks.insert(len(tasks) // 2, (lambda j=j: a2_pre(j + 1), noop))
            pipeline(tasks)
        P.barrier()
        ast.__exit__(None, None, None)

        bst = contextlib.ExitStack()
        bst.__enter__()
        ST = P.sbuf("ST", [128, 2, SEQ], BF16, bst)
        SV = P.sbuf("SV", [128, NT_ALL, 2, 65], BF16, bst)
        WV = P.sbuf("WV", [128, NT_ALL, 2, 65], BF16, bst)
        P.op("pool", [], [SV], lambda: nc.gpsimd.memset(SV[:], 1.0))
        P.op("pool", [], [WV], lambda: nc.gpsimd.memset(WV[:], 1.0))
        KCT = P.sbuf("KCT", [128, 256], BF16, bst)
        VCX = P.sbuf("VCX", [128, 2, 2, 129], BF16, bst)
        P.op("pool", [], [VCX], lambda: nc.gpsimd.memset(VCX[:], 0.0))
        expand = P.sbuf("sb_expand", cshape["expand"], BF16, bst)
        P.dma("sp", expand[:], cdram["expand"], writes=[expand])
        mask_w = P.sbuf("sb_mask_w", cshape["mask_w"], BF16, bst)
        P.dma("sp", mask_w[:], cdram["mask_w"], writes=[mask_w])

        cst = contextlib.ExitStack()
        cst.__enter__()
        CT = P.sbuf("CT", [128, 2, SEQ], BF16, cst)
        rope_all = P.sbuf("sb_rope_all2", cshape["rope_all"], F32, cst)
        P.dma("sp", rope_all[:], cdram["rope_all"], writes=[rope_all])
        wkn = P.sbuf("wkn", [128, 8, 768], BF16, cst)
        for kc in range(8):
            P.dma("pool", wkn[:, kc, :], w_in3[:, kc, 2048:2816], writes=[wkn])
        W1 = P.sbuf("W1", [128, 2, 32, 128], BF16, cst)
        W2d = P.sbuf("W2d", [128, 2, 128], BF16, cst)
        for w in range(2):
            for dup in range(2):
                P.dma("pool", W1[64 * dup:64 * dup + 64, w, :, :],
                      cmp_w1[w].rearrange("(l d) f -> d l f", d=64), writes=[W1])
                P.dma("pool", W2d[:, w, 64 * dup:64 * dup + 64], cmp_w2[w], writes=[W2d])
        pos_sb = P.sbuf("pos_sb", [32, 2, 64], F32, cst)
        for w in range(2):
            P.dma("sp", pos_sb[:, w, :], cmp_pos[w], writes=[pos_sb])
        ovl = P.sbuf("sb_overlap", cshape["overlap"], BF16, cst)
        P.dma("sp", ovl[:], cdram["overlap"], writes=[ovl])
        kvn = [P.sbuf("kvn%d" % i, [128, 768], F32, cst) for i in range(2)]
        psC, psD, psNT = B[0], B[1], B[2]

        psC2 = [B[0], B[3]]
        psD2 = [B[1], B[4]]

        def b1_post(g):
            s = g % 2
            psC, psD = psC2[s], psD2[s]
            cos = rope_all[:, 0, g, :]
            sin = rope_all[:, 1, g, :]
            for slot in (0, 2):
                rope("dve", psC, psC[:, slot * 128:(slot + 1) * 128].rearrange("p (h x) -> p h x", x=64), kvn[s],
                     kvn[s][:, slot * 128:(slot + 1) * 128].rearrange("p (h x) -> p h x", x=64), cos, sin, 2, rtmp, rope_all)
            rope("dve", psD, psD[:, 0:128].rearrange("p (h x) -> p h x", x=64), kvn[s],
                 kvn[s][:, 512:640].rearrange("p (h x) -> p h x", x=64), cos, sin, 2, rtmp, rope_all)
            for slot in (1, 3):
                P.op("act", [psC], [kvn[s]], lambda slot=slot: nc.scalar.copy(
                    out=kvn[s][:, slot * 128:(slot + 1) * 128], in_=psC[:, slot * 128:(slot + 1) * 128]))
            P.op("act", [psD], [kvn[s]], lambda: nc.scalar.copy(out=kvn[s][:, 640:768], in_=psD[:, 128:256]))
            P.dma("sp", o_nkv_p[g * 128:(g + 1) * 128, :], kvn[s][:, 0:512], reads=[kvn[s]], is_output=True)
            if g >= NT_ALL - 4:
                gg = g - (NT_ALL - 4)
                P.dma("sp", o_wkv_p[gg * 128:(gg + 1) * 128, :], kvn[s][:, 512:768], reads=[kvn[s]], is_output=True)
            for wi, slot in enumerate((0, 1, 2, 4)):
                P.op("pe", [kvn[s], C["ident_f"]], [psNT],
                     lambda wi=wi, slot=slot: nc.tensor.transpose(
                         out=psNT[:, wi * 128:(wi + 1) * 128], in_=kvn[s][:, slot * 128:(slot + 1) * 128],
                         identity=C["ident_f"][:]),
                     inc=(wi == 3))
            P.op("act", [psNT], [CT], lambda: nc.scalar.copy(
                out=CT[:, :, g * 128:(g + 1) * 128], in_=psNT[:, 0:256].rearrange("p (h t) -> p h t", h=2)))
            P.op("act", [psNT], [ST], lambda: nc.scalar.copy(
                out=ST[:, :, g * 128:(g + 1) * 128], in_=psNT[:, 256:512].rearrange("p (h t) -> p h t", h=2)))
            P.op("dve", [kvn[s]], [SV], lambda: nc.vector.tensor_copy(
                out=SV[:, g, :, 0:64], in_=kvn[s][:, 384:512].rearrange("p (h e) -> p h e", h=2)))
            P.op("dve", [kvn[s]], [WV], lambda: nc.vector.tensor_copy(
                out=WV[:, g, :, 0:64], in_=kvn[s][:, 640:768].rearrange("p (h e) -> p h e", h=2)))

        if n_kpass > 0:
            load_x(x_all, 0, 0)
            front(xt[0], hb[0], hT[0], 0, "g_pre_mix")
        for g in range(n_kpass):
            s = g % 2
            if g + 1 < n_kpass:
                load_x(x_all, g + 1, (g + 1) % 2)
            proj(psC2[s], 512, hT[s], wkn, 0)
            proj(psD2[s], 256, hT[s], wkn, 512)
            if g + 1 < n_kpass:
                front(xt[(g + 1) % 2], hb[(g + 1) % 2], hT[(g + 1) % 2], (g + 1) % 2, "g_pre_mix")
            b1_post(g)

        posT = P.sbuf("posT", [64, 2, 32], BF16, cst)
        cbias = P.sbuf("cbias", [128, 2], F32, cst)
        psX = B[3]
        for w in range(2):
            P.op("pe", [pos_sb, C["ident_f"]], [psX], lambda w=w: nc.tensor.transpose(
                out=psX[0:64, w * 32:(w + 1) * 32], in_=pos_sb[0:32, w, :], identity=C["ident_f"][0:32, 0:32]))
        P.op("dve", [psX], [posT], lambda: nc.vector.tensor_copy(
            out=posT[:].rearrange("p a b -> p (a b)"), in_=psX[0:64, 0:64]))
        psX2 = B[4]
        for w in range(2):
            for l in range(32):
                P.op("pe", [W1, posT], [psX2], lambda w=w, l=l: nc.tensor.matmul(
                    psX2[:, w:w + 1], lhsT=W1[0:64, w, l, :], rhs=posT[0:64, w, l:l + 1],
                    start=(l == 0), stop=(l == 31)), inc=(l == 31))
        P.op("dve", [psX2], [cbias], lambda: nc.vector.tensor_copy(out=cbias[:], in_=psX2[:, 0:2]))

        def gelu_to(src_ps, n, bias_col, dst_bf, tmpa, tmpb):
            P.op("dve", [src_ps, cbias], [tmpa], lambda: nc.vector.tensor_scalar(
                out=tmpa[:, 0:n], in0=src_ps[:, 0:n], scalar1=bias_col, scalar2=None, op0=ALU.add))
            P.op("dve", [tmpa], [tmpb], lambda: nc.vector.tensor_tensor(
                out=tmpb[:, 0:n], in0=tmpa[:, 0:n], in1=tmpa[:, 0:n], op=ALU.mult))
            P.op("dve", [tmpb], [tmpb], lambda: nc.vector.tensor_scalar(
                out=tmpb[:, 0:n], in0=tmpb[:, 0:n], scalar1=0.044715, scalar2=1.0, op0=ALU.mult, op1=ALU.add))
            P.op("dve", [tmpb, tmpa], [tmpb], lambda: nc.vector.tensor_tensor(
                out=tmpb[:, 0:n], in0=tmpb[:, 0:n], in1=tmpa[:, 0:n], op=ALU.mult))
            P.op("act", [tmpb], [tmpb], lambda: nc.scalar.activation(
                out=tmpb[:, 0:n], in_=tmpb[:, 0:n], func=AF.Exp, scale=-1.5957691216057308))
            P.op("dve", [tmpb], [tmpb], lambda: nc.vector.tensor_scalar(
                out=tmpb[:, 0:n], in0=tmpb[:, 0:n], scalar1=1.0, scalar2=None, op0=ALU.add))
            P.op("dve", [tmpb], [tmpb], lambda: nc.vector.reciprocal(out=tmpb[:, 0:n], in_=tmpb[:, 0:n]))
            P.op("dve", [tmpb, tmpa], [dst_bf], lambda: nc.vector.tensor_tensor(
                out=dst_bf[:, 0:n], in0=tmpb[:, 0:n], in1=tmpa[:, 0:n], op=ALU.mult))

        gtmp = [P.sbuf("gtmp%d" % i, [128, 256], F32, cst) for i in range(2)]
        hcb = P.sbuf("hcb", [128, 256], BF16, cst)
        P.op("dve", [], [hcb], lambda: nc.vector.memset(hcb[:], 0.0))
        psH, psK = B[5], B[6]
        for w in range(2):
            for kvh in range(2):
                lo = 64 * kvh
                for l in range(32):
                    P.op("pe", [W1, CT], [psH], lambda w=w, l=l, lo=lo: nc.tensor.matmul(
                        psH[:, 0:N_CMP_P], lhsT=W1[lo:lo + 64, w, l, :],
                        rhs=CT[lo:lo + 64, w, l:l + 16 * (N_CMP_P - 1) + 1:16],
                        start=(l == 0), stop=(l == 31)), inc=(l == 31))
                gelu_to(psH, N_CMP_P, cbias[:, w:w + 1], hcb, gtmp[0], gtmp[1])
                if w == 0:
                    P.op("pe", [W2d, hcb], [psK], lambda: nc.tensor.matmul(
                        psK[:, 0:256], lhsT=W2d[:, 0, :], rhs=hcb[:, 0:256], start=True, stop=True))
                    P.op("dve", [psK], [KCT], lambda lo=lo: nc.vector.tensor_copy(
                        out=KCT[lo:lo + 64, :], in_=psK[lo:lo + 64, 0:256]))
                else:
                    for nt in range(2):
                        P.op("pe", [W2d, hcb], [psK], lambda nt=nt: nc.tensor.matmul(
                            psK[:, nt * 64:(nt + 1) * 64], lhsT=hcb[:, nt * 128:(nt + 1) * 128], rhs=W2d[:, 1, 0:64],
                            start=True, stop=True))
                    P.op("dve", [psK], [VCX], lambda kvh=kvh: nc.vector.tensor_copy(
                        out=VCX[:, :, kvh, 0:64], in_=psK[:, 0:128].rearrange("p (a b) -> p a b", a=2)))
        for kvh in range(2):
            P.op("dve", [ovl], [VCX], lambda kvh=kvh: nc.vector.tensor_copy(out=VCX[:, :, kvh, 65:129], in_=ovl[:, :, :]))
            P.op("dve", [], [VCX], lambda kvh=kvh: nc.vector.memset(VCX[:, :, kvh, 64:65], 1.0))
        P.barrier()
        cst.__exit__(None, None, None)

        wo = P.sbuf("wo", [128, 8, 1024], BF16, bst)
        wqn = P.sbuf("wqn", [128, 8, 536], BF16, bst)
        for kc in range(8):
            P.dma("pool", wqn[:, kc, 0:512], w_in3[:, kc, 1536:2048], writes=[wqn])
            P.dma("pool", wqn[:, kc, 512:536], w_in3[:, kc, 2816:2840], writes=[wqn])
        for kc in range(8):
            P.dma("pool", wo[:, kc, :], w_out3[:, kc, :], writes=[wo])
        qn_b = [P.sbuf("qn_b%d" % i, [128, 512], BF16, bst) for i in range(2)]
        QnT = [P.sbuf("QnT%d" % i, [128, 4, 128], BF16, bst) for i in range(2)]
        gate = [P.sbuf("gate%d" % i, [128, 24], F32, bst) for i in range(2)]
        mcmp = [P.sbuf("mcmp%d" % i, [128, 2, 128], BF16, bst) for i in range(2)]
        bonus = [P.sbuf("bonus%d" % i, [128, 64], F32, bst) for i in range(2)]
        onsa = P.sbuf("onsa", [128, 512], F32, bst)
        psel = P.sbuf("psel", [128, 2, 64], F32, bst)
        score = P.sbuf("score", [128, 64], F32, bst)
        swork = P.sbuf("swork", [128, 64], F32, bst)
        m8 = [P.sbuf("m8_%d" % i, [128, 8], F32, bst) for i in range(2)]
        bm_b = P.sbuf("bm_b", [128, 128], BF16, bst)
        bmT = P.sbuf("bmT", [128, 1, 128], BF16, bst)
        msb = [P.sbuf("msb%d" % i, [128, 128], BF16, bst) for i in range(2)]
        ms_i = [0]
        omix_b = P.sbuf("omix_b", [128, 1024], BF16, bst)
        omixT = P.sbuf("omixT", [128, 8, 128], BF16, bst)
        wcol = P.sbuf("wcol", [128, 8], F32, bst)
        psQn, psG = B[0], B[1]
        psS = [B[2], B[3]]
        psM, psOa, psOb = B[4], B[5], B[6]
        psY = [B[0], B[1]]
        first_branch = {}

        def b3_pre(j):
            s = j % 2
            load_x(x_own, j, s)
            P.dma("sp", mcmp[s][:], cdram["mask_cmp"][:, j, :, :], writes=[mcmp[s]])
            P.dma("sp", bonus[s][:], cdram["bonus"][:, j, :], writes=[bonus[s]])
            front(xt[s], hb[s], hT[s], s, "g_pre_mix")
            proj(psQn, 512, hT[s], wqn, 0)
            for kc in range(8):
                P.op("pe", [hT[s], wqn], [psG], lambda kc=kc: nc.tensor.matmul(
                    psG[:, 0:24], lhsT=hT[s][:, kc, :], rhs=wqn[:, kc, 512:536], start=(kc == 0), stop=(kc == 7)),
                    inc=(kc == 7))
            for kvh in range(2):
                rope("dve", psQn, psQn[:, kvh * 256:(kvh + 1) * 256].rearrange("p (h x) -> p h x", x=64), qn_b[s],
                     qn_b[s][:, :].rearrange("p (g k x) -> p k g x", g=4, k=2)[:, kvh, :, :],
                     rope_own[:, 0, j, :], rope_own[:, 1, j, :], 4, rtmp, rope_own)
            P.op("act", [psG], [gate[s]], lambda: nc.scalar.activation(out=gate[s][:], in_=psG[:, 0:24], func=AF.Exp, scale=-1.0))
            P.op("dve", [gate[s]], [gate[s]], lambda: nc.vector.tensor_scalar(
                out=gate[s][:], in0=gate[s][:], scalar1=1.0, scalar2=None, op0=ALU.add))
            P.op("dve", [gate[s]], [gate[s]], lambda: nc.vector.reciprocal(out=gate[s][:], in_=gate[s][:]))
            transpose_to(qn_b[s], QnT[s], nchunks=4, evac="dve")

        def branch_evac(po_ap_fn, po_t, s, kvh, g, bi, width):
            hn = kvh * 4 + g
            rl = col()
            P.op("dve", [po_t], [rl], lambda: nc.vector.tensor_scalar(
                out=rl[:], in0=po_ap_fn(64, 65), scalar1=1e-30, scalar2=None, op0=ALU.max))
            P.op("dve", [rl], [rl], lambda: nc.vector.reciprocal(out=rl[:], in_=rl[:]))
            wc = col()
            P.op("dve", [rl, gate[s]], [wc], lambda: nc.vector.tensor_tensor(
                out=wc[:], in0=rl[:], in1=gate[s][:, hn * 3 + bi:hn * 3 + bi + 1], op=ALU.mult))
            dst = onsa[:, hn * 64:(hn + 1) * 64]
            if bi == 0:
                P.op("dve", [po_t, wc], [onsa], lambda: nc.vector.tensor_scalar(
                    out=dst, in0=po_ap_fn(0, 64), scalar1=wc[:, 0:1], scalar2=None, op0=ALU.mult))
            else:
                P.op("dve", [po_t, wc, onsa], [onsa], lambda: nc.vector.scalar_tensor_tensor(
                    out=dst, in0=po_ap_fn(0, 64), scalar=wc[:, 0:1], in1=dst, op0=ALU.mult, op1=ALU.add))
            return rl

        def b3_tasks(j):
            s = j % 2
            par = j % 2
            nkt = 2 * j + 2
            tasks = []
            for kvh in range(2):
                lo = 64 * kvh
                pts = []
                for nt in range(2):
                    ps = psS[gi[0] % 2]
                    pt = PT[pt_i[0] % 6]
                    gi[0] += 1
                    pt_i[0] += 1
                    pts.append(pt)

                    def s1(nt=nt, ps=ps, pt=pt, lo=lo):
                        P.op("pe", [KCT, QnT[s]], [ps], lambda: nc.tensor.matmul(
                            ps[:, 0:512], lhsT=KCT[lo:lo + 64, nt * 128:(nt + 1) * 128],
                            rhs=QnT[s][lo:lo + 64, :, :].rearrange("p a b -> p (a b)"), start=True, stop=True))
                        P.op("act", [ps], [pt], lambda: nc.scalar.activation(
                            out=pt[:].rearrange("p a b -> p (a b)"), in_=ps[:, 0:512], func=AF.Exp, scale=0.125))
                        P.op("dve", [pt, mcmp[s]], [pt], lambda: nc.vector.tensor_tensor(
                            out=pt[:], in0=pt[:], in1=mcmp[s][:, nt, :].unsqueeze(1).to_broadcast([128, 4, 128]), op=ALU.mult))
                    tasks.append((s1, noop))

                def s2c(kvh=kvh, pts=pts, lo=lo):
                    for g in range(4):
                        po_t = psOa if g < 2 else psOb
                        c0 = (g % 2) * 129
                        for nt in range(2):
                            P.op("pe", [pts[nt], VCX], [po_t], lambda nt=nt, g=g, po_t=po_t, c0=c0: nc.tensor.matmul(
                                po_t[:, c0:c0 + 129], lhsT=pts[nt][:, g, :], rhs=VCX[:, nt, kvh, :],
                                start=(nt == 0), stop=(nt == 1)), inc=(nt == 1))
                    for g in range(4):
                        po_t = psOa if g < 2 else psOb
                        c0 = (g % 2) * 129
                        rl = branch_evac(lambda a, b, po_t=po_t, c0=c0: po_t[:, c0 + a:c0 + b], po_t, s, kvh, g, 0, 129)
                        if g == 0:
                            P.op("dve", [po_t, rl], [psel], lambda po_t=po_t, c0=c0, rl=rl: nc.vector.tensor_scalar(
                                out=psel[:, kvh, :], in0=po_t[:, c0 + 65:c0 + 129], scalar1=rl[:, 0:1], scalar2=None, op0=ALU.mult))
                        else:
                            P.op("dve", [po_t, rl, psel], [psel], lambda po_t=po_t, c0=c0, rl=rl: nc.vector.scalar_tensor_tensor(
                                out=psel[:, kvh, :], in0=po_t[:, c0 + 65:c0 + 129], scalar=rl[:, 0:1], in1=psel[:, kvh, :],
                                op0=ALU.mult, op1=ALU.add))
                    P.op("dve", [psel, bonus[s]], [score], lambda: nc.vector.tensor_tensor(
                        out=score[:], in0=psel[:, kvh, :], in1=bonus[s][:], op=ALU.add))
                    P.op("dve", [score], [m8[0]], lambda: nc.vector.max(out=m8[0][:], in_=score[:]))
                    P.op("dve", [score, m8[0]], [swork], lambda: nc.vector.match_replace(
                        out=swork[:], in_to_replace=m8[0][:], in_values=score[:], imm_value=-1e9))
                    P.op("dve", [swork], [m8[1]], lambda: nc.vector.max(out=m8[1][:], in_=swork[:]))
                    P.op("dve", [score, m8[1]], [bm_b], lambda: nc.vector.tensor_scalar(
                        out=bm_b[:, lo:lo + 64], in0=score[:], scalar1=m8[1][:, 7:8], scalar2=None, op0=ALU.is_ge))
                    if debug and kvh == 1:
                        P.dma("sp", dbg_nsa[0, j * 128:(j + 1) * 128, :], onsa[:], reads=[onsa], is_output=True)
                        P.dma("sp", dbg_psel[j * 128:(j + 1) * 128, :], psel[:].rearrange("p a b -> p (a b)"), reads=[psel], is_output=True)
                        P.dma("sp", dbg_bm[j * 128:(j + 1) * 128, :], bm_b[:], reads=[bm_b], is_output=True)
                tasks.append((noop, s2c))
                tasks.append(None)

            def s_bmT():
                transpose_to(bm_b, bmT, nchunks=1, evac="dve")
            tasks.append(None)
            tasks.append((s_bmT, noop))

            for bi, (kidx, Vt) in ((1, (0, SV)), (2, (1, WV))):
                for kvh in range(2):
                    lo = 64 * kvh
                    kts = list(range(nkt)) if bi == 1 else [kt for kt in range(2 * j - 4, 2 * j + 2) if kt >= 0]
                    po_t = psOa if kvh == 0 else psOb
                    for kt in kts:
                        ps = psS[gi[0] % 2]
                        pt = PT[pt_i[0] % 6]
                        gi[0] += 1
                        pt_i[0] += 1
                        rr = kt - 2 * j
                        ms = msb[ms_i[0] % 2]
                        if bi == 1:
                            ms_i[0] += 1

                        def s1(bi=bi, kidx=kidx, kt=kt, ps=ps, pt=pt, lo=lo, rr=rr, ms=ms):
                            P.op("pe", [ST, QnT[s]], [ps], lambda: nc.tensor.matmul(
                                ps[:, 0:512], lhsT=ST[lo:lo + 64, kidx, kt * 128:(kt + 1) * 128],
                                rhs=QnT[s][lo:lo + 64, :, :].rearrange("p a b -> p (a b)"), start=True, stop=True))
                            if bi == 1:
                                P.op("pe", [expand, bmT], [psM], lambda: nc.tensor.matmul(
                                    psM[:, 0:128], lhsT=expand[lo:lo + 64, kt, :], rhs=bmT[lo:lo + 64, 0, :],
                                    start=True, stop=True))
                                if rr >= 0:
                                    P.op("dve", [psM, mask_c], [ms], lambda: nc.vector.tensor_tensor(
                                        out=ms[:], in0=psM[:, 0:128], in1=mask_c[:, par, rr, :], op=ALU.mult))
                                else:
                                    P.op("dve", [psM], [ms], lambda: nc.vector.tensor_copy(out=ms[:], in_=psM[:, 0:128]))
                            P.op("act", [ps], [pt], lambda: nc.scalar.activation(
                                out=pt[:].rearrange("p a b -> p (a b)"), in_=ps[:, 0:512], func=AF.Exp, scale=0.125))
                            if bi == 1:
                                P.op("dve", [pt, ms], [pt], lambda: nc.vector.tensor_tensor(
                                    out=pt[:], in0=pt[:], in1=ms[:].unsqueeze(1).to_broadcast([128, 4, 128]), op=ALU.mult))
                            elif rr not in (-2, -1):
                                P.op("dve", [pt, mask_w], [pt], lambda: nc.vector.tensor_tensor(
                                    out=pt[:], in0=pt[:], in1=mask_w[:, par, rr + 4, :].unsqueeze(1).to_broadcast([128, 4, 128]),
                                    op=ALU.mult))

                        def s2(bi=bi, Vt=Vt, kt=kt, kts=kts, pt=pt, kvh=kvh, po_t=po_t):
                            for g in range(4):
                                P.op("pe", [pt, Vt], [po_t], lambda g=g: nc.tensor.matmul(
                                    po_t[:, g * 65:(g + 1) * 65], lhsT=pt[:, g, :], rhs=Vt[:, kt, kvh, :],
                                    start=(kt == kts[0] and g == 0), stop=(kt == kts[-1]), skip_group_check=True),
                                    inc=(g == 3 and kt == kts[-1]))
                            if kt == kts[-1]:
                                for g in range(4):
                                    branch_evac(lambda a, b, g=g: po_t[:, g * 65 + a:g * 65 + b], po_t, s, kvh, g, bi, 65)
                                if debug and kvh == 1:
                                    P.dma("sp", dbg_nsa[bi, j * 128:(j + 1) * 128, :], onsa[:], reads=[onsa], is_output=True)
                        tasks.append((s1, s2))

            def s_fin():
                P.op("act", [odiff], [omix_b], lambda: nc.scalar.copy(out=omix_b[:, 0:512], in_=odiff[:, j, :]))
                P.op("act", [onsa], [omix_b], lambda: nc.scalar.copy(out=omix_b[:, 512:1024], in_=onsa[:]))
                if debug:
                    P.dma("sp", dbg_omix[j * 128:(j + 1) * 128, :], omix_b[:], reads=[omix_b], is_output=True)
                out_proj_and_delta(omix_b, omixT, wo, psY, lambda half: D1[:, j, half * 512:(half + 1) * 512], s, d1_t=D1)
            tasks.append((noop, s_fin))
            return tasks

        if n_qtiles > 0:
            b3_pre(0)
        for j in range(n_qtiles):
            tasks = b3_tasks(j)
            pipeline(tasks)
            if j + 1 < n_qtiles:
                b3_pre(j + 1)
        bst.__exit__(None, None, None)
        P.barrier()
        pst.__exit__(None, None, None)

    P.barrier()
    g1st.__exit__(None, None, None)
    if do_mlp:
        mst = contextlib.ExitStack()
        mst.__enter__()
        load_gains(("g_pre_mlp", "g_post_mlp"), mst)
        wup = P.sbuf("wup", [128, 8, D_FF], BF16, mst)
        wdn = P.sbuf("wdn", [128, 32, D_MODEL], BF16, mst)
        for kc in range(8):
            for q4 in range(4):
                P.dma("pool", wup[:, kc, q4 * 1024:(q4 + 1) * 1024], w_up3[:, kc, q4 * 1024:(q4 + 1) * 1024], writes=[wup])
        for fc in range(32):
            P.dma("pool", wdn[:, fc, :], w_dn3[:, fc, :], writes=[wdn])
        xm = [P.sbuf("xm%d" % i, [128, D_MODEL], F32, mst) for i in range(2)]
        hbm = [P.sbuf("hbm%d" % i, [128, D_MODEL], BF16, mst) for i in range(2)]
        hmT = [P.sbuf("hmT%d" % i, [128, 8, 128], BF16, mst) for i in range(2)]
        rl_sb = [P.sbuf("rl_sb%d" % i, [128, 256], F32, mst) for i in range(2)]
        hid = [P.sbuf("hid%d" % i, [128, 256], BF16, mst) for i in range(3)]
        ysb = [P.sbuf("ysb%d" % i, [128, 512], F32, mst) for i in range(2)]
        psU = [B[0], B[1]]
        psYm = [[B[2], B[3]], [B[4], B[5]]]
        tiles = [("p", j) for j in range(n_qtiles if do_prompt else 0)] + ([("s", 0)] if do_sample else [])
        pairs = [tiles[i:i + 2] for i in range(0, len(tiles), 2)]
        xi = [0]
        for pair in pairs:
            npair = len(pair)
            xs_ = []
            for ti, (kind, j) in enumerate(pair):
                x_t = xm[xi[0] % 2]
                xi[0] += 1
                xs_.append(x_t)
                src = x_own[j * 128:(j + 1) * 128, :] if kind == "p" else x_smp[:, :]
                P.dma("sp", x_t[:], src, writes=[x_t])
                d_t = D1 if kind == "p" else D1s
                d_ap = D1[:, j, :] if kind == "p" else D1s[:, :]
                P.op("dve", [x_t, d_t], [x_t], lambda x_t=x_t, d_ap=d_ap: nc.vector.tensor_tensor(
                    out=x_t[:], in0=x_t[:], in1=d_ap, op=ALU.add))
                front(x_t, hbm[ti], hmT[ti], ti, "g_pre_mlp")
            ntok = 128 * npair
            hi = [0]
            def mlp_up(fc):
                pu = psU[fc % 2]
                for ti in range(npair):
                    for kc in range(8):
                        P.op("pe", [wup, hmT[ti]], [pu], lambda fc=fc, ti=ti, kc=kc, pu=pu: nc.tensor.matmul(
                            pu[:, ti * 128:(ti + 1) * 128], lhsT=wup[:, kc, fc * 128:(fc + 1) * 128], rhs=hmT[ti][:, kc, :],
                            start=(kc == 0), stop=(kc == 7)), inc=(kc == 7 and ti == npair - 1))
                rl_ = rl_sb[fc % 2]
                hd = hid[fc % 3]
                P.op("act", [pu], [rl_], lambda pu=pu, rl_=rl_: nc.scalar.activation(
                    out=rl_[:, 0:ntok], in_=pu[:, 0:ntok], func=AF.Relu))
                P.op("dve", [rl_], [hd], lambda rl_=rl_, hd=hd: nc.vector.tensor_tensor(
                    out=hd[:, 0:ntok], in0=rl_[:, 0:ntok], in1=rl_[:, 0:ntok], op=ALU.mult))

            def mlp_down(fc):
                hd = hid[fc % 3]
                for ti in range(npair):
                    for half in range(2):
                        py = psYm[ti][half]
                        P.op("pe", [hd, wdn], [py], lambda fc=fc, ti=ti, half=half, py=py, hd=hd: nc.tensor.matmul(
                            py[:, 0:512], lhsT=hd[:, ti * 128:(ti + 1) * 128], rhs=wdn[:, fc, half * 512:(half + 1) * 512],
                            start=(fc == 0), stop=(fc == 31)), inc=(fc == 31))

            mlp_up(0)
            for fc in range(32):
                if fc + 1 < 32:
                    mlp_up(fc + 1)
                mlp_down(fc)
            for ti, (kind, j) in enumerate(pair):
                x_t = xs_[ti]
                py = psYm[ti]
                ssa, ssb, ln, rs = col(), col(), col(), col()
                P.op("act", [py[0]], [junk, ssa], lambda py=py, ssa=ssa: nc.scalar.activation(
                    out=junk[:, 0:512], in_=py[0][:, 0:512], func=AF.Square, accum_out=ssa[:]))
                P.op("act", [py[1]], [junk, ssb], lambda py=py, ssb=ssb: nc.scalar.activation(
                    out=junk[:, 0:512], in_=py[1][:, 0:512], func=AF.Square, accum_out=ssb[:]))
                P.op("dve", [ssa, ssb], [ssa], lambda ssa=ssa, ssb=ssb: nc.vector.tensor_tensor(
                    out=ssa[:], in0=ssa[:], in1=ssb[:], op=ALU.add))
                P.op("act", [ssa, eps_t], [ln], lambda ssa=ssa, ln=ln: nc.scalar.activation(
                    out=ln[:], in_=ssa[:], func=AF.Ln, bias=eps_t[:], scale=1.0 / D_MODEL))
                P.op("act", [ln], [rs], lambda ln=ln, rs=rs: nc.scalar.activation(out=rs[:], in_=ln[:], func=AF.Exp, scale=-0.5))
                for half in range(2):
                    y_t = ysb[half]
                    P.op("dve", [py[half], rs, gb["g_post_mlp"]], [y_t], lambda half=half, py=py, rs=rs, y_t=y_t: nc.vector.scalar_tensor_tensor(
                        out=y_t[:], in0=py[half][:, 0:512], scalar=rs[:, 0:1],
                        in1=gb["g_post_mlp"][:, half * 512:(half + 1) * 512], op0=ALU.mult, op1=ALU.mult))
                    P.op("dve", [y_t, x_t], [x_t], lambda half=half, y_t=y_t, x_t=x_t: nc.vector.tensor_tensor(
                        out=x_t[:, half * 512:(half + 1) * 512], in0=y_t[:], in1=x_t[:, half * 512:(half + 1) * 512], op=ALU.add))
                dst = o_yp[j * 128:(j + 1) * 128, :] if kind == "p" else o_ys[:, :]
                P.dma("sp", dst, x_t[:], reads=[x_t], is_output=True)
        P.barrier()
        mst.__exit__(None, None, None)

    P.finish()


def make_in_maps(inp):
    f32 = lambda a: np.ascontiguousarray(np.asarray(a), dtype=np.float32)
    xp = f32(inp["x_prompt"])
    xs = f32(inp["x_sample"])
    shared = {
        "w_in": f32(inp["w_in"])[0], "w_out": f32(inp["w_out"])[0], "w_up": f32(inp["w_up"])[0],
        "w_down": f32(inp["w_down"])[0],
        "diff_subln": f32(inp["diff_subln"]), "cmp_pos": f32(inp["cmp_pos"])[0], "cmp_w1": f32(inp["cmp_w1"])[0],
        "cmp_w2": f32(inp["cmp_w2"])[0],
        "cache_d": f32(inp["cache_diff_kv"]).reshape(-1, 1024),
        "cache_n": f32(inp["cache_nsa_kv"]).reshape(-1, 512),
    }
    for n in ("g_pre_mix", "g_post_mix", "g_pre_mlp", "g_post_mlp", "lam_q1", "lam_k1", "lam_q2", "lam_k2"):
        shared[n] = f32(inp[n])
    pt = np.ascontiguousarray(np.asarray(inp["page_table"]), dtype=np.int32)
    win = f32(inp["state_nsa_win_kv"])[0].reshape(128, 512, 256)
    consts = [make_consts(0), make_consts(1)]
    maps = []
    for c in range(8):
        b, r = c // 2, c % 2
        G = own_tiles(r)
        m = dict(shared)
        m["x_all"] = xp[b]
        m["x_own"] = np.ascontiguousarray(xp[b].reshape(NT_ALL, 128, D_MODEL)[G].reshape(-1, D_MODEL))
        m["x_smp"] = np.ascontiguousarray(xs[NB_S * c:NB_S * (c + 1)].reshape(128, D_MODEL))
        m["ptab"] = np.ascontiguousarray(pt[NB_S * c:NB_S * (c + 1)].reshape(1, -1))
        m["win_st"] = np.ascontiguousarray(win[NB_S * c:NB_S * (c + 1)])
        for n, a in consts[r].items():
            m["c_" + n] = a
        maps.append(m)
    return maps


def assemble(results):
    yp = np.zeros((4, SEQ, D_MODEL), np.float32)
    ys = np.zeros((128, 8, D_MODEL), np.float32)
    dkv_p = np.zeros((1, 4, SEQ, 2, 4, 128), np.float32)
    nkv_p = np.zeros((1, 4, SEQ, 4, 2, 64), np.float32)
    wkv_p = np.zeros((1, 4, 512, 2, 2, 64), np.float32)
    dkv_s = np.zeros((1, 128, 8, 2, 4, 128), np.float32)
    nkv_s = np.zeros((1, 128, 8, 4, 2, 64), np.float32)
    wkv_s = np.zeros((1, 128, 512, 2, 2, 64), np.float32)
    for c in range(8):
        b, r = c // 2, c % 2
        res = results[c]
        G = own_tiles(r)
        ypv = yp[b].reshape(NT_ALL, 128, D_MODEL)
        ypv[G] = np.asarray(res["o_yp"]).reshape(NT_OWN, 128, D_MODEL)
        ys[NB_S * c:NB_S * (c + 1)] = np.asarray(res["o_ys"]).reshape(NB_S, 8, D_MODEL)
        half = slice(r * (SEQ // 2), (r + 1) * (SEQ // 2))
        dkv_p[0, b, half] = np.asarray(res["o_dkv_p"]).reshape(SEQ, 2, 4, 128)[half]
        nkv_p[0, b, half] = np.asarray(res["o_nkv_p"]).reshape(SEQ, 4, 2, 64)[half]
        if r == 1:
            wkv_p[0, b] = np.asarray(res["o_wkv_p"]).reshape(512, 2, 2, 64)
        dkv_s[0, NB_S * c:NB_S * (c + 1)] = np.asarray(res["o_dkv_s"]).reshape(NB_S, 8, 2, 4, 128)
        nkv_s[0, NB_S * c:NB_S * (c + 1)] = np.asarray(res["o_nkv_s"]).reshape(NB_S, 8, 4, 2, 64)
        wkv_s[0, NB_S * c:NB_S * (c + 1)] = np.asarray(res["o_wkv_s"]).reshape(NB_S, 512, 2, 2, 64)
    return (yp, ys, dkv_p, nkv_p, wkv_p, dkv_s, nkv_s, wkv_s)


_CACHE = {}


def kernel(**inputs):
    if "nc" not in _CACHE:
        _CACHE["nc"] = build_program()[0]
    nc = _CACHE["nc"]
    maps = make_in_maps(inputs)
    res = run_bass_kernel_spmd(nc, maps, core_ids=list(range(8)))
    return assemble(res.results)
```

```python
import contextlib
import math

import numpy as np
import ml_dtypes

import concourse.bass as bass
import concourse.mybir as mybir
from concourse.bass_utils import run_bass_kernel_spmd

F32 = mybir.dt.float32
BF16 = mybir.dt.bfloat16
I32 = mybir.dt.int32
ALU = mybir.AluOpType
AF = mybir.ActivationFunctionType

D_MODEL = 1024
SEQ = 4096
NT_ALL = SEQ // 128
NT_OWN = NT_ALL // 2
D_IN = 2840
D_FF = 4096
EPS = 1e-6
LAM_INIT = 0.8 - 0.6 * math.exp(-0.3 * 0)
N_CMP_P = 255
N_CMP_S = 127
NB_S = 16
PAST = 2048


class T:
    __slots__ = ("ap", "name", "w", "r")

    def __init__(self, ap, name):
        self.ap = ap
        self.name = name
        self.w = None
        self.r = {}

    def __getitem__(self, k):
        return self.ap[k]


class Prog:
    def __init__(self, nc, n_dma_sems=16):
        self.nc = nc
        self.es = contextlib.ExitStack()
        self.eng = {"pe": nc.tensor, "dve": nc.vector, "act": nc.scalar, "pool": nc.gpsimd, "sp": nc.sync}
        self.sem = {}
        for e in ("pe", "dve", "act", "pool"):
            self.sem[("e", e)] = self.es.enter_context(nc.semaphore("s_" + e))
        self.nd = n_dma_sems
        self.dq = {"sp": 0, "pool": 1}
        self.dcount = {"sp": 0, "pool": 0}
        for qi in range(2):
            for k in range(n_dma_sems):
                self.sem[("d", qi * n_dma_sems + k)] = self.es.enter_context(nc.semaphore("s_d%d_%d" % (qi, k)))
        self.cnt = {e: 0 for e in ("pe", "dve", "act", "pool")}
        self.dma_i = 0
        self.seen = {q: {} for q in self.eng}
        self.out_deps = {}
        self.n_inst = 0
        self.n_wait = 0
        self.pe_pending = False
        self.names = {}

    def sbuf(self, name, shape, dtype, stack=None):
        self.names[name] = self.names.get(name, 0) + 1
        if self.names[name] > 1:
            name = "%s__%d" % (name, self.names[name])
        t = (stack or self.es).enter_context(self.nc.sbuf_tensor(name, list(shape), dtype))
        return T(t[tuple(slice(None) for _ in shape)], name)

    def psum(self, name, shape, dtype):
        t = self.es.enter_context(self.nc.psum_tensor(name, list(shape), dtype))
        return t

    def _wait(self, q, key, val):
        if key == ("e", "pe") and q == "pe":
            return
        if self.seen[q].get(key, 0) >= val:
            return
        self.eng[q].wait_ge(self.sem[key], val)
        self.seen[q][key] = val
        self.n_wait += 1

    def _deps(self, q, reads, writes):
        for t in reads:
            if t.w is not None:
                self._wait(q, *t.w)
        for t in writes:
            if t.w is not None:
                self._wait(q, *t.w)
            for k, v in t.r.items():
                self._wait(q, k, v)

    def _mark(self, reads, writes, key, val):
        for t in reads:
            if t.r.get(key, 0) < val:
                t.r[key] = val
        for t in writes:
            t.w = (key, val)
            t.r = {}

    def op(self, e, reads, writes, fn, inc=True):
        self._deps(e, reads, writes)
        ins = fn()
        key = ("e", e)
        if inc:
            self.cnt[e] += 1
            ins.then_inc(self.sem[key], 1)
            val = self.cnt[e]
            if e == "pe":
                self.pe_pending = False
        else:
            assert e == "pe"
            val = self.cnt[e] + 1
            self.pe_pending = True
        self._mark(reads, writes, key, val)
        self.n_inst += 1
        return ins

    def dma(self, q, out, in_, reads=(), writes=(), is_output=False, indirect=None):
        qi = self.dq[q]
        ci = self.dcount[q]
        self.dcount[q] += 1
        k = qi * self.nd + ci % self.nd
        val = 16 * (ci // self.nd + 1)
        self.dma_i += 1
        key = ("d", k)
        if val > 16:
            self._wait(q, key, val - 16)
        self._deps(q, reads, writes)
        if indirect is not None:
            ins = self.eng[q].indirect_dma_start(out=out, out_offset=None, in_=in_, in_offset=indirect)
        else:
            ins = self.eng[q].dma_start(out=out, in_=in_)
        ins.then_inc(self.sem[key], 16)
        self._mark(reads, writes, key, val)
        if is_output:
            self.out_deps[key] = val
        self.n_inst += 1

    def barrier(self):
        assert not self.pe_pending
        for q in self.eng:
            for e in ("pe", "dve", "act", "pool"):
                if self.cnt[e] > 0:
                    self._wait(q, ("e", e), self.cnt[e])
            for qn, qi in self.dq.items():
                for k in range(self.nd):
                    n_used = (self.dcount[qn] - k + self.nd - 1) // self.nd
                    if n_used > 0:
                        self._wait(q, ("d", qi * self.nd + k), 16 * n_used)

    def finish(self):
        for key, val in self.out_deps.items():
            self._wait("sp", key, val)
        for e in ("pe", "dve", "act", "pool"):
            if self.cnt[e] > 0:
                self._wait("sp", ("e", e), self.cnt[e])


def _rope_tab(pos):
    half = 32
    inv = (10000.0 ** (-np.arange(half, dtype=np.float32) / half)).astype(np.float32)
    ang = pos.astype(np.float32)[:, None] * inv[None, :]
    return np.cos(ang).astype(np.float32), np.sin(ang).astype(np.float32)


def own_tiles(r):
    return [2 * j + (r if j % 2 == 0 else 1 - r) for j in range(NT_OWN)]


def _bf(a):
    return np.ascontiguousarray(a.astype(ml_dtypes.bfloat16))


def make_consts(r):
    c = {}
    c["ident_f"] = np.eye(128, dtype=np.float32)
    c["ident_b"] = _bf(np.eye(128, dtype=np.float32))
    cos, sin = _rope_tab(np.arange(SEQ))
    c["rope_all"] = np.ascontiguousarray(
        np.stack([cos.reshape(NT_ALL, 128, 32), sin.reshape(NT_ALL, 128, 32)], 0).transpose(2, 0, 1, 3))
    G = own_tiles(r)
    c["rope_own"] = np.ascontiguousarray(c["rope_all"][:, :, G, :])
    cs, ss = _rope_tab(PAST + (np.arange(128) % 8))
    c["rope_smp"] = np.ascontiguousarray(np.stack([cs, ss], 1))
    kk = np.arange(128)[:, None]
    qq = np.arange(128)[None, :]
    mc = np.zeros((128, 2, 2, 128), np.float32)
    mw = np.zeros((128, 2, 6, 128), np.float32)
    for par in range(2):
        delta = r if par == 0 else 1 - r
        for rr in range(2):
            mc[:, par, rr, :] = ((128 * rr + kk) <= (128 * delta + qq))
        for ri, rr in enumerate(range(-4, 2)):
            d = (128 * delta + qq) - (128 * rr + kk)
            mw[:, par, ri, :] = (d >= 0) & (d < 512)
    c["mask_c"] = _bf(mc)
    c["mask_w"] = _bf(mw)
    mcmp = np.zeros((128, NT_OWN, 2, 128), np.float32)
    bonus = np.zeros((128, NT_OWN, 64), np.float32)
    for j, g in enumerate(G):
        tpos = 128 * g + np.arange(128)
        for nt in range(2):
            n = 128 * nt + np.arange(128)
            mcmp[:, j, nt, :] = ((16 * n + 31)[:, None] <= tpos[None, :]) & (n < N_CMP_P)[:, None]
        cur = tpos // 64
        m = np.arange(64)[None, :]
        bonus[:, j, :] = 10.0 * (m == 0) + 20.0 * (m == cur[:, None]) + 30.0 * (m == cur[:, None] - 1)
    c["mask_cmp"] = _bf(mcmp)
    c["bonus"] = bonus
    n = np.arange(256)
    cs_ = n * 16
    ss_ = np.arange(64) * 64
    ov = ((cs_[:, None] < ss_[None, :] + 64) & (cs_[:, None] + 32 > ss_[None, :]) & (n < N_CMP_P)[:, None]).astype(np.float32)
    c["overlap"] = _bf(ov.reshape(2, 128, 64).transpose(1, 0, 2))
    E = np.zeros((64, NT_ALL, 128), np.float32)
    for kt in range(NT_ALL):
        for k in range(128):
            E[(128 * kt + k) // 64, kt, k] = 1.0
    c["expand"] = _bf(np.concatenate([E, E], 0))
    c["iota_p"] = np.arange(128, dtype=np.float32).reshape(128, 1)
    k = np.arange(128)
    mN = np.zeros((128, NB_S, 8), np.float32)
    for b in range(NB_S):
        mN[:, b, :] = ((k // 8) == b)[:, None] & ((k % 8)[:, None] <= np.arange(8)[None, :])
    c["maskN"] = _bf(mN)
    c["maskW0"] = _bf((k[:, None] > np.arange(8)[None, :]).astype(np.float32))
    Es = np.zeros((64, 17, 128), np.float32)
    for kt in range(16):
        for kk_ in range(128):
            Es[2 * kt + (kk_ // 64), kt, kk_] = 1.0
    Es[32, 16, :] = 1.0
    c["expand_s"] = _bf(Es)
    n_ = np.arange(128)[:, None]
    m_ = np.arange(64)[None, :]
    c["overlap_s"] = _bf(((n_ < N_CMP_S) & (16 * n_ < 64 * m_ + 64) & (16 * n_ + 32 > 64 * m_) & (m_ < 33)).astype(np.float32))
    bs = np.zeros((16, 64), np.float32)
    bs[:, 0] = 10.0
    bs[:, 31] = 30.0
    bs[:, 32] = 20.0
    bs[:, 33:] = -1e9
    c["bonus_s"] = bs
    c["mask127"] = (np.arange(128) < N_CMP_S).astype(np.float32).reshape(128, 1)
    c["ones127"] = _bf(np.repeat(c["mask127"], 128, axis=1))
    return c


CONST_DT = {"ident_f": F32, "ident_b": BF16, "rope_all": F32, "rope_own": F32, "rope_smp": F32,
            "mask_c": BF16, "mask_w": BF16, "mask_cmp": BF16, "bonus": F32, "overlap": BF16,
            "expand": BF16, "iota_p": F32, "maskN": BF16, "maskW0": BF16, "expand_s": BF16, "overlap_s": BF16,
            "bonus_s": F32, "mask127": F32, "ones127": BF16}


def build_program(do_prompt=True, do_sample=True, do_mlp=True, n_kpass=NT_ALL, n_qtiles=NT_OWN, n_phys=2560, debug=False, n_sbatch=NB_S, sstop=99):
    nc = bass.Bass("TRN2", target_bir_lowering=False)
    P = Prog(nc)
    es = P.es
    with es:
        _emit(nc, P, do_prompt, do_sample, do_mlp, n_kpass, n_qtiles, n_phys, debug, n_sbatch, sstop)
    return nc, P


def _emit(nc, P, do_prompt, do_sample, do_mlp, n_kpass, n_qtiles, n_phys, debug=False, n_sbatch=NB_S, sstop=99):
    es = P.es

    def din(name, shape, dt=F32):
        return nc.dram_tensor(name, list(shape), dt, kind="ExternalInput").ap()

    def dout(name, shape, dt=F32):
        return nc.dram_tensor(name, list(shape), dt, kind="ExternalOutput").ap()

    x_all = din("x_all", [SEQ, D_MODEL])
    x_own = din("x_own", [SEQ // 2, D_MODEL])
    x_smp = din("x_smp", [128, D_MODEL])
    w_in = din("w_in", [D_MODEL, D_IN])
    w_out = din("w_out", [D_MODEL, D_MODEL])
    w_up = din("w_up", [D_MODEL, D_FF])
    w_down = din("w_down", [D_FF, D_MODEL])
    gains = {n: din(n, [1, D_MODEL]) for n in ("g_pre_mix", "g_post_mix", "g_pre_mlp", "g_post_mlp")}
    lam_in = {n: din(n, [1, 64]) for n in ("lam_q1", "lam_k1", "lam_q2", "lam_k2")}
    subln = din("diff_subln", [1, 128])
    cmp_pos = din("cmp_pos", [2, 32, 64])
    cmp_w1 = din("cmp_w1", [2, 2048, 128])
    cmp_w2 = din("cmp_w2", [2, 128, 64])
    cache_d = din("cache_d", [n_phys * 128, 1024])
    cache_n = din("cache_n", [n_phys * 128, 512])
    win_st = din("win_st", [NB_S, 512, 256])
    ptab = din("ptab", [1, NB_S * 16], I32)
    cshape = {"ident_f": [128, 128], "ident_b": [128, 128], "rope_all": [128, 2, NT_ALL, 32],
              "rope_own": [128, 2, NT_OWN, 32], "rope_smp": [128, 2, 32], "mask_c": [128, 2, 2, 128],
              "mask_w": [128, 2, 6, 128], "mask_cmp": [128, NT_OWN, 2, 128], "bonus": [128, NT_OWN, 64],
              "overlap": [128, 2, 64], "expand": [128, NT_ALL, 128], "iota_p": [128, 1],
              "maskN": [128, NB_S, 8], "maskW0": [128, 8], "expand_s": [64, 17, 128], "overlap_s": [128, 64],
              "bonus_s": [16, 64], "mask127": [128, 1], "ones127": [128, 128]}
    cdram = {n: din("c_" + n, s, CONST_DT[n]) for n, s in cshape.items()}

    o_yp = dout("o_yp", [SEQ // 2, D_MODEL])
    o_ys = dout("o_ys", [128, D_MODEL])
    o_dkv_p = dout("o_dkv_p", [SEQ, 1024])
    o_nkv_p = dout("o_nkv_p", [SEQ, 512])
    o_wkv_p = dout("o_wkv_p", [512, 256])
    o_dkv_s = dout("o_dkv_s", [128, 1024])
    o_nkv_s = dout("o_nkv_s", [128, 512])
    o_wkv_s = dout("o_wkv_s", [NB_S, 512, 256])
    if debug:
        dbg_omix = dout("dbg_omix", [NT_OWN * 128, 1024], BF16)
        dbg_nsa = dout("dbg_nsa", [3, NT_OWN * 128, 512])
        dbg_psel = dout("dbg_psel", [NT_OWN * 128, 128])
        dbg_bm = dout("dbg_bm", [NT_OWN * 128, 128], BF16)

    psb = [P.psum("psb%d" % i, [128, 512], F32) for i in range(7)]
    psT_t = P.psum("psT", [128, 1024], BF16)

    C = {}
    for n in ("ident_f", "ident_b", "iota_p"):
        C[n] = P.sbuf("sb_" + n, cshape[n], CONST_DT[n])
        P.dma("sp", C[n][:], cdram[n], writes=[C[n]])
    eps_t = P.sbuf("eps_t", [128, 1], F32)
    P.op("dve", [], [eps_t], lambda: nc.vector.memset(eps_t[:], EPS))
    gb = {}

    def load_gains(names, stack):
        for n in names:
            gb[n] = P.sbuf("gb_" + n, [128, D_MODEL], F32, stack)
            P.dma("sp", gb[n][:], gains[n][0:1, :].to_broadcast([128, D_MODEL]), writes=[gb[n]])

    B = [T(psb[i][:, :], "bank%d" % i) for i in range(7)]

    junk = P.sbuf("junk", [128, D_MODEL], BF16)
    st_ss = [P.sbuf("st_ss%d" % i, [128, 1], F32) for i in range(2)]
    st_ln = [P.sbuf("st_ln%d" % i, [128, 1], F32) for i in range(2)]
    st_rs = [P.sbuf("st_rs%d" % i, [128, 1], F32) for i in range(2)]

    def rstd_of(src_t, src_ap, slot, n_feat=D_MODEL):
        ss, ln, rs = st_ss[slot], st_ln[slot], st_rs[slot]
        P.op("act", [src_t], [junk, ss],
             lambda: nc.scalar.activation(out=junk[:, 0:n_feat], in_=src_ap, func=AF.Square, accum_out=ss[:]))
        P.op("act", [ss, eps_t], [ln],
             lambda: nc.scalar.activation(out=ln[:], in_=ss[:], func=AF.Ln, bias=eps_t[:], scale=1.0 / n_feat))
        P.op("act", [ln], [rs],
             lambda: nc.scalar.activation(out=rs[:], in_=ln[:], func=AF.Exp, scale=-0.5))
        return rs

    psT = T(psT_t[:, :], "psT")

    def transpose_to(hb, hT, nchunks=8, evac="act"):
        for kc in range(nchunks):
            P.op("pe", [hb, C["ident_b"]], [psT],
                 lambda kc=kc: nc.tensor.transpose(out=psT[:, kc * 128:(kc + 1) * 128],
                                                   in_=hb[:, kc * 128:(kc + 1) * 128], identity=C["ident_b"][:]),
                 inc=(kc == nchunks - 1))
        if evac == "act":
            P.op("act", [psT], [hT], lambda: nc.scalar.copy(out=hT[:].rearrange("p a b -> p (a b)")[:, 0:nchunks * 128],
                                                             in_=psT[:, 0:nchunks * 128]))
        else:
            P.op("dve", [psT], [hT], lambda: nc.vector.tensor_copy(out=hT[:].rearrange("p a b -> p (a b)")[:, 0:nchunks * 128],
                                                                    in_=psT[:, 0:nchunks * 128]))

    def rope(eng, src_t, src3, dst_t, dst3, cos_ap, sin_ap, nh, tmp, tab_t):
        s4 = src3.rearrange("p h (two x) -> p h two x", two=2)
        d4 = dst3.rearrange("p h (two x) -> p h two x", two=2)
        cb = cos_ap.unsqueeze(1).to_broadcast([128, nh, 32])
        sb = sin_ap.unsqueeze(1).to_broadcast([128, nh, 32])
        x1, x2 = s4[:, :, 0, :], s4[:, :, 1, :]
        t1 = tmp[0][:, 0:nh * 32].rearrange("p (h x) -> p h x", x=32)
        t2 = tmp[1][:, 0:nh * 32].rearrange("p (h x) -> p h x", x=32)
        E = P.eng[eng]
        P.op(eng, [src_t, tab_t], [tmp[0]], lambda: E.tensor_tensor(out=t1, in0=x1, in1=cb, op=ALU.mult))
        P.op(eng, [src_t, tab_t], [tmp[1]], lambda: E.tensor_tensor(out=t2, in0=x2, in1=sb, op=ALU.mult))
        P.op(eng, [tmp[0], tmp[1]], [dst_t], lambda: E.tensor_tensor(out=d4[:, :, 0, :], in0=t1, in1=t2, op=ALU.subtract))
        P.op(eng, [src_t, tab_t], [tmp[0]], lambda: E.tensor_tensor(out=t1, in0=x1, in1=sb, op=ALU.mult))
        P.op(eng, [src_t, tab_t], [tmp[1]], lambda: E.tensor_tensor(out=t2, in0=x2, in1=cb, op=ALU.mult))
        P.op(eng, [tmp[0], tmp[1]], [dst_t], lambda: E.tensor_tensor(out=d4[:, :, 1, :], in0=t1, in1=t2, op=ALU.add))

    rtmp = [P.sbuf("rtmp%d" % i, [128, 256], F32) for i in range(2)]

    def load_w(name, dst, src3, eng="pool"):
        for kc in range(src3.shape[1]):
            P.dma(eng, dst[:, kc, :], src3[:, kc, :], writes=[dst])

    w_in3 = w_in.rearrange("(kc p) n -> p kc n", p=128)
    w_out3 = w_out.rearrange("(kc p) n -> p kc n", p=128)
    w_up3 = w_up.rearrange("(kc p) n -> p kc n", p=128)
    w_dn3 = w_down.rearrange("(kc p) n -> p kc n", p=128)

    def pipeline(tasks, depth=2):
        pend = []
        for t in tasks:
            if t is None:
                for p_ in pend:
                    p_[1]()
                pend = []
                continue
            t[0]()
            pend.append(t)
            if len(pend) > depth:
                pend.pop(0)[1]()
        for p_ in pend:
            p_[1]()

    noop = lambda: None

    lam_sb = {}
    for n in lam_in:
        lam_sb[n] = P.sbuf("sb_" + n, [128, 64], F32)
        P.dma("sp", lam_sb[n][:], lam_in[n][0:1, :].to_broadcast([128, 64]), writes=[lam_sb[n]])
    lam_s = [P.sbuf("lam_s%d" % i, [128, 1], F32) for i in range(2)]
    lam_e = [P.sbuf("lam_e%d" % i, [128, 1], F32) for i in range(2)]
    neg_lam = P.sbuf("neg_lam", [128, 1], F32)
    junk64 = P.sbuf("junk64", [128, 64], F32)
    for i, (a, b) in enumerate((("lam_q1", "lam_k1"), ("lam_q2", "lam_k2"))):
        P.op("dve", [lam_sb[a], lam_sb[b]], [junk64],
             lambda a=a, b=b: nc.vector.tensor_tensor(out=junk64[:], in0=lam_sb[a][:], in1=lam_sb[b][:], op=ALU.mult))
        P.op("act", [junk64], [junk64, lam_s[i]],
             lambda i=i: nc.scalar.activation(out=junk64[:], in_=junk64[:], func=AF.Copy, accum_out=lam_s[i][:]))
        P.op("act", [lam_s[i]], [lam_e[i]],
             lambda i=i: nc.scalar.activation(out=lam_e[i][:], in_=lam_s[i][:], func=AF.Exp))
    P.op("dve", [lam_e[0], lam_e[1]], [neg_lam],
         lambda: nc.vector.tensor_tensor(out=neg_lam[:], in0=lam_e[1][:], in1=lam_e[0][:], op=ALU.subtract))
    P.op("dve", [neg_lam], [neg_lam],
         lambda: nc.vector.tensor_scalar(out=neg_lam[:], in0=neg_lam[:], scalar1=-LAM_INIT, scalar2=None, op0=ALU.add))
    sg = P.sbuf("sg", [128, 128], F32)
    P.dma("sp", sg[:], subln[0:1, :].to_broadcast([128, 128]), writes=[sg])
    P.op("dve", [sg], [sg],
         lambda: nc.vector.tensor_scalar(out=sg[:], in0=sg[:], scalar1=1.0 - LAM_INIT, scalar2=None, op0=ALU.mult))

    col_ring = [P.sbuf("col%d" % i, [128, 1], F32) for i in range(12)]
    col_i = [0]

    def col():
        c_ = col_ring[col_i[0] % len(col_ring)]
        col_i[0] += 1
        return c_

    D1s = P.sbuf("D1s", [128, D_MODEL], BF16)
    P.op("pool", [], [D1s], lambda: nc.gpsimd.memset(D1s[:], 0.0))

    def front(xt_t, hb_t, hT_t, slot, gname):
        rs = rstd_of(xt_t, xt_t[:], slot)
        P.op("dve", [xt_t, rs, gb[gname]], [hb_t],
             lambda: nc.vector.scalar_tensor_tensor(out=hb_t[:], in0=xt_t[:], scalar=rs[:, 0:1],
                                                    in1=gb[gname][:], op0=ALU.mult, op1=ALU.mult))
        transpose_to(hb_t, hT_t)

    def proj(ps, n, hT_t, w_t, c0):
        for kc in range(8):
            P.op("pe", [hT_t, w_t], [ps],
                 lambda kc=kc: nc.tensor.matmul(ps[:, 0:n], lhsT=hT_t[:, kc, :], rhs=w_t[:, kc, c0:c0 + n],
                                                start=(kc == 0), stop=(kc == 7)),
                 inc=(kc == 7))

    def out_proj_and_delta(omix_b, omixT, wo, psY, d1_ap_fn, slot, skip_transpose=False, d1_t=None):
        if not skip_transpose:
            transpose_to(omix_b, omixT)
        for half in range(2):
            proj(psY[half], 512, omixT, wo, half * 512)
        ssa, ssb = col(), col()
        P.op("act", [psY[0]], [junk, ssa],
             lambda: nc.scalar.activation(out=junk[:, 0:512], in_=psY[0][:, 0:512], func=AF.Square, accum_out=ssa[:]))
        P.op("act", [psY[1]], [junk, ssb],
             lambda: nc.scalar.activation(out=junk[:, 0:512], in_=psY[1][:, 0:512], func=AF.Square, accum_out=ssb[:]))
        P.op("dve", [ssa, ssb], [ssa], lambda: nc.vector.tensor_tensor(out=ssa[:], in0=ssa[:], in1=ssb[:], op=ALU.add))
        ln, rs = st_ln[slot], st_rs[slot]
        P.op("act", [ssa, eps_t], [ln],
             lambda: nc.scalar.activation(out=ln[:], in_=ssa[:], func=AF.Ln, bias=eps_t[:], scale=1.0 / D_MODEL))
        P.op("act", [ln], [rs], lambda: nc.scalar.activation(out=rs[:], in_=ln[:], func=AF.Exp, scale=-0.5))
        for half in range(2):
            P.op("dve", [psY[half], rs, gb["g_post_mix"]], [d1_t],
                 lambda half=half: nc.vector.scalar_tensor_tensor(
                     out=d1_ap_fn(half), in0=psY[half][:, 0:512], scalar=rs[:, 0:1],
                     in1=gb["g_post_mix"][:, half * 512:(half + 1) * 512], op0=ALU.mult, op1=ALU.mult))

    if do_sample:
        sst = contextlib.ExitStack()
        sst.__enter__()
        load_gains(("g_pre_mix", "g_post_mix"), sst)
        sc = {}
        for n in ("rope_smp", "maskN", "maskW0", "expand_s", "overlap_s", "bonus_s", "ones127"):
            sc[n] = P.sbuf("sb_" + n, cshape[n], CONST_DT[n], sst)
            P.dma("sp", sc[n][:], cdram[n], writes=[sc[n]])
        ones_bf = P.sbuf("ones_bf", [128, 128], BF16, sst)
        P.op("dve", [], [ones_bf], lambda: nc.vector.memset(ones_bf[:], 1.0))
        ones_f = P.sbuf("ones_f", [128, 128], F32, sst)
        P.op("dve", [], [ones_f], lambda: nc.vector.memset(ones_f[:], 1.0))
        sgT = P.sbuf("sgT", [128, 1], F32, sst)
        P.dma("sp", sgT[:], subln.rearrange("o e -> e o"), writes=[sgT])
        P.op("dve", [sgT], [sgT], lambda: nc.vector.tensor_scalar(
            out=sgT[:], in0=sgT[:], scalar1=1.0 - LAM_INIT, scalar2=None, op0=ALU.mult))
        pti = P.sbuf("pti", [128, NB_S * 16], I32, sst)
        ptf = P.sbuf("ptf", [128, NB_S * 16], F32, sst)
        idx = P.sbuf("idx", [128, NB_S * 16], I32, sst)
        P.dma("sp", pti[:], ptab[0:1, :].to_broadcast([128, NB_S * 16]), writes=[pti])
        P.op("dve", [pti], [ptf], lambda: nc.vector.tensor_copy(out=ptf[:], in_=pti[:]))
        P.op("dve", [ptf, C["iota_p"]], [ptf], lambda: nc.vector.tensor_scalar(
            out=ptf[:], in0=ptf[:], scalar1=128.0, scalar2=C["iota_p"][:, 0:1], op0=ALU.mult, op1=ALU.add))
        P.op("dve", [ptf], [idx], lambda: nc.vector.tensor_copy(out=idx[:], in_=ptf[:]))
        wo_s = P.sbuf("wo_s", [128, 8, 1024], BF16, sst)
        for kc in range(4):
            P.dma("pool", wo_s[:, kc, :], w_out3[:, kc, :], writes=[wo_s])
        for g in range(4):
            for kvh in range(2):
                r0 = 512 + (kvh * 4 + g) * 64
                P.dma("pool", wo_s[64 * kvh:64 * kvh + 64, 4 + g, :], w_out[r0:r0 + 64, :], writes=[wo_s])
        W1 = P.sbuf("W1s", [128, 2, 32, 128], BF16, sst)
        W2d = P.sbuf("W2ds", [128, 2, 128], BF16, sst)
        for w in range(2):
            for dup in range(2):
                P.dma("pool", W1[64 * dup:64 * dup + 64, w, :, :],
                      cmp_w1[w].rearrange("(l d) f -> d l f", d=64), writes=[W1])
                P.dma("pool", W2d[:, w, 64 * dup:64 * dup + 64], cmp_w2[w], writes=[W2d])
        pos_sb = P.sbuf("pos_sbs", [32, 2, 64], F32, sst)
        for w in range(2):
            P.dma("sp", pos_sb[:, w, :], cmp_pos[w], writes=[pos_sb])
        posT = P.sbuf("posTs", [64, 2, 32], BF16, sst)
        cbias = P.sbuf("cbiass", [128, 2], F32, sst)
        for w in range(2):
            P.op("pe", [pos_sb, C["ident_f"]], [B[3]], lambda w=w: nc.tensor.transpose(
                out=B[3][0:64, w * 32:(w + 1) * 32], in_=pos_sb[0:32, w, :], identity=C["ident_f"][0:32, 0:32]))
        P.op("dve", [B[3]], [posT], lambda: nc.vector.tensor_copy(
            out=posT[:].rearrange("p a b -> p (a b)"), in_=B[3][0:64, 0:64]))
        for w in range(2):
            for l in range(32):
                P.op("pe", [W1, posT], [B[4]], lambda w=w, l=l: nc.tensor.matmul(
                    B[4][:, w:w + 1], lhsT=W1[0:64, w, l, :], rhs=posT[0:64, w, l:l + 1],
                    start=(l == 0), stop=(l == 31)), inc=(l == 31))
        P.op("dve", [B[4]], [cbias], lambda: nc.vector.tensor_copy(out=cbias[:], in_=B[4][:, 0:2]))

        QdTblk = P.sbuf("QdTblk", [128, 4, NB_S, 2, 8], BF16, sst)
        QnT_s = P.sbuf("QnT_s", [128, 4, 128], BF16, sst)
        QnTblk = P.sbuf("QnTblk", [128, NB_S, 2, 4, 8], BF16, sst)
        KdTn = P.sbuf("KdTn", [128, 4, 128], BF16, sst)
        NTn = P.sbuf("NTn", [128, 2, 128], BF16, sst)
        vall_s = P.sbuf("vall_s", [128, 768], BF16, sst)
        GB = P.sbuf("GB", [128, 3, 8, 128], F32, sst)
        omixT_s = P.sbuf("omixT_s", [128, 8, 128], BF16, sst)
        P.op("dve", [], [QdTblk], lambda: nc.vector.memset(QdTblk[:], 0.0))
        P.op("dve", [], [omixT_s], lambda: nc.vector.memset(omixT_s[:], 0.0))

        s0 = contextlib.ExitStack()
        s0.__enter__()
        wS = P.sbuf("wS", [128, 8, D_IN], BF16, s0)
        for kc in range(8):
            P.dma("pool", wS[:, kc, :], w_in3[:, kc, :], writes=[wS])
        xs_t = P.sbuf("xs_t", [128, D_MODEL], F32, s0)
        hbs = P.sbuf("hbs", [128, D_MODEL], BF16, s0)
        hTs = P.sbuf("hTs", [128, 8, 128], BF16, s0)
        qd_s = P.sbuf("qd_s", [128, 512], BF16, s0)
        qn_s = P.sbuf("qn_s", [128, 512], BF16, s0)
        QdT_s = P.sbuf("QdT_s", [128, 4, 128], BF16, s0)
        kvd_s = P.sbuf("kvd_s", [128, 1024], F32, s0)
        kvn_s = P.sbuf("kvn_s", [128, 768], F32, s0)
        gate_s = P.sbuf("gate_s", [128, 24], F32, s0)
        tmpG = P.sbuf("tmpG", [128, 8, 128], F32, s0)
        P.dma("sp", xs_t[:], x_smp[:, :], writes=[xs_t])
        P.dma("sp", o_wkv_s[:, 0:504, :], win_st[:, 8:512, :], is_output=True)
        front(xs_t, hbs, hTs, 0, "g_pre_mix")
        for (bk, c0, n) in ((0, 0, 512), (1, 512, 512), (2, 1024, 512), (3, 1536, 512), (4, 2048, 512), (5, 2560, 256), (6, 2816, 24)):
            proj(B[bk], n, hTs, wS, c0)
        cos, sin = sc["rope_smp"][:, 0, :], sc["rope_smp"][:, 1, :]
        v64 = lambda ap: ap.rearrange("p (h x) -> p h x", x=64)
        rope("dve", B[0], v64(B[0][:, 0:512]), qd_s, v64(qd_s[:, :]), cos, sin, 8, rtmp, sc["rope_smp"])
        rope("dve", B[1], v64(B[1][:, 0:512]), kvd_s, v64(kvd_s[:, 0:512]), cos, sin, 8, rtmp, sc["rope_smp"])
        P.op("act", [B[2]], [kvd_s], lambda: nc.scalar.copy(out=kvd_s[:, 512:1024], in_=B[2][:, 0:512]))
        for kvh in range(2):
            rope("dve", B[3], v64(B[3][:, kvh * 256:(kvh + 1) * 256]), qn_s,
                 qn_s[:, :].rearrange("p (g k x) -> p k g x", g=4, k=2)[:, kvh, :, :], cos, sin, 4, rtmp, sc["rope_smp"])
        for slot in (0, 2):
            rope("dve", B[4], v64(B[4][:, slot * 128:(slot + 1) * 128]), kvn_s, v64(kvn_s[:, slot * 128:(slot + 1) * 128]),
                 cos, sin, 2, rtmp, sc["rope_smp"])
        rope("dve", B[5], v64(B[5][:, 0:128]), kvn_s, v64(kvn_s[:, 512:640]), cos, sin, 2, rtmp, sc["rope_smp"])
        for slot in (1, 3):
            P.op("act", [B[4]], [kvn_s], lambda slot=slot: nc.scalar.copy(
                out=kvn_s[:, slot * 128:(slot + 1) * 128], in_=B[4][:, slot * 128:(slot + 1) * 128]))
        P.op("act", [B[5]], [kvn_s], lambda: nc.scalar.copy(out=kvn_s[:, 640:768], in_=B[5][:, 128:256]))
        P.op("act", [B[6]], [gate_s], lambda: nc.scalar.activation(out=gate_s[:], in_=B[6][:, 0:24], func=AF.Exp, scale=-1.0))
        P.op("dve", [gate_s], [gate_s], lambda: nc.vector.tensor_scalar(
            out=gate_s[:], in0=gate_s[:], scalar1=1.0, scalar2=None, op0=ALU.add))
        P.op("dve", [gate_s], [gate_s], lambda: nc.vector.reciprocal(out=gate_s[:], in_=gate_s[:]))
        P.dma("sp", o_dkv_s[:, :], kvd_s[:], reads=[kvd_s], is_output=True)
        P.dma("sp", o_nkv_s[:, :], kvn_s[:, 0:512], reads=[kvn_s], is_output=True)
        for b in range(NB_S):
            P.dma("sp", o_wkv_s[b, 504:512, :], kvn_s[b * 8:(b + 1) * 8, 512:768], reads=[kvn_s], is_output=True)
        transpose_to(qd_s, QdT_s, nchunks=4, evac="dve")
        for c in range(2):
            lo = 64 * c
            P.op("dve", [QdT_s], [QdTblk], lambda c=c, lo=lo: nc.vector.tensor_copy(
                out=QdTblk[lo:lo + 64, :, :, c, :], in_=QdT_s[lo:lo + 64, :, :].rearrange("p h (b q) -> p h b q", q=8)))
        transpose_to(qn_s, QnT_s, nchunks=4, evac="dve")
        P.op("dve", [], [QnTblk], lambda: nc.vector.memset(QnTblk[:], 0.0))
        for kvh in range(2):
            lo = 64 * kvh
            P.op("dve", [QnT_s], [QnTblk], lambda kvh=kvh, lo=lo: nc.vector.tensor_copy(
                out=QnTblk[lo:lo + 64, :, kvh, :, :].rearrange("p b g q -> p g b q"),
                in_=QnT_s[lo:lo + 64, :, :].rearrange("p g (b q) -> p g b q", q=8)))
        for h in range(4):
            P.op("pe", [kvd_s, C["ident_f"]], [B[0]], lambda h=h: nc.tensor.transpose(
                out=B[0][:, h * 128:(h + 1) * 128], in_=kvd_s[:, h * 128:(h + 1) * 128], identity=C["ident_f"][:]),
                inc=(h == 3))
        P.op("act", [B[0]], [KdTn], lambda: nc.scalar.copy(out=KdTn[:].rearrange("p a b -> p (a b)"), in_=B[0][:, 0:512]))
        for wi, slot in enumerate((2, 4)):
            P.op("pe", [kvn_s, C["ident_f"]], [B[1]], lambda wi=wi, slot=slot: nc.tensor.transpose(
                out=B[1][:, wi * 128:(wi + 1) * 128], in_=kvn_s[:, slot * 128:(slot + 1) * 128], identity=C["ident_f"][:]),
                inc=(wi == 1))
        P.op("act", [B[1]], [NTn], lambda: nc.scalar.copy(out=NTn[:].rearrange("p a b -> p (a b)"), in_=B[1][:, 0:256]))
        P.op("dve", [kvd_s], [vall_s], lambda: nc.vector.tensor_copy(out=vall_s[:, 0:512], in_=kvd_s[:, 512:1024]))
        P.op("dve", [kvn_s], [vall_s], lambda: nc.vector.tensor_copy(out=vall_s[:, 512:640], in_=kvn_s[:, 384:512]))
        P.op("dve", [kvn_s], [vall_s], lambda: nc.vector.tensor_copy(out=vall_s[:, 640:768], in_=kvn_s[:, 640:768]))
        g3 = gate_s[:, :].rearrange("p (h b) -> p h b", b=3)
        for bi in range(3):
            P.op("dve", [gate_s, C["ident_f"]], [tmpG], lambda bi=bi: nc.vector.tensor_tensor(
                out=tmpG[:], in0=g3[:, :, bi].unsqueeze(2).to_broadcast([128, 8, 128]),
                in1=C["ident_f"][:, :].unsqueeze(1).to_broadcast([128, 8, 128]), op=ALU.mult))
            for half in range(2):
                P.op("pe", [ones_f, tmpG], [B[2 + half]], lambda half=half: nc.tensor.matmul(
                    B[2 + half][:, 0:512], lhsT=ones_f[:, :],
                    rhs=tmpG[:, half * 4:(half + 1) * 4, :].rearrange("p a b -> p (a b)"), start=True, stop=True))
                P.op("act", [B[2 + half]], [GB], lambda bi=bi, half=half: nc.scalar.copy(
                    out=GB[:, bi, half * 4:(half + 1) * 4, :].rearrange("p a b -> p (a b)"), in_=B[2 + half][:, 0:512]))
        P.barrier()
        s0.__exit__(None, None, None)

        pgd = [P.sbuf("pgd%d" % i, [128, 16, 1024], BF16, sst) for i in range(2)]
        pgn = P.sbuf("pgn", [128, 16, 512], BF16, sst)
        wst = P.sbuf("wst", [128, 4, 256], BF16, sst)
        KdT_b = P.sbuf("KdT_b", [128, 4, PAST], BF16, sst)
        CS_b = P.sbuf("CS_b", [128, 3, PAST], BF16, sst)
        WkT_b = P.sbuf("WkT_b", [128, 512], BF16, sst)
        PTd = P.sbuf("PTd", [128, 17, 64], BF16, sst)
        PsT = P.sbuf("PsT", [128, 17, 64], BF16, sst)
        PcT = P.sbuf("PcT", [128, 64], BF16, sst)
        msk_sb = P.sbuf("msk_sb", [128, 17, 16], BF16, sst)
        KCT_b = P.sbuf("KCT_b", [128, 128], BF16, sst)
        VC_b = P.sbuf("VC_b", [128, 128], BF16, sst)
        hcb = P.sbuf("hcbs", [128, 128], BF16, sst)
        P.op("dve", [], [hcb], lambda: nc.vector.memset(hcb[:], 0.0))
        gtmp = [P.sbuf("gtmps%d" % i, [128, 128], F32, sst) for i in range(2)]
        rl_sb = P.sbuf("rl_sbs", [128, 64], F32, sst)
        w_sb = P.sbuf("w_sbs", [128, 64], F32, sst)
        t_sb = [P.sbuf("t_sbs%d" % i, [128, 32], F32, sst) for i in range(3)]
        od_s = P.sbuf("od_s", [128, 32], F32, sst)
        accT = P.sbuf("accT", [128, 32], F32, sst)
        tmpP = P.sbuf("tmpP", [64, 64], F32, sst)
        pselT = P.sbuf("pselT", [64, 16], F32, sst)
        score = P.sbuf("score_s", [16, 64], F32, sst)
        swork = P.sbuf("swork_s", [16, 64], F32, sst)
        m8 = [P.sbuf("m8s_%d" % i, [16, 8], F32, sst) for i in range(2)]
        bm_s = P.sbuf("bm_s", [16, 64], BF16, sst)
        bmT2 = P.sbuf("bmT2", [64, 16], BF16, sst)

        def gather_d(b):
            for s_ in range(16):
                col_ = b * 16 + s_
                P.dma("pool", pgd[b % 2][:, s_, :], cache_d[:, :], reads=[idx], writes=[pgd[b % 2]],
                      indirect=bass.IndirectOffsetOnAxis(ap=idx[:, col_:col_ + 1], axis=0))

        def gather_n(b):
            for s_ in range(16):
                col_ = b * 16 + s_
                P.dma("pool", pgn[:, s_, :], cache_n[:, :], reads=[idx], writes=[pgn],
                      indirect=bass.IndirectOffsetOnAxis(ap=idx[:, col_:col_ + 1], axis=0))

        def gather_w(b):
            P.dma("pool", wst[:], win_st[b].rearrange("(t p) c -> p t c", p=128), writes=[wst])

        def gelu_s(src_ps, n, bias_col, dst_bf, tmpa, tmpb):
            P.op("dve", [src_ps, cbias], [tmpa], lambda: nc.vector.tensor_scalar(
                out=tmpa[:, 0:n], in0=src_ps[:, 0:n], scalar1=bias_col, scalar2=None, op0=ALU.add))
            P.op("dve", [tmpa], [tmpb], lambda: nc.vector.tensor_tensor(
                out=tmpb[:, 0:n], in0=tmpa[:, 0:n], in1=tmpa[:, 0:n], op=ALU.mult))
            P.op("dve", [tmpb], [tmpb], lambda: nc.vector.tensor_scalar(
                out=tmpb[:, 0:n], in0=tmpb[:, 0:n], scalar1=0.044715, scalar2=1.0, op0=ALU.mult, op1=ALU.add))
            P.op("dve", [tmpb, tmpa], [tmpb], lambda: nc.vector.tensor_tensor(
                out=tmpb[:, 0:n], in0=tmpb[:, 0:n], in1=tmpa[:, 0:n], op=ALU.mult))
            P.op("act", [tmpb], [tmpb], lambda: nc.scalar.activation(
                out=tmpb[:, 0:n], in_=tmpb[:, 0:n], func=AF.Exp, scale=-1.5957691216057308))
            P.op("dve", [tmpb], [tmpb], lambda: nc.vector.tensor_scalar(
                out=tmpb[:, 0:n], in0=tmpb[:, 0:n], scalar1=1.0, scalar2=None, op0=ALU.add))
            P.op("dve", [tmpb], [tmpb], lambda: nc.vector.reciprocal(out=tmpb[:, 0:n], in_=tmpb[:, 0:n]))
            P.op("dve", [tmpb, tmpa], [dst_bf], lambda: nc.vector.tensor_tensor(
                out=dst_bf[:, 0:n], in0=tmpb[:, 0:n], in1=tmpa[:, 0:n], op=ALU.mult))

        evi = [0]
        psT2 = T(psb[6][:, :].bitcast(BF16), "psT2")
        tr_i = [0]

        def tr_bank():
            k = tr_i[0] % 2
            tr_i[0] += 1
            return (psT, [psT]) if k == 0 else (psT2, [psT2, B[6]])

        def evac_copy(src_ts, src_ap, dst_t, dst_ap):
            if evi[0] % 2 == 0:
                P.op("act", src_ts, [dst_t], lambda: nc.scalar.copy(out=dst_ap, in_=src_ap))
            else:
                P.op("dve", src_ts, [dst_t], lambda: nc.vector.tensor_copy(out=dst_ap, in_=src_ap))
            evi[0] += 1

        def nsa_branch_evac(Y, Z, b, bi, first):
            P.op("dve", [Z], [rl_sb], lambda: nc.vector.tensor_scalar(
                out=rl_sb[:], in0=Z[:, 0:64], scalar1=1e-30, scalar2=None, op0=ALU.max))
            P.op("dve", [rl_sb], [rl_sb], lambda: nc.vector.reciprocal(out=rl_sb[:], in_=rl_sb[:]))
            P.op("dve", [rl_sb, GB], [w_sb], lambda: nc.vector.tensor_tensor(
                out=w_sb[:].rearrange("p (h q) -> p h q", q=8), in0=rl_sb[:].rearrange("p (h q) -> p h q", q=8),
                in1=GB[:, bi, :, b * 8:(b + 1) * 8], op=ALU.mult))
            dst = accT if first else t_sb[2]
            for kvh in range(2):
                lo = 64 * kvh
                P.op("dve", [Y, w_sb], [dst], lambda kvh=kvh, lo=lo: nc.vector.tensor_tensor(
                    out=dst[lo:lo + 64, :], in0=Y[lo:lo + 64, kvh * 32:(kvh + 1) * 32],
                    in1=w_sb[lo:lo + 64, kvh * 32:(kvh + 1) * 32], op=ALU.mult))
            if not first:
                P.op("dve", [accT, t_sb[2]], [accT], lambda: nc.vector.tensor_tensor(
                    out=accT[:], in0=accT[:], in1=t_sb[2][:], op=ALU.add))

        if n_sbatch > 0:
            gather_d(0)
            gather_n(0)
            gather_w(0)
        for b in range(n_sbatch):
            pg = pgd[b % 2]
            if sstop <= 1:
                continue
            for sp_ in range(8):
                pt_, pts_ = tr_bank()
                for s2 in range(2):
                    for h in range(4):
                        i = s2 * 4 + h
                        P.op("pe", [pg, C["ident_b"]], pts_, lambda sp_=sp_, s2=s2, h=h, i=i, pt_=pt_: nc.tensor.transpose(
                            out=pt_[:, i * 128:(i + 1) * 128], in_=pg[:, sp_ * 2 + s2, h * 128:(h + 1) * 128],
                            identity=C["ident_b"][:]), inc=(i == 7))
                evac_copy(pts_, pt_[:, :].rearrange("p (s h t) -> p s h t", s=2, h=4), KdT_b,
                          KdT_b[:, :, sp_ * 256:(sp_ + 1) * 256].rearrange("p h (s t) -> p s h t", s=2))
            if b + 1 < n_sbatch:
                pass
            if sstop <= 2:
                continue
            for kt in range(17):
                bank = B[kt // 8]
                for h in range(4):
                    lhs = KdT_b[:, h, kt * 128:(kt + 1) * 128] if kt < 16 else KdTn[:, h, :]
                    c0 = (kt % 8) * 64 + h * 16
                    P.op("pe", [KdT_b, KdTn, QdTblk], [bank], lambda bank=bank, lhs=lhs, c0=c0, h=h: nc.tensor.matmul(
                        bank[:, c0:c0 + 16], lhsT=lhs, rhs=QdTblk[:, h, b, :, :].rearrange("p a b -> p (a b)"),
                        start=True, stop=True), inc=(h == 3 and (kt % 8 == 7 or kt == 16)))
            for bk, k0, nk in ((0, 0, 8), (1, 8, 8), (2, 16, 1)):
                P.op("act", [B[bk]], [PTd], lambda bk=bk, k0=k0, nk=nk: nc.scalar.activation(
                    out=PTd[:, k0:k0 + nk, :].rearrange("p a b -> p (a b)"), in_=B[bk][:, 0:nk * 64], func=AF.Exp, scale=0.125))
            P.op("dve", [PTd, sc["maskN"]], [PTd], lambda: nc.vector.tensor_tensor(
                out=PTd[:, 16, :].rearrange("p (a q) -> p a q", q=8), in0=PTd[:, 16, :].rearrange("p (a q) -> p a q", q=8),
                in1=sc["maskN"][:, b, :].unsqueeze(1).to_broadcast([128, 8, 8]), op=ALU.mult))
            for h in range(4):
                for kt in range(17):
                    v = pg[:, kt, 512 + h * 128:512 + (h + 1) * 128] if kt < 16 else vall_s[:, h * 128:(h + 1) * 128]
                    P.op("pe", [pg, vall_s, PTd], [B[3]], lambda v=v, kt=kt, h=h: nc.tensor.matmul(
                        B[3][:, h * 16:(h + 1) * 16], lhsT=v, rhs=PTd[:, kt, h * 16:(h + 1) * 16],
                        start=(kt == 0), stop=(kt == 16)), inc=(kt == 16))
            for kt in range(17):
                P.op("pe", [ones_bf, PTd], [B[4]], lambda kt=kt: nc.tensor.matmul(
                    B[4][:, 0:64], lhsT=ones_bf[:, :], rhs=PTd[:, kt, :], start=(kt == 0), stop=(kt == 16)), inc=(kt == 16))
            if b + 1 < n_sbatch:
                gather_d(b + 1)
            P.op("dve", [B[4]], [rl_sb], lambda: nc.vector.reciprocal(out=rl_sb[:], in_=B[4][:, 0:64]))
            Yv = B[3][:, 0:64].rearrange("p (h c q) -> p h c q", h=4, c=2)
            rv = rl_sb[:, :].rearrange("p (h c q) -> p h c q", h=4, c=2)
            t1 = t_sb[0][:, :].rearrange("p (h q) -> p h q", q=8)
            t2 = t_sb[1][:, :].rearrange("p (h q) -> p h q", q=8)
            P.op("dve", [B[3], rl_sb], [t_sb[0]], lambda: nc.vector.tensor_tensor(out=t1, in0=Yv[:, :, 0, :], in1=rv[:, :, 0, :], op=ALU.mult))
            P.op("dve", [B[3], rl_sb], [t_sb[1]], lambda: nc.vector.tensor_tensor(out=t2, in0=Yv[:, :, 1, :], in1=rv[:, :, 1, :], op=ALU.mult))
            P.op("dve", [t_sb[0], t_sb[1], neg_lam], [od_s], lambda: nc.vector.scalar_tensor_tensor(
                out=od_s[:], in0=t_sb[1][:], scalar=neg_lam[:, 0:1], in1=t_sb[0][:], op0=ALU.mult, op1=ALU.add))
            P.op("dve", [od_s], [t_sb[0]], lambda: nc.vector.tensor_tensor(out=t_sb[0][:], in0=od_s[:], in1=od_s[:], op=ALU.mult))
            P.op("pe", [ones_f, t_sb[0]], [B[5]], lambda: nc.tensor.matmul(
                B[5][:, 0:32], lhsT=ones_f[:, :], rhs=t_sb[0][:, :], start=True, stop=True))
            P.op("act", [B[5], eps_t], [t_sb[1]], lambda: nc.scalar.activation(
                out=t_sb[1][:], in_=B[5][:, 0:32], func=AF.Ln, bias=eps_t[:], scale=1.0 / 128))
            P.op("act", [t_sb[1]], [t_sb[1]], lambda: nc.scalar.activation(out=t_sb[1][:], in_=t_sb[1][:], func=AF.Exp, scale=-0.5))
            P.op("dve", [od_s, sgT, t_sb[1]], [omixT_s], lambda: nc.vector.scalar_tensor_tensor(
                out=omixT_s[:, 0:4, b * 8:(b + 1) * 8], in0=od_s[:, :].rearrange("p (h q) -> p h q", q=8),
                scalar=sgT[:, 0:1], in1=t_sb[1][:, :].rearrange("p (h q) -> p h q", q=8), op0=ALU.mult, op1=ALU.mult))

            if sstop <= 3:
                continue
            for sp_ in range(8):
                pt_, pts_ = tr_bank()
                for s2 in range(2):
                    for w in range(3):
                        i = s2 * 3 + w
                        P.op("pe", [pgn, C["ident_b"]], pts_, lambda sp_=sp_, s2=s2, w=w, i=i, pt_=pt_: nc.tensor.transpose(
                            out=pt_[:, i * 128:(i + 1) * 128], in_=pgn[:, sp_ * 2 + s2, w * 128:(w + 1) * 128],
                            identity=C["ident_b"][:]), inc=(i == 5))
                evac_copy(pts_, pt_[:, 0:768].rearrange("p (s w t) -> p s w t", s=2, w=3), CS_b,
                          CS_b[:, :, sp_ * 256:(sp_ + 1) * 256].rearrange("p w (s t) -> p s w t", s=2))
            pt_, pts_ = tr_bank()
            for wt in range(4):
                P.op("pe", [wst, C["ident_b"]], pts_, lambda wt=wt, pt_=pt_: nc.tensor.transpose(
                    out=pt_[:, wt * 128:(wt + 1) * 128], in_=wst[:, wt, 0:128], identity=C["ident_b"][:]), inc=(wt == 3))
            evac_copy(pts_, pt_[:, 0:512], WkT_b, WkT_b[:, :])
            if sstop <= 4:
                continue
            for w in range(2):
                for kvh in range(2):
                    lo = 64 * kvh
                    for l in range(32):
                        P.op("pe", [W1, CS_b], [B[5]], lambda w=w, l=l, lo=lo: nc.tensor.matmul(
                            B[5][:, 0:N_CMP_S], lhsT=W1[lo:lo + 64, w, l, :],
                            rhs=CS_b[lo:lo + 64, w, l:l + 16 * (N_CMP_S - 1) + 1:16],
                            start=(l == 0), stop=(l == 31)), inc=(l == 31))
                    gelu_s(B[5], N_CMP_S, cbias[:, w:w + 1], hcb, gtmp[0], gtmp[1])
                    if w == 0:
                        P.op("pe", [W2d, hcb], [B[6]], lambda: nc.tensor.matmul(
                            B[6][:, 0:128], lhsT=W2d[:, 0, :], rhs=hcb[:, 0:128], start=True, stop=True))
                        P.op("dve", [B[6]], [KCT_b], lambda lo=lo: nc.vector.tensor_copy(
                            out=KCT_b[lo:lo + 64, :], in_=B[6][lo:lo + 64, 0:128]))
                    else:
                        P.op("pe", [W2d, hcb], [B[6]], lambda: nc.tensor.matmul(
                            B[6][:, 0:64], lhsT=hcb[:, 0:128], rhs=W2d[:, 1, 0:64], start=True, stop=True))
                        P.op("dve", [B[6]], [VC_b], lambda lo=lo: nc.vector.tensor_copy(out=VC_b[:, lo:lo + 64], in_=B[6][:, 0:64]))
            if sstop <= 5:
                continue
            P.op("pe", [KCT_b, QnTblk], [B[5]], lambda: nc.tensor.matmul(
                B[5][:, 0:64], lhsT=KCT_b[:, :], rhs=QnTblk[:, b, :, :, :].rearrange("p k g q -> p (k g q)"),
                start=True, stop=True))
            if sstop <= 5.2:
                continue
            P.op("act", [B[5]], [PcT], lambda: nc.scalar.activation(out=PcT[:], in_=B[5][:, 0:64], func=AF.Exp, scale=0.125))
            if sstop <= 5.4:
                continue
            P.op("pe", [VC_b, PcT], [B[3]], lambda: nc.tensor.matmul(B[3][:, 0:64], lhsT=VC_b[:, :], rhs=PcT[:, :], start=True, stop=True))
            P.op("pe", [sc["ones127"], PcT], [B[4]], lambda: nc.tensor.matmul(
                B[4][:, 0:64], lhsT=sc["ones127"][:, :], rhs=PcT[:, :], start=True, stop=True))
            P.op("pe", [sc["overlap_s"], PcT], [B[6]], lambda: nc.tensor.matmul(
                B[6][0:64, 0:64], lhsT=sc["overlap_s"][:, :], rhs=PcT[:, :], start=True, stop=True))
            if sstop <= 5.6:
                continue
            nsa_branch_evac(B[3], B[4], b, 0, True)
            if sstop <= 6:
                continue
            P.op("dve", [B[6], rl_sb], [tmpP], lambda: nc.vector.tensor_tensor(
                out=tmpP[:], in0=B[6][0:64, 0:64], in1=rl_sb[0:64, :], op=ALU.mult))
            tp = tmpP[:, :].rearrange("p (k g q) -> p k g q", k=2, g=4)
            pv = pselT[:, :].rearrange("p (k q) -> p k q", k=2)
            P.op("dve", [tmpP], [pselT], lambda: nc.vector.tensor_tensor(out=pv, in0=tp[:, :, 0, :], in1=tp[:, :, 1, :], op=ALU.add))
            P.op("dve", [tmpP, pselT], [pselT], lambda: nc.vector.tensor_tensor(out=pv, in0=pv, in1=tp[:, :, 2, :], op=ALU.add))
            P.op("dve", [tmpP, pselT], [pselT], lambda: nc.vector.tensor_tensor(out=pv, in0=pv, in1=tp[:, :, 3, :], op=ALU.add))
            P.op("pe", [pselT, C["ident_f"]], [B[6]], lambda: nc.tensor.transpose(
                out=B[6][0:16, 64:128], in_=pselT[0:64, :], identity=C["ident_f"][0:64, 0:64]))
            P.op("dve", [B[6], sc["bonus_s"]], [score], lambda: nc.vector.tensor_tensor(
                out=score[:], in0=B[6][0:16, 64:128], in1=sc["bonus_s"][:], op=ALU.add))
            P.op("dve", [score], [m8[0]], lambda: nc.vector.max(out=m8[0][:], in_=score[:]))
            P.op("dve", [score, m8[0]], [swork], lambda: nc.vector.match_replace(
                out=swork[:], in_to_replace=m8[0][:], in_values=score[:], imm_value=-2e9))
            P.op("dve", [swork], [m8[1]], lambda: nc.vector.max(out=m8[1][:], in_=swork[:]))
            P.op("dve", [score, m8[1]], [bm_s], lambda: nc.vector.tensor_scalar(
                out=bm_s[:], in0=score[:], scalar1=m8[1][:, 7:8], scalar2=None, op0=ALU.is_ge))
            P.op("pe", [bm_s, C["ident_b"]], [psT], lambda: nc.tensor.transpose(
                out=psT[0:64, 0:16], in_=bm_s[0:16, :], identity=C["ident_b"][0:16, 0:16]))
            P.op("dve", [psT], [bmT2], lambda: nc.vector.tensor_copy(out=bmT2[:], in_=psT[0:64, 0:16]))
            if sstop <= 7:
                continue
            for kt in range(17):
                bank = B[kt // 8]
                lhs = CS_b[:, 2, kt * 128:(kt + 1) * 128] if kt < 16 else NTn[:, 0, :]
                c0 = (kt % 8) * 64
                P.op("pe", [CS_b, NTn, QnTblk], [bank], lambda bank=bank, lhs=lhs, c0=c0: nc.tensor.matmul(
                    bank[:, c0:c0 + 64], lhsT=lhs, rhs=QnTblk[:, b, :, :, :].rearrange("p k g q -> p (k g q)"),
                    start=True, stop=True), inc=(kt % 8 == 7 or kt == 16))
            for kt in range(17):
                P.op("pe", [sc["expand_s"], bmT2], [B[5]], lambda kt=kt: nc.tensor.matmul(
                    B[5][:, kt * 16:(kt + 1) * 16], lhsT=sc["expand_s"][:, kt, :], rhs=bmT2[:, :], start=True, stop=True),
                    inc=(kt == 16))
            for bk, k0, nk in ((0, 0, 8), (1, 8, 8), (2, 16, 1)):
                P.op("act", [B[bk]], [PsT], lambda bk=bk, k0=k0, nk=nk: nc.scalar.activation(
                    out=PsT[:, k0:k0 + nk, :].rearrange("p a b -> p (a b)"), in_=B[bk][:, 0:nk * 64], func=AF.Exp, scale=0.125))
            P.op("dve", [B[5]], [msk_sb], lambda: nc.vector.tensor_copy(
                out=msk_sb[:].rearrange("p a b -> p (a b)"), in_=B[5][:, 0:272]))
            P.op("dve", [msk_sb, sc["maskN"]], [msk_sb], lambda: nc.vector.tensor_tensor(
                out=msk_sb[:, 16, :].rearrange("p (k q) -> p k q", k=2), in0=msk_sb[:, 16, :].rearrange("p (k q) -> p k q", k=2),
                in1=sc["maskN"][:, b, :].unsqueeze(1).to_broadcast([128, 2, 8]), op=ALU.mult))
            P.op("dve", [PsT, msk_sb], [PsT], lambda: nc.vector.tensor_tensor(
                out=PsT[:].rearrange("p t (k g q) -> p (t k) g q", k=2, g=4),
                in0=PsT[:].rearrange("p t (k g q) -> p (t k) g q", k=2, g=4),
                in1=msk_sb[:].rearrange("p t (k q) -> p (t k) q", k=2).unsqueeze(2).to_broadcast([128, 34, 4, 8]), op=ALU.mult))
            for kt in range(17):
                v = pgn[:, kt, 384:512] if kt < 16 else vall_s[:, 512:640]
                P.op("pe", [pgn, vall_s, PsT], [B[3]], lambda v=v, kt=kt: nc.tensor.matmul(
                    B[3][:, 0:64], lhsT=v, rhs=PsT[:, kt, :], start=(kt == 0), stop=(kt == 16)), inc=(kt == 16))
            for kt in range(17):
                P.op("pe", [ones_bf, PsT], [B[4]], lambda kt=kt: nc.tensor.matmul(
                    B[4][:, 0:64], lhsT=ones_bf[:, :], rhs=PsT[:, kt, :], start=(kt == 0), stop=(kt == 16)), inc=(kt == 16))
            if b + 1 < n_sbatch:
                gather_n(b + 1)
            nsa_branch_evac(B[3], B[4], b, 1, False)
            if sstop <= 8:
                continue
            for kt in range(5):
                lhs = WkT_b[:, kt * 128:(kt + 1) * 128] if kt < 4 else NTn[:, 1, :]
                c0 = kt * 64
                P.op("pe", [WkT_b, NTn, QnTblk], [B[0]], lambda lhs=lhs, c0=c0: nc.tensor.matmul(
                    B[0][:, c0:c0 + 64], lhsT=lhs, rhs=QnTblk[:, b, :, :, :].rearrange("p k g q -> p (k g q)"),
                    start=True, stop=True), inc=(kt == 4))
            P.op("act", [B[0]], [PsT], lambda: nc.scalar.activation(
                out=PsT[:, 0:5, :].rearrange("p a b -> p (a b)"), in_=B[0][:, 0:320], func=AF.Exp, scale=0.125))
            P.op("dve", [PsT, sc["maskW0"]], [PsT], lambda: nc.vector.tensor_tensor(
                out=PsT[:, 0, :].rearrange("p (a q) -> p a q", q=8), in0=PsT[:, 0, :].rearrange("p (a q) -> p a q", q=8),
                in1=sc["maskW0"][:, :].unsqueeze(1).to_broadcast([128, 8, 8]), op=ALU.mult))
            P.op("dve", [PsT, sc["maskN"]], [PsT], lambda: nc.vector.tensor_tensor(
                out=PsT[:, 4, :].rearrange("p (a q) -> p a q", q=8), in0=PsT[:, 4, :].rearrange("p (a q) -> p a q", q=8),
                in1=sc["maskN"][:, b, :].unsqueeze(1).to_broadcast([128, 8, 8]), op=ALU.mult))
            for kt in range(5):
                v = wst[:, kt, 128:256] if kt < 4 else vall_s[:, 640:768]
                P.op("pe", [wst, vall_s, PsT], [B[3]], lambda v=v, kt=kt: nc.tensor.matmul(
                    B[3][:, 0:64], lhsT=v, rhs=PsT[:, kt, :], start=(kt == 0), stop=(kt == 4)), inc=(kt == 4))
            for kt in range(5):
                P.op("pe", [ones_bf, PsT], [B[4]], lambda kt=kt: nc.tensor.matmul(
                    B[4][:, 0:64], lhsT=ones_bf[:, :], rhs=PsT[:, kt, :], start=(kt == 0), stop=(kt == 4)), inc=(kt == 4))
            if b + 1 < n_sbatch:
                gather_w(b + 1)
            nsa_branch_evac(B[3], B[4], b, 2, False)
            P.op("act", [accT], [omixT_s], lambda: nc.scalar.copy(
                out=omixT_s[:, 4:8, b * 8:(b + 1) * 8], in_=accT[:, :].rearrange("p (g q) -> p g q", q=8)))

        if n_sbatch < NB_S:
            pass
        out_proj_and_delta(None, omixT_s, wo_s, [B[0], B[1]], lambda half: D1s[:, half * 512:(half + 1) * 512], 0,
                           skip_transpose=True, d1_t=D1s)
        P.barrier()
        sst.__exit__(None, None, None)

    D1 = P.sbuf("D1", [128, NT_OWN, D_MODEL], BF16)
    P.op("pool", [], [D1], lambda: nc.gpsimd.memset(D1[:], 0.0))
    g1st = contextlib.ExitStack()
    g1st.__enter__()
    load_gains(("g_pre_mix", "g_post_mix"), g1st)

    if do_prompt:
        pst = contextlib.ExitStack()
        pst.__enter__()
        odiff = P.sbuf("odiff", [128, NT_OWN, 512], BF16, pst)
        P.op("pool", [], [odiff], lambda: nc.gpsimd.memset(odiff[:], 0.0))
        rope_own = P.sbuf("sb_rope_own", cshape["rope_own"], F32, pst)
        P.dma("sp", rope_own[:], cdram["rope_own"], writes=[rope_own])
        mask_c = P.sbuf("sb_mask_c", cshape["mask_c"], BF16, pst)
        P.dma("sp", mask_c[:], cdram["mask_c"], writes=[mask_c])
        xt = [P.sbuf("xt%d" % i, [128, D_MODEL], F32, pst) for i in range(2)]
        hb = [P.sbuf("hb%d" % i, [128, D_MODEL], BF16, pst) for i in range(2)]
        hT = [P.sbuf("hT%d" % i, [128, 8, 128], BF16, pst) for i in range(2)]
        PT = [P.sbuf("PT%d" % i, [128, 4, 128], BF16, pst) for i in range(6)]
        pt_i = [0]

        def load_x(src, g, slot):
            P.dma("sp", xt[slot][:], src[g * 128:(g + 1) * 128, :], writes=[xt[slot]])

        ast = contextlib.ExitStack()
        ast.__enter__()
        KdT = P.sbuf("KdT", [128, 4, SEQ], BF16, ast)
        Vd = P.sbuf("Vd", [128, NT_ALL, 4, 129], BF16, ast)
        P.op("pool", [], [Vd], lambda: nc.gpsimd.memset(Vd[:], 1.0))
        rope_all = P.sbuf("sb_rope_all", cshape["rope_all"], F32, ast)
        P.dma("sp", rope_all[:], cdram["rope_all"], writes=[rope_all])
        wA = P.sbuf("wA", [128, 8, 1536], BF16, ast)
        for kc in range(8):
            P.dma("pool", wA[:, kc, 0:1024], w_in3[:, kc, 512:1536], writes=[wA])
        for kc in range(8):
            P.dma("pool", wA[:, kc, 1024:1536], w_in3[:, kc, 0:512], writes=[wA])
        kvd = [P.sbuf("kvd%d" % i, [128, 1024], F32, ast) for i in range(2)]
        psA, psB, psKT = B[0], B[1], B[2]

        psA2 = [B[0], B[3]]
        psB2 = [B[1], B[4]]

        def a1_post(g):
            s = g % 2
            pA, pB = psA2[s], psB2[s]
            cos = rope_all[:, 0, g, :]
            sin = rope_all[:, 1, g, :]
            rope("dve", pA, pA[:, 0:512].rearrange("p (h x) -> p h x", x=64), kvd[s],
                 kvd[s][:, 0:512].rearrange("p (h x) -> p h x", x=64), cos, sin, 8, rtmp, rope_all)
            P.op("act", [pB], [kvd[s]], lambda: nc.scalar.copy(out=kvd[s][:, 512:1024], in_=pB[:, 0:512]))
            P.dma("sp", o_dkv_p[g * 128:(g + 1) * 128, :], kvd[s][:], reads=[kvd[s]], is_output=True)
            for h in range(4):
                P.op("pe", [kvd[s], C["ident_f"]], [psKT],
                     lambda h=h: nc.tensor.transpose(out=psKT[:, h * 128:(h + 1) * 128],
                                                     in_=kvd[s][:, h * 128:(h + 1) * 128], identity=C["ident_f"][:]),
                     inc=(h == 3))
            P.op("act", [psKT], [KdT], lambda: nc.scalar.copy(
                out=KdT[:, :, g * 128:(g + 1) * 128], in_=psKT[:, :].rearrange("p (h t) -> p h t", h=4)))
            P.op("dve", [kvd[s]], [Vd], lambda: nc.vector.tensor_copy(
                out=Vd[:, g, :, 0:128], in_=kvd[s][:, 512:1024].rearrange("p (h e) -> p h e", h=4)))

        if n_kpass > 0:
            load_x(x_all, 0, 0)
            front(xt[0], hb[0], hT[0], 0, "g_pre_mix")
        for g in range(n_kpass):
            s = g % 2
            if g + 1 < n_kpass:
                load_x(x_all, g + 1, (g + 1) % 2)
            proj(psA2[s], 512, hT[s], wA, 0)
            proj(psB2[s], 512, hT[s], wA, 512)
            if g + 1 < n_kpass:
                front(xt[(g + 1) % 2], hb[(g + 1) % 2], hT[(g + 1) % 2], (g + 1) % 2, "g_pre_mix")
            a1_post(g)

        qd_b = [P.sbuf("qd_b%d" % i, [128, 512], BF16, ast) for i in range(2)]
        QdT = [P.sbuf("QdT%d" % i, [128, 4, 128], BF16, ast) for i in range(2)]
        o1_sb = P.sbuf("o1_sb", [128, 128], F32, ast)
        od_sb = P.sbuf("od_sb", [128, 128], F32, ast)
        psQ = B[0]
        psS = [B[1], B[2]]
        psO = [B[3], B[4]]
        gi = [0]
        oi = [0]

        def a2_pre(j):
            s = j % 2
            load_x(x_own, j, s)
            front(xt[s], hb[s], hT[s], s, "g_pre_mix")
            proj(psQ, 512, hT[s], wA, 1024)
            rope("dve", psQ, psQ[:, 0:512].rearrange("p (h x) -> p h x", x=64), qd_b[s],
                 qd_b[s][:, :].rearrange("p (h x) -> p h x", x=64), rope_own[:, 0, j, :], rope_own[:, 1, j, :], 8, rtmp, rope_own)
            transpose_to(qd_b[s], QdT[s], nchunks=4, evac="dve")

        def a2_tasks(j):
            s = j % 2
            par = j % 2
            nkt = 2 * j + 2
            tasks = []
            for h in range(4):
                acc = {}
                for c in range(2):
                    po = psO[oi[0] % 2]
                    oi[0] += 1
                    acc[c] = po
                    groups = [list(range(a, min(a + 4, nkt))) for a in range(0, nkt, 4)]
                    for gidx, grp in enumerate(groups):
                        ps = psS[gi[0] % 2]
                        pt = PT[pt_i[0] % 6]
                        gi[0] += 1
                        pt_i[0] += 1

                        def s1(h=h, c=c, grp=grp, ps=ps, pt=pt):
                            for i, kt in enumerate(grp):
                                P.op("pe", [KdT, QdT[s]], [ps],
                                     lambda i=i, kt=kt: nc.tensor.matmul(
                                         ps[:, i * 128:(i + 1) * 128],
                                         lhsT=KdT[64 * c:64 * c + 64, h, kt * 128:(kt + 1) * 128],
                                         rhs=QdT[s][64 * c:64 * c + 64, h, :], start=True, stop=True),
                                     inc=(i == len(grp) - 1))
                            n = len(grp) * 128
                            P.op("act", [ps], [pt], lambda: nc.scalar.activation(
                                out=pt[:].rearrange("p a b -> p (a b)")[:, 0:n], in_=ps[:, 0:n], func=AF.Exp, scale=0.125))
                            if grp[-1] == nkt - 1:
                                i0 = len(grp) - 2
                                P.op("dve", [pt, mask_c], [pt], lambda: nc.vector.tensor_tensor(
                                    out=pt[:, i0:i0 + 2, :], in0=pt[:, i0:i0 + 2, :], in1=mask_c[:, par, :, :], op=ALU.mult))

                        def s2(h=h, c=c, grp=grp, pt=pt, po=po, acc=acc):
                            for i, kt in enumerate(grp):
                                P.op("pe", [pt, Vd], [po],
                                     lambda i=i, kt=kt: nc.tensor.matmul(
                                         po[:, 0:129], lhsT=pt[:, i, :], rhs=Vd[:, kt, h, :],
                                         start=(kt == 0), stop=(kt == nkt - 1)),
                                     inc=(kt == nkt - 1))
                            if grp[-1] != nkt - 1:
                                return
                            rl = col()
                            P.op("dve", [po], [rl], lambda: nc.vector.reciprocal(out=rl[:], in_=po[:, 128:129]))
                            if c == 0:
                                P.op("dve", [po, rl], [o1_sb], lambda: nc.vector.tensor_scalar(
                                    out=o1_sb[:], in0=po[:, 0:128], scalar1=rl[:, 0:1], scalar2=None, op0=ALU.mult))
                                return
                            P.op("dve", [rl, neg_lam], [rl], lambda: nc.vector.tensor_tensor(
                                out=rl[:], in0=rl[:], in1=neg_lam[:], op=ALU.mult))
                            P.op("dve", [po, rl, o1_sb], [od_sb], lambda: nc.vector.scalar_tensor_tensor(
                                out=od_sb[:], in0=po[:, 0:128], scalar=rl[:, 0:1], in1=o1_sb[:], op0=ALU.mult, op1=ALU.add))
                            ss, ln, rs = col(), col(), col()
                            P.op("act", [od_sb], [junk, ss], lambda: nc.scalar.activation(
                                out=junk[:, 0:128], in_=od_sb[:], func=AF.Square, accum_out=ss[:]))
                            P.op("act", [ss, eps_t], [ln], lambda: nc.scalar.activation(
                                out=ln[:], in_=ss[:], func=AF.Ln, bias=eps_t[:], scale=1.0 / 128))
                            P.op("act", [ln], [rs], lambda: nc.scalar.activation(out=rs[:], in_=ln[:], func=AF.Exp, scale=-0.5))
                            P.op("dve", [od_sb, rs, sg], [odiff], lambda: nc.vector.scalar_tensor_tensor(
                                out=odiff[:, j, h * 128:(h + 1) * 128], in0=od_sb[:], scalar=rs[:, 0:1], in1=sg[:],
                                op0=ALU.mult, op1=ALU.mult))

                        tasks.append((s1, s2))
            return tasks

        if n_qtiles > 0:
            a2_pre(0)
        for j in range(n_qtiles):
            tasks = a2_tasks(j)
            if j + 1 < n_qtiles:
                tasks.insert(len(tasks) // 2, (lambda j=j: a2_pre(j + 1), noop))
            pipeline(tasks)
        P.barrier()
        ast.__exit__(None, None, None)

        bst = contextlib.ExitStack()
        bst.__enter__()
        ST = P.sbuf("ST", [128, 2, SEQ], BF16, bst)
        SV = P.sbuf("SV", [128, NT_ALL, 2, 65], BF16, bst)
        WV = P.sbuf("WV", [128, NT_ALL, 2, 65], BF16, bst)
        P.op("pool", [], [SV], lambda: nc.gpsimd.memset(SV[:], 1.0))
        P.op("pool", [], [WV], lambda: nc.gpsimd.memset(WV[:], 1.0))
        KCT = P.sbuf("KCT", [128, 256], BF16, bst)
        VCX = P.sbuf("VCX", [128, 2, 2, 129], BF16, bst)
        P.op("pool", [], [VCX], lambda: nc.gpsimd.memset(VCX[:], 0.0))
        expand = P.sbuf("sb_expand", cshape["expand"], BF16, bst)
        P.dma("sp", expand[:], cdram["expand"], writes=[expand])
        mask_w = P.sbuf("sb_mask_w", cshape["mask_w"], BF16, bst)
        P.dma("sp", mask_w[:], cdram["mask_w"], writes=[mask_w])

        cst = contextlib.ExitStack()
        cst.__enter__()
        CT = P.sbuf("CT", [128, 2, SEQ], BF16, cst)
        rope_all = P.sbuf("sb_rope_all2", cshape["rope_all"], F32, cst)
        P.dma("sp", rope_all[:], cdram["rope_all"], writes=[rope_all])
        wkn = P.sbuf("wkn", [128, 8, 768], BF16, cst)
        for kc in range(8):
            P.dma("pool", wkn[:, kc, :], w_in3[:, kc, 2048:2816], writes=[wkn])
        W1 = P.sbuf("W1", [128, 2, 32, 128], BF16, cst)
        W2d = P.sbuf("W2d", [128, 2, 128], BF16, cst)
        for w in range(2):
            for dup in range(2):
                P.dma("pool", W1[64 * dup:64 * dup + 64, w, :, :],
                      cmp_w1[w].rearrange("(l d) f -> d l f", d=64), writes=[W1])
                P.dma("pool", W2d[:, w, 64 * dup:64 * dup + 64], cmp_w2[w], writes=[W2d])
        pos_sb = P.sbuf("pos_sb", [32, 2, 64], F32, cst)
        for w in range(2):
            P.dma("sp", pos_sb[:, w, :], cmp_pos[w], writes=[pos_sb])
        ovl = P.sbuf("sb_overlap", cshape["overlap"], BF16, cst)
        P.dma("sp", ovl[:], cdram["overlap"], writes=[ovl])
        kvn = [P.sbuf("kvn%d" % i, [128, 768], F32, cst) for i in range(2)]
        psC, psD, psNT = B[0], B[1], B[2]

        psC2 = [B[0], B[3]]
        psD2 = [B[1], B[4]]

        def b1_post(g):
            s = g % 2
            psC, psD = psC2[s], psD2[s]
            cos = rope_all[:, 0, g, :]
            sin = rope_all[:, 1, g, :]
            for slot in (0, 2):
                rope("dve", psC, psC[:, slot * 128:(slot + 1) * 128].rearrange("p (h x) -> p h x", x=64), kvn[s],
                     kvn[s][:, slot * 128:(slot + 1) * 128].rearrange("p (h x) -> p h x", x=64), cos, sin, 2, rtmp, rope_all)
            rope("dve", psD, psD[:, 0:128].rearrange("p (h x) -> p h x", x=64), kvn[s],
                 kvn[s][:, 512:640].rearrange("p (h x) -> p h x", x=64), cos, sin, 2, rtmp, rope_all)
            for slot in (1, 3):
                P.op("act", [psC], [kvn[s]], lambda slot=slot: nc.scalar.copy(
                    out=kvn[s][:, slot * 128:(slot + 1) * 128], in_=psC[:, slot * 128:(slot + 1) * 128]))
            P.op("act", [psD], [kvn[s]], lambda: nc.scalar.copy(out=kvn[s][:, 640:768], in_=psD[:, 128:256]))
            P.dma("sp", o_nkv_p[g * 128:(g + 1) * 128, :], kvn[s][:, 0:512], reads=[kvn[s]], is_output=True)
            if g >= NT_ALL - 4:
                gg = g - (NT_ALL - 4)
                P.dma("sp", o_wkv_p[gg * 128:(gg + 1) * 128, :], kvn[s][:, 512:768], reads=[kvn[s]], is_output=True)
            for wi, slot in enumerate((0, 1, 2, 4)):
                P.op("pe", [kvn[s], C["ident_f"]], [psNT],
                     lambda wi=wi, slot=slot: nc.tensor.transpose(
                         out=psNT[:, wi * 128:(wi + 1) * 128], in_=kvn[s][:, slot * 128:(slot + 1) * 128],
                         identity=C["ident_f"][:]),
                     inc=(wi == 3))
            P.op("act", [psNT], [CT], lambda: nc.scalar.copy(
                out=CT[:, :, g * 128:(g + 1) * 128], in_=psNT[:, 0:256].rearrange("p (h t) -> p h t", h=2)))
            P.op("act", [psNT], [ST], lambda: nc.scalar.copy(
                out=ST[:, :, g * 128:(g + 1) * 128], in_=psNT[:, 256:512].rearrange("p (h t) -> p h t", h=2)))
            P.op("dve", [kvn[s]], [SV], lambda: nc.vector.tensor_copy(
                out=SV[:, g, :, 0:64], in_=kvn[s][:, 384:512].rearrange("p (h e) -> p h e", h=2)))
            P.op("dve", [kvn[s]], [WV], lambda: nc.vector.tensor_copy(
                out=WV[:, g, :, 0:64], in_=kvn[s][:, 640:768].rearrange("p (h e) -> p h e", h=2)))

        if n_kpass > 0:
            load_x(x_all, 0, 0)
            front(xt[0], hb[0], hT[0], 0, "g_pre_mix")
        for g in range(n_kpass):
            s = g % 2
            if g + 1 < n_kpass:
                load_x(x_all, g + 1, (g + 1) % 2)
            proj(psC2[s], 512, hT[s], wkn, 0)
            proj(psD2[s], 256, hT[s], wkn, 512)
            if g + 1 < n_kpass:
                front(xt[(g + 1) % 2], hb[(g + 1) % 2], hT[(g + 1) % 2], (g + 1) % 2, "g_pre_mix")
            b1_post(g)

        posT = P.sbuf("posT", [64, 2, 32], BF16, cst)
        cbias = P.sbuf("cbias", [128, 2], F32, cst)
        psX = B[3]
        for w in range(2):
            P.op("pe", [pos_sb, C["ident_f"]], [psX], lambda w=w: nc.tensor.transpose(
                out=psX[0:64, w * 32:(w + 1) * 32], in_=pos_sb[0:32, w, :], identity=C["ident_f"][0:32, 0:32]))
        P.op("dve", [psX], [posT], lambda: nc.vector.tensor_copy(
            out=posT[:].rearrange("p a b -> p (a b)"), in_=psX[0:64, 0:64]))
        psX2 = B[4]
        for w in range(2):
            for l in range(32):
                P.op("pe", [W1, posT], [psX2], lambda w=w, l=l: nc.tensor.matmul(
                    psX2[:, w:w + 1], lhsT=W1[0:64, w, l, :], rhs=posT[0:64, w, l:l + 1],
                    start=(l == 0), stop=(l == 31)), inc=(l == 31))
        P.op("dve", [psX2], [cbias], lambda: nc.vector.tensor_copy(out=cbias[:], in_=psX2[:, 0:2]))

        def gelu_to(src_ps, n, bias_col, dst_bf, tmpa, tmpb):
            P.op("dve", [src_ps, cbias], [tmpa], lambda: nc.vector.tensor_scalar(
                out=tmpa[:, 0:n], in0=src_ps[:, 0:n], scalar1=bias_col, scalar2=None, op0=ALU.add))
            P.op("dve", [tmpa], [tmpb], lambda: nc.vector.tensor_tensor(
                out=tmpb[:, 0:n], in0=tmpa[:, 0:n], in1=tmpa[:, 0:n], op=ALU.mult))
            P.op("dve", [tmpb], [tmpb], lambda: nc.vector.tensor_scalar(
                out=tmpb[:, 0:n], in0=tmpb[:, 0:n], scalar1=0.044715, scalar2=1.0, op0=ALU.mult, op1=ALU.add))
            P.op("dve", [tmpb, tmpa], [tmpb], lambda: nc.vector.tensor_tensor(
                out=tmpb[:, 0:n], in0=tmpb[:, 0:n], in1=tmpa[:, 0:n], op=ALU.mult))
            P.op("act", [tmpb], [tmpb], lambda: nc.scalar.activation(
                out=tmpb[:, 0:n], in_=tmpb[:, 0:n], func=AF.Exp, scale=-1.5957691216057308))
            P.op("dve", [tmpb], [tmpb], lambda: nc.vector.tensor_scalar(
                out=tmpb[:, 0:n], in0=tmpb[:, 0:n], scalar1=1.0, scalar2=None, op0=ALU.add))
            P.op("dve", [tmpb], [tmpb], lambda: nc.vector.reciprocal(out=tmpb[:, 0:n], in_=tmpb[:, 0:n]))
            P.op("dve", [tmpb, tmpa], [dst_bf], lambda: nc.vector.tensor_tensor(
                out=dst_bf[:, 0:n], in0=tmpb[:, 0:n], in1=tmpa[:, 0:n], op=ALU.mult))

        gtmp = [P.sbuf("gtmp%d" % i, [128, 256], F32, cst) for i in range(2)]
        hcb = P.sbuf("hcb", [128, 256], BF16, cst)
        P.op("dve", [], [hcb], lambda: nc.vector.memset(hcb[:], 0.0))
        psH, psK = B[5], B[6]
        for w in range(2):
            for kvh in range(2):
                lo = 64 * kvh
                for l in range(32):
                    P.op("pe", [W1, CT], [psH], lambda w=w, l=l, lo=lo: nc.tensor.matmul(
                        psH[:, 0:N_CMP_P], lhsT=W1[lo:lo + 64, w, l, :],
                        rhs=CT[lo:lo + 64, w, l:l + 16 * (N_CMP_P - 1) + 1:16],
                        start=(l == 0), stop=(l == 31)), inc=(l == 31))
                gelu_to(psH, N_CMP_P, cbias[:, w:w + 1], hcb, gtmp[0], gtmp[1])
                if w == 0:
                    P.op("pe", [W2d, hcb], [psK], lambda: nc.tensor.matmul(
                        psK[:, 0:256], lhsT=W2d[:, 0, :], rhs=hcb[:, 0:256], start=True, stop=True))
                    P.op("dve", [psK], [KCT], lambda lo=lo: nc.vector.tensor_copy(
                        out=KCT[lo:lo + 64, :], in_=psK[lo:lo + 64, 0:256]))
                else:
                    for nt in range(2):
                        P.op("pe", [W2d, hcb], [psK], lambda nt=nt: nc.tensor.matmul(
                            psK[:, nt * 64:(nt + 1) * 64], lhsT=hcb[:, nt * 128:(nt + 1) * 128], rhs=W2d[:, 1, 0:64],
                            start=True, stop=True))
                    P.op("dve", [psK], [VCX], lambda kvh=kvh: nc.vector.tensor_copy(
                        out=VCX[:, :, kvh, 0:64], in_=psK[:, 0:128].rearrange("p (a b) -> p a b", a=2)))
        for kvh in range(2):
            P.op("dve", [ovl], [VCX], lambda kvh=kvh: nc.vector.tensor_copy(out=VCX[:, :, kvh, 65:129], in_=ovl[:, :, :]))
            P.op("dve", [], [VCX], lambda kvh=kvh: nc.vector.memset(VCX[:, :, kvh, 64:65], 1.0))
        P.barrier()
        cst.__exit__(None, None, None)

        wo = P.sbuf("wo", [128, 8, 1024], BF16, bst)
        wqn = P.sbuf("wqn", [128, 8, 536], BF16, bst)
        for kc in range(8):
            P.dma("pool", wqn[:, kc, 0:512], w_in3[:, kc, 1536:2048], writes=[wqn])
            P.dma("pool", wqn[:, kc, 512:536], w_in3[:, kc, 2816:2840], writes=[wqn])
        for kc in range(8):
            P.dma("pool", wo[:, kc, :], w_out3[:, kc, :], writes=[wo])
        qn_b = [P.sbuf("qn_b%d" % i, [128, 512], BF16, bst) for i in range(2)]
        QnT = [P.sbuf("QnT%d" % i, [128, 4, 128], BF16, bst) for i in range(2)]
        gate = [P.sbuf("gate%d" % i, [128, 24], F32, bst) for i in range(2)]
        mcmp = [P.sbuf("mcmp%d" % i, [128, 2, 128], BF16, bst) for i in range(2)]
        bonus = [P.sbuf("bonus%d" % i, [128, 64], F32, bst) for i in range(2)]
        onsa = P.sbuf("onsa", [128, 512], F32, bst)
        psel = P.sbuf("psel", [128, 2, 64], F32, bst)
        score = P.sbuf("score", [128, 64], F32, bst)
        swork = P.sbuf("swork", [128, 64], F32, bst)
        m8 = [P.sbuf("m8_%d" % i, [128, 8], F32, bst) for i in range(2)]
        bm_b = P.sbuf("bm_b", [128, 128], BF16, bst)
        bmT = P.sbuf("bmT", [128, 1, 128], BF16, bst)
        msb = [P.sbuf("msb%d" % i, [128, 128], BF16, bst) for i in range(2)]
        ms_i = [0]
        omix_b = P.sbuf("omix_b", [128, 1024], BF16, bst)
        omixT = P.sbuf("omixT", [128, 8, 128], BF16, bst)
        wcol = P.sbuf("wcol", [128, 8], F32, bst)
        psQn, psG = B[0], B[1]
        psS = [B[2], B[3]]
        psM, psOa, psOb = B[4], B[5], B[6]
        psY = [B[0], B[1]]
        first_branch = {}

        def b3_pre(j):
            s = j % 2
            load_x(x_own, j, s)
            P.dma("sp", mcmp[s][:], cdram["mask_cmp"][:, j, :, :], writes=[mcmp[s]])
            P.dma("sp", bonus[s][:], cdram["bonus"][:, j, :], writes=[bonus[s]])
            front(xt[s], hb[s], hT[s], s, "g_pre_mix")
            proj(psQn, 512, hT[s], wqn, 0)
            for kc in range(8):
                P.op("pe", [hT[s], wqn], [psG], lambda kc=kc: nc.tensor.matmul(
                    psG[:, 0:24], lhsT=hT[s][:, kc, :], rhs=wqn[:, kc, 512:536], start=(kc == 0), stop=(kc == 7)),
                    inc=(kc == 7))
            for kvh in range(2):
                rope("dve", psQn, psQn[:, kvh * 256:(kvh + 1) * 256].rearrange("p (h x) -> p h x", x=64), qn_b[s],
                     qn_b[s][:, :].rearrange("p (g k x) -> p k g x", g=4, k=2)[:, kvh, :, :],
                     rope_own[:, 0, j, :], rope_own[:, 1, j, :], 4, rtmp, rope_own)
            P.op("act", [psG], [gate[s]], lambda: nc.scalar.activation(out=gate[s][:], in_=psG[:, 0:24], func=AF.Exp, scale=-1.0))
            P.op("dve", [gate[s]], [gate[s]], lambda: nc.vector.tensor_scalar(
                out=gate[s][:], in0=gate[s][:], scalar1=1.0, scalar2=None, op0=ALU.add))
            P.op("dve", [gate[s]], [gate[s]], lambda: nc.vector.reciprocal(out=gate[s][:], in_=gate[s][:]))
            transpose_to(qn_b[s], QnT[s], nchunks=4, evac="dve")

        def branch_evac(po_ap_fn, po_t, s, kvh, g, bi, width):
            hn = kvh * 4 + g
            rl = col()
            P.op("dve", [po_t], [rl], lambda: nc.vector.tensor_scalar(
                out=rl[:], in0=po_ap_fn(64, 65), scalar1=1e-30, scalar2=None, op0=ALU.max))
            P.op("dve", [rl], [rl], lambda: nc.vector.reciprocal(out=rl[:], in_=rl[:]))
            wc = col()
            P.op("dve", [rl, gate[s]], [wc], lambda: nc.vector.tensor_tensor(
                out=wc[:], in0=rl[:], in1=gate[s][:, hn * 3 + bi:hn * 3 + bi + 1], op=ALU.mult))
            dst = onsa[:, hn * 64:(hn + 1) * 64]
            if bi == 0:
                P.op("dve", [po_t, wc], [onsa], lambda: nc.vector.tensor_scalar(
                    out=dst, in0=po_ap_fn(0, 64), scalar1=wc[:, 0:1], scalar2=None, op0=ALU.mult))
            else:
                P.op("dve", [po_t, wc, onsa], [onsa], lambda: nc.vector.scalar_tensor_tensor(
                    out=dst, in0=po_ap_fn(0, 64), scalar=wc[:, 0:1], in1=dst, op0=ALU.mult, op1=ALU.add))
            return rl

        def b3_tasks(j):
            s = j % 2
            par = j % 2
            nkt = 2 * j + 2
            tasks = []
            for kvh in range(2):
                lo = 64 * kvh
                pts = []
                for nt in range(2):
                    ps = psS[gi[0] % 2]
                    pt = PT[pt_i[0] % 6]
                    gi[0] += 1
                    pt_i[0] += 1
                    pts.append(pt)

                    def s1(nt=nt, ps=ps, pt=pt, lo=lo):
                        P.op("pe", [KCT, QnT[s]], [ps], lambda: nc.tensor.matmul(
                            ps[:, 0:512], lhsT=KCT[lo:lo + 64, nt * 128:(nt + 1) * 128],
                            rhs=QnT[s][lo:lo + 64, :, :].rearrange("p a b -> p (a b)"), start=True, stop=True))
                        P.op("act", [ps], [pt], lambda: nc.scalar.activation(
                            out=pt[:].rearrange("p a b -> p (a b)"), in_=ps[:, 0:512], func=AF.Exp, scale=0.125))
                        P.op("dve", [pt, mcmp[s]], [pt], lambda: nc.vector.tensor_tensor(
                            out=pt[:], in0=pt[:], in1=mcmp[s][:, nt, :].unsqueeze(1).to_broadcast([128, 4, 128]), op=ALU.mult))
                    tasks.append((s1, noop))

                def s2c(kvh=kvh, pts=pts, lo=lo):
                    for g in range(4):
                        po_t = psOa if g < 2 else psOb
                        c0 = (g % 2) * 129
                        for nt in range(2):
                            P.op("pe", [pts[nt], VCX], [po_t], lambda nt=nt, g=g, po_t=po_t, c0=c0: nc.tensor.matmul(
                                po_t[:, c0:c0 + 129], lhsT=pts[nt][:, g, :], rhs=VCX[:, nt, kvh, :],
                                start=(nt == 0), stop=(nt == 1)), inc=(nt == 1))
                    for g in range(4):
                        po_t = psOa if g < 2 else psOb
                        c0 = (g % 2) * 129
                        rl = branch_evac(lambda a, b, po_t=po_t, c0=c0: po_t[:, c0 + a:c0 + b], po_t, s, kvh, g, 0, 129)
                        if g == 0:
                            P.op("dve", [po_t, rl], [psel], lambda po_t=po_t, c0=c0, rl=rl: nc.vector.tensor_scalar(
                                out=psel[:, kvh, :], in0=po_t[:, c0 + 65:c0 + 129], scalar1=rl[:, 0:1], scalar2=None, op0=ALU.mult))
                        else:
                            P.op("dve", [po_t, rl, psel], [psel], lambda po_t=po_t, c0=c0, rl=rl: nc.vector.scalar_tensor_tensor(
                                out=psel[:, kvh, :], in0=po_t[:, c0 + 65:c0 + 129], scalar=rl[:, 0:1], in1=psel[:, kvh, :],
                                op0=ALU.mult, op1=ALU.add))
                    P.op("dve", [psel, bonus[s]], [score], lambda: nc.vector.tensor_tensor(
                        out=score[:], in0=psel[:, kvh, :], in1=bonus[s][:], op=ALU.add))
                    P.op("dve", [score], [m8[0]], lambda: nc.vector.max(out=m8[0][:], in_=score[:]))
                    P.op("dve", [score, m8[0]], [swork], lambda: nc.vector.match_replace(
                        out=swork[:], in_to_replace=m8[0][:], in_values=score[:], imm_value=-1e9))
                    P.op("dve", [swork], [m8[1]], lambda: nc.vector.max(out=m8[1][:], in_=swork[:]))
                    P.op("dve", [score, m8[1]], [bm_b], lambda: nc.vector.tensor_scalar(
                        out=bm_b[:, lo:lo + 64], in0=score[:], scalar1=m8[1][:, 7:8], scalar2=None, op0=ALU.is_ge))
                    if debug and kvh == 1:
                        P.dma("sp", dbg_nsa[0, j * 128:(j + 1) * 128, :], onsa[:], reads=[onsa], is_output=True)
                        P.dma("sp", dbg_psel[j * 128:(j + 1) * 128, :], psel[:].rearrange("p a b -> p (a b)"), reads=[psel], is_output=True)
                        P.dma("sp", dbg_bm[j * 128:(j + 1) * 128, :], bm_b[:], reads=[bm_b], is_output=True)
                tasks.append((noop, s2c))
                tasks.append(None)

            def s_bmT():
                transpose_to(bm_b, bmT, nchunks=1, evac="dve")
            tasks.append(None)
            tasks.append((s_bmT, noop))

            for bi, (kidx, Vt) in ((1, (0, SV)), (2, (1, WV))):
                for kvh in range(2):
                    lo = 64 * kvh
                    kts = list(range(nkt)) if bi == 1 else [kt for kt in range(2 * j - 4, 2 * j + 2) if kt >= 0]
                    po_t = psOa if kvh == 0 else psOb
                    for kt in kts:
                        ps = psS[gi[0] % 2]
                        pt = PT[pt_i[0] % 6]
                        gi[0] += 1
                        pt_i[0] += 1
                        rr = kt - 2 * j
                        ms = msb[ms_i[0] % 2]
                        if bi == 1:
                            ms_i[0] += 1

                        def s1(bi=bi, kidx=kidx, kt=kt, ps=ps, pt=pt, lo=lo, rr=rr, ms=ms):
                            P.op("pe", [ST, QnT[s]], [ps], lambda: nc.tensor.matmul(
                                ps[:, 0:512], lhsT=ST[lo:lo + 64, kidx, kt * 128:(kt + 1) * 128],
                                rhs=QnT[s][lo:lo + 64, :, :].rearrange("p a b -> p (a b)"), start=True, stop=True))
                            if bi == 1:
                                P.op("pe", [expand, bmT], [psM], lambda: nc.tensor.matmul(
                                    psM[:, 0:128], lhsT=expand[lo:lo + 64, kt, :], rhs=bmT[lo:lo + 64, 0, :],
                                    start=True, stop=True))
                                if rr >= 0:
                                    P.op("dve", [psM, mask_c], [ms], lambda: nc.vector.tensor_tensor(
                                        out=ms[:], in0=psM[:, 0:128], in1=mask_c[:, par, rr, :], op=ALU.mult))
                                else:
                                    P.op("dve", [psM], [ms], lambda: nc.vector.tensor_copy(out=ms[:], in_=psM[:, 0:128]))
                            P.op("act", [ps], [pt], lambda: nc.scalar.activation(
                                out=pt[:].rearrange("p a b -> p (a b)"), in_=ps[:, 0:512], func=AF.Exp, scale=0.125))
                            if bi == 1:
                                P.op("dve", [pt, ms], [pt], lambda: nc.vector.tensor_tensor(
                                    out=pt[:], in0=pt[:], in1=ms[:].unsqueeze(1).to_broadcast([128, 4, 128]), op=ALU.mult))
                            elif rr not in (-2, -1):
                                P.op("dve", [pt, mask_w], [pt], lambda: nc.vector.tensor_tensor(
                                    out=pt[:], in0=pt[:], in1=mask_w[:, par, rr + 4, :].unsqueeze(1).to_broadcast([128, 4, 128]),
                                    op=ALU.mult))

                        def s2(bi=bi, Vt=Vt, kt=kt, kts=kts, pt=pt, kvh=kvh, po_t=po_t):
                            for g in range(4):
                                P.op("pe", [pt, Vt], [po_t], lambda g=g: nc.tensor.matmul(
                                    po_t[:, g * 65:(g + 1) * 65], lhsT=pt[:, g, :], rhs=Vt[:, kt, kvh, :],
                                    start=(kt == kts[0] and g == 0), stop=(kt == kts[-1]), skip_group_check=True),
                                    inc=(g == 3 and kt == kts[-1]))
                            if kt == kts[-1]:
                                for g in range(4):
                                    branch_evac(lambda a, b, g=g: po_t[:, g * 65 + a:g * 65 + b], po_t, s, kvh, g, bi, 65)
                                if debug and kvh == 1:
                                    P.dma("sp", dbg_nsa[bi, j * 128:(j + 1) * 128, :], onsa[:], reads=[onsa], is_output=True)
                        tasks.append((s1, s2))

            def s_fin():
                P.op("act", [odiff], [omix_b], lambda: nc.scalar.copy(out=omix_b[:, 0:512], in_=odiff[:, j, :]))
                P.op("act", [onsa], [omix_b], lambda: nc.scalar.copy(out=omix_b[:, 512:1024], in_=onsa[:]))
                if debug:
                    P.dma("sp", dbg_omix[j * 128:(j + 1) * 128, :], omix_b[:], reads=[omix_b], is_output=True)
                out_proj_and_delta(omix_b, omixT, wo, psY, lambda half: D1[:, j, half * 512:(half + 1) * 512], s, d1_t=D1)
            tasks.append((noop, s_fin))
            return tasks

        if n_qtiles > 0:
            b3_pre(0)
        for j in range(n_qtiles):
            tasks = b3_tasks(j)
            if j + 1 < n_qtiles:
                tasks.insert((2 * len(tasks)) // 3, (lambda j=j: b3_pre(j + 1), noop))
            pipeline(tasks)
        bst.__exit__(None, None, None)
        P.barrier()
        pst.__exit__(None, None, None)

    P.barrier()
    g1st.__exit__(None, None, None)
    if do_mlp:
        mst = contextlib.ExitStack()
        mst.__enter__()
        load_gains(("g_pre_mlp", "g_post_mlp"), mst)
        wup = P.sbuf("wup", [128, 8, D_FF], BF16, mst)
        wdn = P.sbuf("wdn", [128, 32, D_MODEL], BF16, mst)
        for kc in range(8):
            for q4 in range(4):
                P.dma("pool", wup[:, kc, q4 * 1024:(q4 + 1) * 1024], w_up3[:, kc, q4 * 1024:(q4 + 1) * 1024], writes=[wup])
        for fc in range(32):
            P.dma("pool", wdn[:, fc, :], w_dn3[:, fc, :], writes=[wdn])
        xm = [P.sbuf("xm%d" % i, [128, D_MODEL], F32, mst) for i in range(2)]
        hbm = [P.sbuf("hbm%d" % i, [128, D_MODEL], BF16, mst) for i in range(2)]
        hmT = [P.sbuf("hmT%d" % i, [128, 8, 128], BF16, mst) for i in range(2)]
        rl_sb = [P.sbuf("rl_sb%d" % i, [128, 256], F32, mst) for i in range(2)]
        hid = [P.sbuf("hid%d" % i, [128, 256], BF16, mst) for i in range(3)]
        ysb = [P.sbuf("ysb%d" % i, [128, 512], F32, mst) for i in range(2)]
        psU = [B[0], B[1]]
        psYm = [[B[2], B[3]], [B[4], B[5]]]
        tiles = [("p", j) for j in range(n_qtiles if do_prompt else 0)] + ([("s", 0)] if do_sample else [])
        pairs = [tiles[i:i + 2] for i in range(0, len(tiles), 2)]
        xi = [0]
        for pair in pairs:
            npair = len(pair)
            xs_ = []
            for ti, (kind, j) in enumerate(pair):
                x_t = xm[xi[0] % 2]
                xi[0] += 1
                xs_.append(x_t)
                src = x_own[j * 128:(j + 1) * 128, :] if kind == "p" else x_smp[:, :]
                P.dma("sp", x_t[:], src, writes=[x_t])
                d_t = D1 if kind == "p" else D1s
                d_ap = D1[:, j, :] if kind == "p" else D1s[:, :]
                P.op("dve", [x_t, d_t], [x_t], lambda x_t=x_t, d_ap=d_ap: nc.vector.tensor_tensor(
                    out=x_t[:], in0=x_t[:], in1=d_ap, op=ALU.add))
                front(x_t, hbm[ti], hmT[ti], ti, "g_pre_mlp")
            ntok = 128 * npair
            hi = [0]
            def mlp_up(fc):
                pu = psU[fc % 2]
                for ti in range(npair):
                    for kc in range(8):
                        P.op("pe", [wup, hmT[ti]], [pu], lambda fc=fc, ti=ti, kc=kc, pu=pu: nc.tensor.matmul(
                            pu[:, ti * 128:(ti + 1) * 128], lhsT=wup[:, kc, fc * 128:(fc + 1) * 128], rhs=hmT[ti][:, kc, :],
                            start=(kc == 0), stop=(kc == 7)), inc=(kc == 7 and ti == npair - 1))
                rl_ = rl_sb[fc % 2]
                hd = hid[fc % 3]
                P.op("act", [pu], [rl_], lambda pu=pu, rl_=rl_: nc.scalar.activation(
                    out=rl_[:, 0:ntok], in_=pu[:, 0:ntok], func=AF.Relu))
                P.op("dve", [rl_], [hd], lambda rl_=rl_, hd=hd: nc.vector.tensor_tensor(
                    out=hd[:, 0:ntok], in0=rl_[:, 0:ntok], in1=rl_[:, 0:ntok], op=ALU.mult))

            def mlp_down(fc):
                hd = hid[fc % 3]
                for ti in range(npair):
                    for half in range(2):
                        py = psYm[ti][half]
                        P.op("pe", [hd, wdn], [py], lambda fc=fc, ti=ti, half=half, py=py, hd=hd: nc.tensor.matmul(
                            py[:, 0:512], lhsT=hd[:, ti * 128:(ti + 1) * 128], rhs=wdn[:, fc, half * 512:(half + 1) * 512],
                            start=(fc == 0), stop=(fc == 31)), inc=(fc == 31))

            mlp_up(0)
            for fc in range(32):
                if fc + 1 < 32:
                    mlp_up(fc + 1)
                mlp_down(fc)
            for ti, (kind, j) in enumerate(pair):
                x_t = xs_[ti]
                py = psYm[ti]
                ssa, ssb, ln, rs = col(), col(), col(), col()
                P.op("act", [py[0]], [junk, ssa], lambda py=py, ssa=ssa: nc.scalar.activation(
                    out=junk[:, 0:512], in_=py[0][:, 0:512], func=AF.Square, accum_out=ssa[:]))
                P.op("act", [py[1]], [junk, ssb], lambda py=py, ssb=ssb: nc.scalar.activation(
                    out=junk[:, 0:512], in_=py[1][:, 0:512], func=AF.Square, accum_out=ssb[:]))
                P.op("dve", [ssa, ssb], [ssa], lambda ssa=ssa, ssb=ssb: nc.vector.tensor_tensor(
                    out=ssa[:], in0=ssa[:], in1=ssb[:], op=ALU.add))
                P.op("act", [ssa, eps_t], [ln], lambda ssa=ssa, ln=ln: nc.scalar.activation(
                    out=ln[:], in_=ssa[:], func=AF.Ln, bias=eps_t[:], scale=1.0 / D_MODEL))
                P.op("act", [ln], [rs], lambda ln=ln, rs=rs: nc.scalar.activation(out=rs[:], in_=ln[:], func=AF.Exp, scale=-0.5))
                for half in range(2):
                    y_t = ysb[half]
                    P.op("dve", [py[half], rs, gb["g_post_mlp"]], [y_t], lambda half=half, py=py, rs=rs, y_t=y_t: nc.vector.scalar_tensor_tensor(
                        out=y_t[:], in0=py[half][:, 0:512], scalar=rs[:, 0:1],
                        in1=gb["g_post_mlp"][:, half * 512:(half + 1) * 512], op0=ALU.mult, op1=ALU.mult))
                    P.op("dve", [y_t, x_t], [x_t], lambda half=half, y_t=y_t, x_t=x_t: nc.vector.tensor_tensor(
                        out=x_t[:, half * 512:(half + 1) * 512], in0=y_t[:], in1=x_t[:, half * 512:(half + 1) * 512], op=ALU.add))
                dst = o_yp[j * 128:(j + 1) * 128, :] if kind == "p" else o_ys[:, :]
                P.dma("sp", dst, x_t[:], reads=[x_t], is_output=True)
        P.barrier()
        mst.__exit__(None, None, None)

    P.finish()


def make_in_maps(inp):
    f32 = lambda a: np.ascontiguousarray(np.asarray(a), dtype=np.float32)
    xp = f32(inp["x_prompt"])
    xs = f32(inp["x_sample"])
    shared = {
        "w_in": f32(inp["w_in"])[0], "w_out": f32(inp["w_out"])[0], "w_up": f32(inp["w_up"])[0],
        "w_down": f32(inp["w_down"])[0],
        "diff_subln": f32(inp["diff_subln"]), "cmp_pos": f32(inp["cmp_pos"])[0], "cmp_w1": f32(inp["cmp_w1"])[0],
        "cmp_w2": f32(inp["cmp_w2"])[0],
        "cache_d": f32(inp["cache_diff_kv"]).reshape(-1, 1024),
        "cache_n": f32(inp["cache_nsa_kv"]).reshape(-1, 512),
    }
    for n in ("g_pre_mix", "g_post_mix", "g_pre_mlp", "g_post_mlp", "lam_q1", "lam_k1", "lam_q2", "lam_k2"):
        shared[n] = f32(inp[n])
    pt = np.ascontiguousarray(np.asarray(inp["page_table"]), dtype=np.int32)
    win = f32(inp["state_nsa_win_kv"])[0].reshape(128, 512, 256)
    consts = [make_consts(0), make_consts(1)]
    maps = []
    for c in range(8):
        b, r = c // 2, c % 2
        G = own_tiles(r)
        m = dict(shared)
        m["x_all"] = xp[b]
        m["x_own"] = np.ascontiguousarray(xp[b].reshape(NT_ALL, 128, D_MODEL)[G].reshape(-1, D_MODEL))
        m["x_smp"] = np.ascontiguousarray(xs[NB_S * c:NB_S * (c + 1)].reshape(128, D_MODEL))
        m["ptab"] = np.ascontiguousarray(pt[NB_S * c:NB_S * (c + 1)].reshape(1, -1))
        m["win_st"] = np.ascontiguousarray(win[NB_S * c:NB_S * (c + 1)])
        for n, a in consts[r].items():
            m["c_" + n] = a
        maps.append(m)
    return maps


def assemble(results):
    yp = np.zeros((4, SEQ, D_MODEL), np.float32)
    ys = np.zeros((128, 8, D_MODEL), np.float32)
    dkv_p = np.zeros((1, 4, SEQ, 2, 4, 128), np.float32)
    nkv_p = np.zeros((1, 4, SEQ, 4, 2, 64), np.float32)
    wkv_p = np.zeros((1, 4, 512, 2, 2, 64), np.float32)
    dkv_s = np.zeros((1, 128, 8, 2, 4, 128), np.float32)
    nkv_s = np.zeros((1, 128, 8, 4, 2, 64), np.float32)
    wkv_s = np.zeros((1, 128, 512, 2, 2, 64), np.float32)
    for c in range(8):
        b, r = c // 2, c % 2
        res = results[c]
        G = own_tiles(r)
        ypv = yp[b].reshape(NT_ALL, 128, D_MODEL)
        ypv[G] = np.asarray(res["o_yp"]).reshape(NT_OWN, 128, D_MODEL)
        ys[NB_S * c:NB_S * (c + 1)] = np.asarray(res["o_ys"]).reshape(NB_S, 8, D_MODEL)
        half = slice(r * (SEQ // 2), (r + 1) * (SEQ // 2))
        dkv_p[0, b, half] = np.asarray(res["o_dkv_p"]).reshape(SEQ, 2, 4, 128)[half]
        nkv_p[0, b, half] = np.asarray(res["o_nkv_p"]).reshape(SEQ, 4, 2, 64)[half]
        if r == 1:
            wkv_p[0, b] = np.asarray(res["o_wkv_p"]).reshape(512, 2, 2, 64)
        dkv_s[0, NB_S * c:NB_S * (c + 1)] = np.asarray(res["o_dkv_s"]).reshape(NB_S, 8, 2, 4, 128)
        nkv_s[0, NB_S * c:NB_S * (c + 1)] = np.asarray(res["o_nkv_s"]).reshape(NB_S, 8, 4, 2, 64)
        wkv_s[0, NB_S * c:NB_S * (c + 1)] = np.asarray(res["o_wkv_s"]).reshape(NB_S, 512, 2, 2, 64)
    return (yp, ys, dkv_p, nkv_p, wkv_p, dkv_s, nkv_s, wkv_s)


_CACHE = {}


def kernel(**inputs):
    if "nc" not in _CACHE:
        _CACHE["nc"] = build_program()[0]
    nc = _CACHE["nc"]
    maps = make_in_maps(inputs)
    res = run_bass_kernel_spmd(nc, maps, core_ids=list(range(8)))
    return assemble(res.results)
```

```python
import contextlib
import math

import numpy as np
import ml_dtypes

import concourse.bass as bass
import concourse.mybir as mybir
from concourse.bass_utils import run_bass_kernel_spmd

F32 = mybir.dt.float32
BF16 = mybir.dt.bfloat16
I32 = mybir.dt.int32
ALU = mybir.AluOpType
AF = mybir.ActivationFunctionType

D_MODEL = 1024
SEQ = 4096
NT_ALL = SEQ // 128
NT_OWN = NT_ALL // 2
D_IN = 2840
D_FF = 4096
EPS = 1e-6
LAM_INIT = 0.8 - 0.6 * math.exp(-0.3 * 0)
N_CMP_P = 255
N_CMP_S = 127
NB_S = 16
PAST = 2048


class T:
    __slots__ = ("ap", "name", "w", "r")

    def __init__(self, ap, name):
        self.ap = ap
        self.name = name
        self.w = None
        self.r = {}

    def __getitem__(self, k):
        return self.ap[k]


class Prog:
    def __init__(self, nc, n_dma_sems=16):
        self.nc = nc
        self.es = contextlib.ExitStack()
        self.eng = {"pe": nc.tensor, "dve": nc.vector, "act": nc.scalar, "pool": nc.gpsimd, "sp": nc.sync}
        self.sem = {}
        for e in ("pe", "dve", "act", "pool"):
            self.sem[("e", e)] = self.es.enter_context(nc.semaphore("s_" + e))
        self.nd = n_dma_sems
        self.dq = {"sp": 0, "pool": 1}
        self.dcount = {"sp": 0, "pool": 0}
        for qi in range(2):
            for k in range(n_dma_sems):
                self.sem[("d", qi * n_dma_sems + k)] = self.es.enter_context(nc.semaphore("s_d%d_%d" % (qi, k)))
        self.cnt = {e: 0 for e in ("pe", "dve", "act", "pool")}
        self.dma_i = 0
        self.seen = {q: {} for q in self.eng}
        self.out_deps = {}
        self.n_inst = 0
        self.n_wait = 0
        self.pe_pending = False
        self.names = {}

    def sbuf(self, name, shape, dtype, stack=None):
        self.names[name] = self.names.get(name, 0) + 1
        if self.names[name] > 1:
            name = "%s__%d" % (name, self.names[name])
        t = (stack or self.es).enter_context(self.nc.sbuf_tensor(name, list(shape), dtype))
        return T(t[tuple(slice(None) for _ in shape)], name)

    def psum(self, name, shape, dtype):
        t = self.es.enter_context(self.nc.psum_tensor(name, list(shape), dtype))
        return t

    def _wait(self, q, key, val):
        if key == ("e", "pe") and q == "pe":
            return
        if self.seen[q].get(key, 0) >= val:
            return
        self.eng[q].wait_ge(self.sem[key], val)
        self.seen[q][key] = val
        self.n_wait += 1

    def _deps(self, q, reads, writes):
        for t in reads:
            if t.w is not None:
                self._wait(q, *t.w)
        for t in writes:
            if t.w is not None:
                self._wait(q, *t.w)
            for k, v in t.r.items():
                self._wait(q, k, v)

    def _mark(self, reads, writes, key, val):
        for t in reads:
            if t.r.get(key, 0) < val:
                t.r[key] = val
        for t in writes:
            t.w = (key, val)
            t.r = {}

    def op(self, e, reads, writes, fn, inc=True):
        self._deps(e, reads, writes)
        ins = fn()
        key = ("e", e)
        if inc:
            self.cnt[e] += 1
            ins.then_inc(self.sem[key], 1)
            val = self.cnt[e]
            if e == "pe":
                self.pe_pending = False
        else:
            assert e == "pe"
            val = self.cnt[e] + 1
            self.pe_pending = True
        self._mark(reads, writes, key, val)
        self.n_inst += 1
        return ins

    def dma(self, q, out, in_, reads=(), writes=(), is_output=False, indirect=None):
        qi = self.dq[q]
        ci = self.dcount[q]
        self.dcount[q] += 1
        k = qi * self.nd + ci % self.nd
        val = 16 * (ci // self.nd + 1)
        self.dma_i += 1
        key = ("d", k)
        if val > 16:
            self._wait(q, key, val - 16)
        self._deps(q, reads, writes)
        if indirect is not None:
            ins = self.eng[q].indirect_dma_start(out=out, out_offset=None, in_=in_, in_offset=indirect)
        else:
            ins = self.eng[q].dma_start(out=out, in_=in_)
        ins.then_inc(self.sem[key], 16)
        self._mark(reads, writes, key, val)
        if is_output:
            self.out_deps[key] = val
        self.n_inst += 1

    def barrier(self):
        assert not self.pe_pending
        for q in self.eng:
            for e in ("pe", "dve", "act", "pool"):
                if self.cnt[e] > 0:
                    self._wait(q, ("e", e), self.cnt[e])
            for qn, qi in self.dq.items():
                for k in range(self.nd):
                    n_used = (self.dcount[qn] - k + self.nd - 1) // self.nd
                    if n_used > 0:
                        self._wait(q, ("d", qi * self.nd + k), 16 * n_used)

    def finish(self):
        for key, val in self.out_deps.items():
            self._wait("sp", key, val)
        for e in ("pe", "dve", "act", "pool"):
            if self.cnt[e] > 0:
                self._wait("sp", ("e", e), self.cnt[e])


def _rope_tab(pos):
    half = 32
    inv = (10000.0 ** (-np.arange(half, dtype=np.float32) / half)).astype(np.float32)
    ang = pos.astype(np.float32)[:, None] * inv[None, :]
    return np.cos(ang).astype(np.float32), np.sin(ang).astype(np.float32)


def own_tiles(r):
    return [2 * j + (r if j % 2 == 0 else 1 - r) for j in range(NT_OWN)]


def _bf(a):
    return np.ascontiguousarray(a.astype(ml_dtypes.bfloat16))


def make_consts(r):
    c = {}
    c["ident_f"] = np.eye(128, dtype=np.float32)
    c["ident_b"] = _bf(np.eye(128, dtype=np.float32))
    cos, sin = _rope_tab(np.arange(SEQ))
    c["rope_all"] = np.ascontiguousarray(
        np.stack([cos.reshape(NT_ALL, 128, 32), sin.reshape(NT_ALL, 128, 32)], 0).transpose(2, 0, 1, 3))
    G = own_tiles(r)
    c["rope_own"] = np.ascontiguousarray(c["rope_all"][:, :, G, :])
    cs, ss = _rope_tab(PAST + (np.arange(128) % 8))
    c["rope_smp"] = np.ascontiguousarray(np.stack([cs, ss], 1))
    kk = np.arange(128)[:, None]
    qq = np.arange(128)[None, :]
    mc = np.zeros((128, 2, 2, 128), np.float32)
    mw = np.zeros((128, 2, 6, 128), np.float32)
    for par in range(2):
        delta = r if par == 0 else 1 - r
        for rr in range(2):
            mc[:, par, rr, :] = ((128 * rr + kk) <= (128 * delta + qq))
        for ri, rr in enumerate(range(-4, 2)):
            d = (128 * delta + qq) - (128 * rr + kk)
            mw[:, par, ri, :] = (d >= 0) & (d < 512)
    c["mask_c"] = _bf(mc)
    c["mask_w"] = _bf(mw)
    mcmp = np.zeros((128, NT_OWN, 2, 128), np.float32)
    bonus = np.zeros((128, NT_OWN, 64), np.float32)
    for j, g in enumerate(G):
        tpos = 128 * g + np.arange(128)
        for nt in range(2):
            n = 128 * nt + np.arange(128)
            mcmp[:, j, nt, :] = ((16 * n + 31)[:, None] <= tpos[None, :]) & (n < N_CMP_P)[:, None]
        cur = tpos // 64
        m = np.arange(64)[None, :]
        bonus[:, j, :] = 10.0 * (m == 0) + 20.0 * (m == cur[:, None]) + 30.0 * (m == cur[:, None] - 1)
    c["mask_cmp"] = _bf(mcmp)
    c["bonus"] = bonus
    n = np.arange(256)
    cs_ = n * 16
    ss_ = np.arange(64) * 64
    ov = ((cs_[:, None] < ss_[None, :] + 64) & (cs_[:, None] + 32 > ss_[None, :]) & (n < N_CMP_P)[:, None]).astype(np.float32)
    c["overlap"] = _bf(ov.reshape(2, 128, 64).transpose(1, 0, 2))
    E = np.zeros((64, NT_ALL, 128), np.float32)
    for kt in range(NT_ALL):
        for k in range(128):
            E[(128 * kt + k) // 64, kt, k] = 1.0
    c["expand"] = _bf(np.concatenate([E, E], 0))
    c["iota_p"] = np.arange(128, dtype=np.float32).reshape(128, 1)
    k = np.arange(128)
    mN = np.zeros((128, NB_S, 8), np.float32)
    for b in range(NB_S):
        mN[:, b, :] = ((k // 8) == b)[:, None] & ((k % 8)[:, None] <= np.arange(8)[None, :])
    c["maskN"] = _bf(mN)
    c["maskW0"] = _bf((k[:, None] > np.arange(8)[None, :]).astype(np.float32))
    Es = np.zeros((64, 17, 128), np.float32)
    for kt in range(16):
        for kk_ in range(128):
            Es[2 * kt + (kk_ // 64), kt, kk_] = 1.0
    Es[32, 16, :] = 1.0
    c["expand_s"] = _bf(Es)
    n_ = np.arange(128)[:, None]
    m_ = np.arange(64)[None, :]
    c["overlap_s"] = _bf(((n_ < N_CMP_S) & (16 * n_ < 64 * m_ + 64) & (16 * n_ + 32 > 64 * m_) & (m_ < 33)).astype(np.float32))
    bs = np.zeros((16, 64), np.float32)
    bs[:, 0] = 10.0
    bs[:, 31] = 30.0
    bs[:, 32] = 20.0
    bs[:, 33:] = -1e9
    c["bonus_s"] = bs
    c["mask127"] = (np.arange(128) < N_CMP_S).astype(np.float32).reshape(128, 1)
    c["ones127"] = _bf(np.repeat(c["mask127"], 128, axis=1))
    return c


CONST_DT = {"ident_f": F32, "ident_b": BF16, "rope_all": F32, "rope_own": F32, "rope_smp": F32,
            "mask_c": BF16, "mask_w": BF16, "mask_cmp": BF16, "bonus": F32, "overlap": BF16,
            "expand": BF16, "iota_p": F32, "maskN": BF16, "maskW0": BF16, "expand_s": BF16, "overlap_s": BF16,
            "bonus_s": F32, "mask127": F32, "ones127": BF16}


def build_program(do_prompt=True, do_sample=True, do_mlp=True, n_kpass=NT_ALL, n_qtiles=NT_OWN, n_phys=2560, debug=False, n_sbatch=NB_S, sstop=99):
    nc = bass.Bass("TRN2", target_bir_lowering=False)
    P = Prog(nc)
    es = P.es
    with es:
        _emit(nc, P, do_prompt, do_sample, do_mlp, n_kpass, n_qtiles, n_phys, debug, n_sbatch, sstop)
    return nc, P


def _emit(nc, P, do_prompt, do_sample, do_mlp, n_kpass, n_qtiles, n_phys, debug=False, n_sbatch=NB_S, sstop=99):
    es = P.es

    def din(name, shape, dt=F32):
        return nc.dram_tensor(name, list(shape), dt, kind="ExternalInput").ap()

    def dout(name, shape, dt=F32):
        return nc.dram_tensor(name, list(shape), dt, kind="ExternalOutput").ap()

    x_all = din("x_all", [SEQ, D_MODEL])
    x_own = din("x_own", [SEQ // 2, D_MODEL])
    x_smp = din("x_smp", [128, D_MODEL])
    w_in = din("w_in", [D_MODEL, D_IN])
    w_out = din("w_out", [D_MODEL, D_MODEL])
    w_up = din("w_up", [D_MODEL, D_FF])
    w_down = din("w_down", [D_FF, D_MODEL])
    gains = {n: din(n, [1, D_MODEL]) for n in ("g_pre_mix", "g_post_mix", "g_pre_mlp", "g_post_mlp")}
    lam_in = {n: din(n, [1, 64]) for n in ("lam_q1", "lam_k1", "lam_q2", "lam_k2")}
    subln = din("diff_subln", [1, 128])
    cmp_pos = din("cmp_pos", [2, 32, 64])
    cmp_w1 = din("cmp_w1", [2, 2048, 128])
    cmp_w2 = din("cmp_w2", [2, 128, 64])
    cache_d = din("cache_d", [n_phys * 128, 1024])
    cache_n = din("cache_n", [n_phys * 128, 512])
    win_st = din("win_st", [NB_S, 512, 256])
    ptab = din("ptab", [1, NB_S * 16], I32)
    cshape = {"ident_f": [128, 128], "ident_b": [128, 128], "rope_all": [128, 2, NT_ALL, 32],
              "rope_own": [128, 2, NT_OWN, 32], "rope_smp": [128, 2, 32], "mask_c": [128, 2, 2, 128],
              "mask_w": [128, 2, 6, 128], "mask_cmp": [128, NT_OWN, 2, 128], "bonus": [128, NT_OWN, 64],
              "overlap": [128, 2, 64], "expand": [128, NT_ALL, 128], "iota_p": [128, 1],
              "maskN": [128, NB_S, 8], "maskW0": [128, 8], "expand_s": [64, 17, 128], "overlap_s": [128, 64],
              "bonus_s": [16, 64], "mask127": [128, 1], "ones127": [128, 128]}
    cdram = {n: din("c_" + n, s, CONST_DT[n]) for n, s in cshape.items()}

    o_yp = dout("o_yp", [SEQ // 2, D_MODEL])
    o_ys = dout("o_ys", [128, D_MODEL])
    o_dkv_p = dout("o_dkv_p", [SEQ, 1024])
    o_nkv_p = dout("o_nkv_p", [SEQ, 512])
    o_wkv_p = dout("o_wkv_p", [512, 256])
    o_dkv_s = dout("o_dkv_s", [128, 1024])
    o_nkv_s = dout("o_nkv_s", [128, 512])
    o_wkv_s = dout("o_wkv_s", [NB_S, 512, 256])
    if debug:
        dbg_omix = dout("dbg_omix", [NT_OWN * 128, 1024], BF16)
        dbg_nsa = dout("dbg_nsa", [3, NT_OWN * 128, 512])
        dbg_psel = dout("dbg_psel", [NT_OWN * 128, 128])
        dbg_bm = dout("dbg_bm", [NT_OWN * 128, 128], BF16)

    psb = [P.psum("psb%d" % i, [128, 512], F32) for i in range(7)]
    psT_t = P.psum("psT", [128, 1024], BF16)

    C = {}
    for n in ("ident_f", "ident_b", "iota_p"):
        C[n] = P.sbuf("sb_" + n, cshape[n], CONST_DT[n])
        P.dma("sp", C[n][:], cdram[n], writes=[C[n]])
    eps_t = P.sbuf("eps_t", [128, 1], F32)
    P.op("dve", [], [eps_t], lambda: nc.vector.memset(eps_t[:], EPS))
    gb = {}

    def load_gains(names, stack):
        for n in names:
            gb[n] = P.sbuf("gb_" + n, [128, D_MODEL], F32, stack)
            P.dma("sp", gb[n][:], gains[n][0:1, :].to_broadcast([128, D_MODEL]), writes=[gb[n]])

    B = [T(psb[i][:, :], "bank%d" % i) for i in range(7)]

    junk = P.sbuf("junk", [128, D_MODEL], BF16)
    st_ss = [P.sbuf("st_ss%d" % i, [128, 1], F32) for i in range(2)]
    st_ln = [P.sbuf("st_ln%d" % i, [128, 1], F32) for i in range(2)]
    st_rs = [P.sbuf("st_rs%d" % i, [128, 1], F32) for i in range(2)]

    def rstd_of(src_t, src_ap, slot, n_feat=D_MODEL):
        ss, ln, rs = st_ss[slot], st_ln[slot], st_rs[slot]
        P.op("act", [src_t], [junk, ss],
             lambda: nc.scalar.activation(out=junk[:, 0:n_feat], in_=src_ap, func=AF.Square, accum_out=ss[:]))
        P.op("act", [ss, eps_t], [ln],
             lambda: nc.scalar.activation(out=ln[:], in_=ss[:], func=AF.Ln, bias=eps_t[:], scale=1.0 / n_feat))
        P.op("act", [ln], [rs],
             lambda: nc.scalar.activation(out=rs[:], in_=ln[:], func=AF.Exp, scale=-0.5))
        return rs

    psT = T(psT_t[:, :], "psT")

    def transpose_to(hb, hT, nchunks=8, evac="act"):
        for kc in range(nchunks):
            P.op("pe", [hb, C["ident_b"]], [psT],
                 lambda kc=kc: nc.tensor.transpose(out=psT[:, kc * 128:(kc + 1) * 128],
                                                   in_=hb[:, kc * 128:(kc + 1) * 128], identity=C["ident_b"][:]),
                 inc=(kc == nchunks - 1))
        if evac == "act":
            P.op("act", [psT], [hT], lambda: nc.scalar.copy(out=hT[:].rearrange("p a b -> p (a b)")[:, 0:nchunks * 128],
                                                             in_=psT[:, 0:nchunks * 128]))
        else:
            P.op("dve", [psT], [hT], lambda: nc.vector.tensor_copy(out=hT[:].rearrange("p a b -> p (a b)")[:, 0:nchunks * 128],
                                                                    in_=psT[:, 0:nchunks * 128]))

    def rope(eng, src_t, src3, dst_t, dst3, cos_ap, sin_ap, nh, tmp, tab_t):
        s4 = src3.rearrange("p h (two x) -> p h two x", two=2)
        d4 = dst3.rearrange("p h (two x) -> p h two x", two=2)
        cb = cos_ap.unsqueeze(1).to_broadcast([128, nh, 32])
        sb = sin_ap.unsqueeze(1).to_broadcast([128, nh, 32])
        x1, x2 = s4[:, :, 0, :], s4[:, :, 1, :]
        t1 = tmp[0][:, 0:nh * 32].rearrange("p (h x) -> p h x", x=32)
        t2 = tmp[1][:, 0:nh * 32].rearrange("p (h x) -> p h x", x=32)
        E = P.eng[eng]
        P.op(eng, [src_t, tab_t], [tmp[0]], lambda: E.tensor_tensor(out=t1, in0=x1, in1=cb, op=ALU.mult))
        P.op(eng, [src_t, tab_t], [tmp[1]], lambda: E.tensor_tensor(out=t2, in0=x2, in1=sb, op=ALU.mult))
        P.op(eng, [tmp[0], tmp[1]], [dst_t], lambda: E.tensor_tensor(out=d4[:, :, 0, :], in0=t1, in1=t2, op=ALU.subtract))
        P.op(eng, [src_t, tab_t], [tmp[0]], lambda: E.tensor_tensor(out=t1, in0=x1, in1=sb, op=ALU.mult))
        P.op(eng, [src_t, tab_t], [tmp[1]], lambda: E.tensor_tensor(out=t2, in0=x2, in1=cb, op=ALU.mult))
        P.op(eng, [tmp[0], tmp[1]], [dst_t], lambda: E.tensor_tensor(out=d4[:, :, 1, :], in0=t1, in1=t2, op=ALU.add))

    rtmp = [P.sbuf("rtmp%d" % i, [128, 256], F32) for i in range(2)]

    def load_w(name, dst, src3, eng="pool"):
        for kc in range(src3.shape[1]):
            P.dma(eng, dst[:, kc, :], src3[:, kc, :], writes=[dst])

    w_in3 = w_in.rearrange("(kc p) n -> p kc n", p=128)
    w_out3 = w_out.rearrange("(kc p) n -> p kc n", p=128)
    w_up3 = w_up.rearrange("(kc p) n -> p kc n", p=128)
    w_dn3 = w_down.rearrange("(kc p) n -> p kc n", p=128)

    def pipeline(tasks, depth=2):
        pend = []
        for t in tasks:
            if t is None:
                for p_ in pend:
                    p_[1]()
                pend = []
                continue
            t[0]()
            pend.append(t)
            if len(pend) > depth:
                pend.pop(0)[1]()
        for p_ in pend:
            p_[1]()

    noop = lambda: None

    lam_sb = {}
    for n in lam_in:
        lam_sb[n] = P.sbuf("sb_" + n, [128, 64], F32)
        P.dma("sp", lam_sb[n][:], lam_in[n][0:1, :].to_broadcast([128, 64]), writes=[lam_sb[n]])
    lam_s = [P.sbuf("lam_s%d" % i, [128, 1], F32) for i in range(2)]
    lam_e = [P.sbuf("lam_e%d" % i, [128, 1], F32) for i in range(2)]
    neg_lam = P.sbuf("neg_lam", [128, 1], F32)
    junk64 = P.sbuf("junk64", [128, 64], F32)
    for i, (a, b) in enumerate((("lam_q1", "lam_k1"), ("lam_q2", "lam_k2"))):
        P.op("dve", [lam_sb[a], lam_sb[b]], [junk64],
             lambda a=a, b=b: nc.vector.tensor_tensor(out=junk64[:], in0=lam_sb[a][:], in1=lam_sb[b][:], op=ALU.mult))
        P.op("act", [junk64], [junk64, lam_s[i]],
             lambda i=i: nc.scalar.activation(out=junk64[:], in_=junk64[:], func=AF.Copy, accum_out=lam_s[i][:]))
        P.op("act", [lam_s[i]], [lam_e[i]],
             lambda i=i: nc.scalar.activation(out=lam_e[i][:], in_=lam_s[i][:], func=AF.Exp))
    P.op("dve", [lam_e[0], lam_e[1]], [neg_lam],
         lambda: nc.vector.tensor_tensor(out=neg_lam[:], in0=lam_e[1][:], in1=lam_e[0][:], op=ALU.subtract))
    P.op("dve", [neg_lam], [neg_lam],
         lambda: nc.vector.tensor_scalar(out=neg_lam[:], in0=neg_lam[:], scalar1=-LAM_INIT, scalar2=None, op0=ALU.add))
    sg = P.sbuf("sg", [128, 128], F32)
    P.dma("sp", sg[:], subln[0:1, :].to_broadcast([128, 128]), writes=[sg])
    P.op("dve", [sg], [sg],
         lambda: nc.vector.tensor_scalar(out=sg[:], in0=sg[:], scalar1=1.0 - LAM_INIT, scalar2=None, op0=ALU.mult))

    col_ring = [P.sbuf("col%d" % i, [128, 1], F32) for i in range(12)]
    col_i = [0]

    def col():
        c_ = col_ring[col_i[0] % len(col_ring)]
        col_i[0] += 1
        return c_

    D1s = P.sbuf("D1s", [128, D_MODEL], BF16)
    P.op("pool", [], [D1s], lambda: nc.gpsimd.memset(D1s[:], 0.0))

    def front(xt_t, hb_t, hT_t, slot, gname):
        rs = rstd_of(xt_t, xt_t[:], slot)
        P.op("dve", [xt_t, rs, gb[gname]], [hb_t],
             lambda: nc.vector.scalar_tensor_tensor(out=hb_t[:], in0=xt_t[:], scalar=rs[:, 0:1],
                                                    in1=gb[gname][:], op0=ALU.mult, op1=ALU.mult))
        transpose_to(hb_t, hT_t)

    def proj(ps, n, hT_t, w_t, c0):
        for kc in range(8):
            P.op("pe", [hT_t, w_t], [ps],
                 lambda kc=kc: nc.tensor.matmul(ps[:, 0:n], lhsT=hT_t[:, kc, :], rhs=w_t[:, kc, c0:c0 + n],
                                                start=(kc == 0), stop=(kc == 7)),
                 inc=(kc == 7))

    def out_proj_and_delta(omix_b, omixT, wo, psY, d1_ap_fn, slot, skip_transpose=False, d1_t=None):
        if not skip_transpose:
            transpose_to(omix_b, omixT)
        for half in range(2):
            proj(psY[half], 512, omixT, wo, half * 512)
        ssa, ssb = col(), col()
        P.op("act", [psY[0]], [junk, ssa],
             lambda: nc.scalar.activation(out=junk[:, 0:512], in_=psY[0][:, 0:512], func=AF.Square, accum_out=ssa[:]))
        P.op("act", [psY[1]], [junk, ssb],
             lambda: nc.scalar.activation(out=junk[:, 0:512], in_=psY[1][:, 0:512], func=AF.Square, accum_out=ssb[:]))
        P.op("dve", [ssa, ssb], [ssa], lambda: nc.vector.tensor_tensor(out=ssa[:], in0=ssa[:], in1=ssb[:], op=ALU.add))
        ln, rs = st_ln[slot], st_rs[slot]
        P.op("act", [ssa, eps_t], [ln],
             lambda: nc.scalar.activation(out=ln[:], in_=ssa[:], func=AF.Ln, bias=eps_t[:], scale=1.0 / D_MODEL))
        P.op("act", [ln], [rs], lambda: nc.scalar.activation(out=rs[:], in_=ln[:], func=AF.Exp, scale=-0.5))
        for half in range(2):
            P.op("dve", [psY[half], rs, gb["g_post_mix"]], [d1_t],
                 lambda half=half: nc.vector.scalar_tensor_tensor(
                     out=d1_ap_fn(half), in0=psY[half][:, 0:512], scalar=rs[:, 0:1],
                     in1=gb["g_post_mix"][:, half * 512:(half + 1) * 512], op0=ALU.mult, op1=ALU.mult))

    if do_sample:
        sst = contextlib.ExitStack()
        sst.__enter__()
        load_gains(("g_pre_mix", "g_post_mix"), sst)
        sc = {}
        for n in ("rope_smp", "maskN", "maskW0", "expand_s", "overlap_s", "bonus_s", "ones127"):
            sc[n] = P.sbuf("sb_" + n, cshape[n], CONST_DT[n], sst)
            P.dma("sp", sc[n][:], cdram[n], writes=[sc[n]])
        ones_bf = P.sbuf("ones_bf", [128, 128], BF16, sst)
        P.op("dve", [], [ones_bf], lambda: nc.vector.memset(ones_bf[:], 1.0))
        ones_f = P.sbuf("ones_f", [128, 128], F32, sst)
        P.op("dve", [], [ones_f], lambda: nc.vector.memset(ones_f[:], 1.0))
        sgT = P.sbuf("sgT", [128, 1], F32, sst)
        P.dma("sp", sgT[:], subln.rearrange("o e -> e o"), writes=[sgT])
        P.op("dve", [sgT], [sgT], lambda: nc.vector.tensor_scalar(
            out=sgT[:], in0=sgT[:], scalar1=1.0 - LAM_INIT, scalar2=None, op0=ALU.mult))
        pti = P.sbuf("pti", [128, NB_S * 16], I32, sst)
        ptf = P.sbuf("ptf", [128, NB_S * 16], F32, sst)
        idx = P.sbuf("idx", [128, NB_S * 16], I32, sst)
        P.dma("sp", pti[:], ptab[0:1, :].to_broadcast([128, NB_S * 16]), writes=[pti])
        P.op("dve", [pti], [ptf], lambda: nc.vector.tensor_copy(out=ptf[:], in_=pti[:]))
        P.op("dve", [ptf, C["iota_p"]], [ptf], lambda: nc.vector.tensor_scalar(
            out=ptf[:], in0=ptf[:], scalar1=128.0, scalar2=C["iota_p"][:, 0:1], op0=ALU.mult, op1=ALU.add))
        P.op("dve", [ptf], [idx], lambda: nc.vector.tensor_copy(out=idx[:], in_=ptf[:]))
        wo_s = P.sbuf("wo_s", [128, 8, 1024], BF16, sst)
        for kc in range(4):
            P.dma("pool", wo_s[:, kc, :], w_out3[:, kc, :], writes=[wo_s])
        for g in range(4):
            for kvh in range(2):
                r0 = 512 + (kvh * 4 + g) * 64
                P.dma("pool", wo_s[64 * kvh:64 * kvh + 64, 4 + g, :], w_out[r0:r0 + 64, :], writes=[wo_s])
        W1 = P.sbuf("W1s", [128, 2, 32, 128], BF16, sst)
        W2d = P.sbuf("W2ds", [128, 2, 128], BF16, sst)
        for w in range(2):
            for dup in range(2):
                P.dma("pool", W1[64 * dup:64 * dup + 64, w, :, :],
                      cmp_w1[w].rearrange("(l d) f -> d l f", d=64), writes=[W1])
                P.dma("pool", W2d[:, w, 64 * dup:64 * dup + 64], cmp_w2[w], writes=[W2d])
        pos_sb = P.sbuf("pos_sbs", [32, 2, 64], F32, sst)
        for w in range(2):
            P.dma("sp", pos_sb[:, w, :], cmp_pos[w], writes=[pos_sb])
        posT = P.sbuf("posTs", [64, 2, 32], BF16, sst)
        cbias = P.sbuf("cbiass", [128, 2], F32, sst)
        for w in range(2):
            P.op("pe", [pos_sb, C["ident_f"]], [B[3]], lambda w=w: nc.tensor.transpose(
                out=B[3][0:64, w * 32:(w + 1) * 32], in_=pos_sb[0:32, w, :], identity=C["ident_f"][0:32, 0:32]))
        P.op("dve", [B[3]], [posT], lambda: nc.vector.tensor_copy(
            out=posT[:].rearrange("p a b -> p (a b)"), in_=B[3][0:64, 0:64]))
        for w in range(2):
            for l in range(32):
                P.op("pe", [W1, posT], [B[4]], lambda w=w, l=l: nc.tensor.matmul(
                    B[4][:, w:w + 1], lhsT=W1[0:64, w, l, :], rhs=posT[0:64, w, l:l + 1],
                    start=(l == 0), stop=(l == 31)), inc=(l == 31))
        P.op("dve", [B[4]], [cbias], lambda: nc.vector.tensor_copy(out=cbias[:], in_=B[4][:, 0:2]))

        QdTblk = P.sbuf("QdTblk", [128, 4, NB_S, 2, 8], BF16, sst)
        QnT_s = P.sbuf("QnT_s", [128, 4, 128], BF16, sst)
        QnTblk = P.sbuf("QnTblk", [128, NB_S, 2, 4, 8], BF16, sst)
        KdTn = P.sbuf("KdTn", [128, 4, 128], BF16, sst)
        NTn = P.sbuf("NTn", [128, 2, 128], BF16, sst)
        vall_s = P.sbuf("vall_s", [128, 768], BF16, sst)
        GB = P.sbuf("GB", [128, 3, 8, 128], F32, sst)
        omixT_s = P.sbuf("omixT_s", [128, 8, 128], BF16, sst)
        P.op("dve", [], [QdTblk], lambda: nc.vector.memset(QdTblk[:], 0.0))
        P.op("dve", [], [omixT_s], lambda: nc.vector.memset(omixT_s[:], 0.0))

        s0 = contextlib.ExitStack()
        s0.__enter__()
        wS = P.sbuf("wS", [128, 8, D_IN], BF16, s0)
        for kc in range(8):
            P.dma("pool", wS[:, kc, :], w_in3[:, kc, :], writes=[wS])
        xs_t = P.sbuf("xs_t", [128, D_MODEL], F32, s0)
        hbs = P.sbuf("hbs", [128, D_MODEL], BF16, s0)
        hTs = P.sbuf("hTs", [128, 8, 128], BF16, s0)
        qd_s = P.sbuf("qd_s", [128, 512], BF16, s0)
        qn_s = P.sbuf("qn_s", [128, 512], BF16, s0)
        QdT_s = P.sbuf("QdT_s", [128, 4, 128], BF16, s0)
        kvd_s = P.sbuf("kvd_s", [128, 1024], F32, s0)
        kvn_s = P.sbuf("kvn_s", [128, 768], F32, s0)
        gate_s = P.sbuf("gate_s", [128, 24], F32, s0)
        tmpG = P.sbuf("tmpG", [128, 8, 128], F32, s0)
        P.dma("sp", xs_t[:], x_smp[:, :], writes=[xs_t])
        P.dma("sp", o_wkv_s[:, 0:504, :], win_st[:, 8:512, :], is_output=True)
        front(xs_t, hbs, hTs, 0, "g_pre_mix")
        for (bk, c0, n) in ((0, 0, 512), (1, 512, 512), (2, 1024, 512), (3, 1536, 512), (4, 2048, 512), (5, 2560, 256), (6, 2816, 24)):
            proj(B[bk], n, hTs, wS, c0)
        cos, sin = sc["rope_smp"][:, 0, :], sc["rope_smp"][:, 1, :]
        v64 = lambda ap: ap.rearrange("p (h x) -> p h x", x=64)
        rope("dve", B[0], v64(B[0][:, 0:512]), qd_s, v64(qd_s[:, :]), cos, sin, 8, rtmp, sc["rope_smp"])
        rope("dve", B[1], v64(B[1][:, 0:512]), kvd_s, v64(kvd_s[:, 0:512]), cos, sin, 8, rtmp, sc["rope_smp"])
        P.op("act", [B[2]], [kvd_s], lambda: nc.scalar.copy(out=kvd_s[:, 512:1024], in_=B[2][:, 0:512]))
        for kvh in range(2):
            rope("dve", B[3], v64(B[3][:, kvh * 256:(kvh + 1) * 256]), qn_s,
                 qn_s[:, :].rearrange("p (g k x) -> p k g x", g=4, k=2)[:, kvh, :, :], cos, sin, 4, rtmp, sc["rope_smp"])
        for slot in (0, 2):
            rope("dve", B[4], v64(B[4][:, slot * 128:(slot + 1) * 128]), kvn_s, v64(kvn_s[:, slot * 128:(slot + 1) * 128]),
                 cos, sin, 2, rtmp, sc["rope_smp"])
        rope("dve", B[5], v64(B[5][:, 0:128]), kvn_s, v64(kvn_s[:, 512:640]), cos, sin, 2, rtmp, sc["rope_smp"])
        for slot in (1, 3):
            P.op("act", [B[4]], [kvn_s], lambda slot=slot: nc.scalar.copy(
                out=kvn_s[:, slot * 128:(slot + 1) * 128], in_=B[4][:, slot * 128:(slot + 1) * 128]))
        P.op("act", [B[5]], [kvn_s], lambda: nc.scalar.copy(out=kvn_s[:, 640:768], in_=B[5][:, 128:256]))
        P.op("act", [B[6]], [gate_s], lambda: nc.scalar.activation(out=gate_s[:], in_=B[6][:, 0:24], func=AF.Exp, scale=-1.0))
        P.op("dve", [gate_s], [gate_s], lambda: nc.vector.tensor_scalar(
            out=gate_s[:], in0=gate_s[:], scalar1=1.0, scalar2=None, op0=ALU.add))
        P.op("dve", [gate_s], [gate_s], lambda: nc.vector.reciprocal(out=gate_s[:], in_=gate_s[:]))
        P.dma("sp", o_dkv_s[:, :], kvd_s[:], reads=[kvd_s], is_output=True)
        P.dma("sp", o_nkv_s[:, :], kvn_s[:, 0:512], reads=[kvn_s], is_output=True)
        for b in range(NB_S):
            P.dma("sp", o_wkv_s[b, 504:512, :], kvn_s[b * 8:(b + 1) * 8, 512:768], reads=[kvn_s], is_output=True)
        transpose_to(qd_s, QdT_s, nchunks=4, evac="dve")
        for c in range(2):
            lo = 64 * c
            P.op("dve", [QdT_s], [QdTblk], lambda c=c, lo=lo: nc.vector.tensor_copy(
                out=QdTblk[lo:lo + 64, :, :, c, :], in_=QdT_s[lo:lo + 64, :, :].rearrange("p h (b q) -> p h b q", q=8)))
        transpose_to(qn_s, QnT_s, nchunks=4, evac="dve")
        P.op("dve", [], [QnTblk], lambda: nc.vector.memset(QnTblk[:], 0.0))
        for kvh in range(2):
            lo = 64 * kvh
            P.op("dve", [QnT_s], [QnTblk], lambda kvh=kvh, lo=lo: nc.vector.tensor_copy(
                out=QnTblk[lo:lo + 64, :, kvh, :, :].rearrange("p b g q -> p g b q"),
                in_=QnT_s[lo:lo + 64, :, :].rearrange("p g (b q) -> p g b q", q=8)))
        for h in range(4):
            P.op("pe", [kvd_s, C["ident_f"]], [B[0]], lambda h=h: nc.tensor.transpose(
                out=B[0][:, h * 128:(h + 1) * 128], in_=kvd_s[:, h * 128:(h + 1) * 128], identity=C["ident_f"][:]),
                inc=(h == 3))
        P.op("act", [B[0]], [KdTn], lambda: nc.scalar.copy(out=KdTn[:].rearrange("p a b -> p (a b)"), in_=B[0][:, 0:512]))
        for wi, slot in enumerate((2, 4)):
            P.op("pe", [kvn_s, C["ident_f"]], [B[1]], lambda wi=wi, slot=slot: nc.tensor.transpose(
                out=B[1][:, wi * 128:(wi + 1) * 128], in_=kvn_s[:, slot * 128:(slot + 1) * 128], identity=C["ident_f"][:]),
                inc=(wi == 1))
        P.op("act", [B[1]], [NTn], lambda: nc.scalar.copy(out=NTn[:].rearrange("p a b -> p (a b)"), in_=B[1][:, 0:256]))
        P.op("dve", [kvd_s], [vall_s], lambda: nc.vector.tensor_copy(out=vall_s[:, 0:512], in_=kvd_s[:, 512:1024]))
        P.op("dve", [kvn_s], [vall_s], lambda: nc.vector.tensor_copy(out=vall_s[:, 512:640], in_=kvn_s[:, 384:512]))
        P.op("dve", [kvn_s], [vall_s], lambda: nc.vector.tensor_copy(out=vall_s[:, 640:768], in_=kvn_s[:, 640:768]))
        g3 = gate_s[:, :].rearrange("p (h b) -> p h b", b=3)
        for bi in range(3):
            P.op("dve", [gate_s, C["ident_f"]], [tmpG], lambda bi=bi: nc.vector.tensor_tensor(
                out=tmpG[:], in0=g3[:, :, bi].unsqueeze(2).to_broadcast([128, 8, 128]),
                in1=C["ident_f"][:, :].unsqueeze(1).to_broadcast([128, 8, 128]), op=ALU.mult))
            for half in range(2):
                P.op("pe", [ones_f, tmpG], [B[2 + half]], lambda half=half: nc.tensor.matmul(
                    B[2 + half][:, 0:512], lhsT=ones_f[:, :],
                    rhs=tmpG[:, half * 4:(half + 1) * 4, :].rearrange("p a b -> p (a b)"), start=True, stop=True))
                P.op("act", [B[2 + half]], [GB], lambda bi=bi, half=half: nc.scalar.copy(
                    out=GB[:, bi, half * 4:(half + 1) * 4, :].rearrange("p a b -> p (a b)"), in_=B[2 + half][:, 0:512]))
        P.barrier()
        s0.__exit__(None, None, None)

        pgd = [P.sbuf("pgd%d" % i, [128, 16, 1024], BF16, sst) for i in range(2)]
        pgn = P.sbuf("pgn", [128, 16, 512], BF16, sst)
        wst = P.sbuf("wst", [128, 4, 256], BF16, sst)
        KdT_b = P.sbuf("KdT_b", [128, 4, PAST], BF16, sst)
        CS_b = P.sbuf("CS_b", [128, 3, PAST], BF16, sst)
        WkT_b = P.sbuf("WkT_b", [128, 512], BF16, sst)
        PTd = P.sbuf("PTd", [128, 17, 64], BF16, sst)
        PsT = P.sbuf("PsT", [128, 17, 64], BF16, sst)
        PcT = P.sbuf("PcT", [128, 64], BF16, sst)
        msk_sb = P.sbuf("msk_sb", [128, 17, 16], BF16, sst)
        KCT_b = P.sbuf("KCT_b", [128, 128], BF16, sst)
        VC_b = P.sbuf("VC_b", [128, 128], BF16, sst)
        hcb = P.sbuf("hcbs", [128, 128], BF16, sst)
        P.op("dve", [], [hcb], lambda: nc.vector.memset(hcb[:], 0.0))
        gtmp = [P.sbuf("gtmps%d" % i, [128, 128], F32, sst) for i in range(2)]
        rl_sb = P.sbuf("rl_sbs", [128, 64], F32, sst)
        w_sb = P.sbuf("w_sbs", [128, 64], F32, sst)
        t_sb = [P.sbuf("t_sbs%d" % i, [128, 32], F32, sst) for i in range(3)]
        od_s = P.sbuf("od_s", [128, 32], F32, sst)
        accT = P.sbuf("accT", [128, 32], F32, sst)
        tmpP = P.sbuf("tmpP", [64, 64], F32, sst)
        pselT = P.sbuf("pselT", [64, 16], F32, sst)
        score = P.sbuf("score_s", [16, 64], F32, sst)
        swork = P.sbuf("swork_s", [16, 64], F32, sst)
        m8 = [P.sbuf("m8s_%d" % i, [16, 8], F32, sst) for i in range(2)]
        bm_s = P.sbuf("bm_s", [16, 64], BF16, sst)
        bmT2 = P.sbuf("bmT2", [64, 16], BF16, sst)

        def gather_d(b):
            for s_ in range(16):
                col_ = b * 16 + s_
                P.dma("pool", pgd[b % 2][:, s_, :], cache_d[:, :], reads=[idx], writes=[pgd[b % 2]],
                      indirect=bass.IndirectOffsetOnAxis(ap=idx[:, col_:col_ + 1], axis=0))

        def gather_n(b):
            for s_ in range(16):
                col_ = b * 16 + s_
                P.dma("pool", pgn[:, s_, :], cache_n[:, :], reads=[idx], writes=[pgn],
                      indirect=bass.IndirectOffsetOnAxis(ap=idx[:, col_:col_ + 1], axis=0))

        def gather_w(b):
            P.dma("pool", wst[:], win_st[b].rearrange("(t p) c -> p t c", p=128), writes=[wst])

        def gelu_s(src_ps, n, bias_col, dst_bf, tmpa, tmpb):
            P.op("dve", [src_ps, cbias], [tmpa], lambda: nc.vector.tensor_scalar(
                out=tmpa[:, 0:n], in0=src_ps[:, 0:n], scalar1=bias_col, scalar2=None, op0=ALU.add))
            P.op("dve", [tmpa], [tmpb], lambda: nc.vector.tensor_tensor(
                out=tmpb[:, 0:n], in0=tmpa[:, 0:n], in1=tmpa[:, 0:n], op=ALU.mult))
            P.op("dve", [tmpb], [tmpb], lambda: nc.vector.tensor_scalar(
                out=tmpb[:, 0:n], in0=tmpb[:, 0:n], scalar1=0.044715, scalar2=1.0, op0=ALU.mult, op1=ALU.add))
            P.op("dve", [tmpb, tmpa], [tmpb], lambda: nc.vector.tensor_tensor(
                out=tmpb[:, 0:n], in0=tmpb[:, 0:n], in1=tmpa[:, 0:n], op=ALU.mult))
            P.op("act", [tmpb], [tmpb], lambda: nc.scalar.activation(
                out=tmpb[:, 0:n], in_=tmpb[:, 0:n], func=AF.Exp, scale=-1.5957691216057308))
            P.op("dve", [tmpb], [tmpb], lambda: nc.vector.tensor_scalar(
                out=tmpb[:, 0:n], in0=tmpb[:, 0:n], scalar1=1.0, scalar2=None, op0=ALU.add))
            P.op("dve", [tmpb], [tmpb], lambda: nc.vector.reciprocal(out=tmpb[:, 0:n], in_=tmpb[:, 0:n]))
            P.op("dve", [tmpb, tmpa], [dst_bf], lambda: nc.vector.tensor_tensor(
                out=dst_bf[:, 0:n], in0=tmpb[:, 0:n], in1=tmpa[:, 0:n], op=ALU.mult))

        evi = [0]
        psT2 = T(psb[6][:, :].bitcast(BF16), "psT2")
        tr_i = [0]

        def tr_bank():
            k = tr_i[0] % 2
            tr_i[0] += 1
            return (psT, [psT]) if k == 0 else (psT2, [psT2, B[6]])

        def evac_copy(src_ts, src_ap, dst_t, dst_ap):
            if evi[0] % 2 == 0:
                P.op("act", src_ts, [dst_t], lambda: nc.scalar.copy(out=dst_ap, in_=src_ap))
            else:
                P.op("dve", src_ts, [dst_t], lambda: nc.vector.tensor_copy(out=dst_ap, in_=src_ap))
            evi[0] += 1

        def nsa_branch_evac(Y, Z, b, bi, first):
            P.op("dve", [Z], [rl_sb], lambda: nc.vector.tensor_scalar(
                out=rl_sb[:], in0=Z[:, 0:64], scalar1=1e-30, scalar2=None, op0=ALU.max))
            P.op("dve", [rl_sb], [rl_sb], lambda: nc.vector.reciprocal(out=rl_sb[:], in_=rl_sb[:]))
            P.op("dve", [rl_sb, GB], [w_sb], lambda: nc.vector.tensor_tensor(
                out=w_sb[:].rearrange("p (h q) -> p h q", q=8), in0=rl_sb[:].rearrange("p (h q) -> p h q", q=8),
                in1=GB[:, bi, :, b * 8:(b + 1) * 8], op=ALU.mult))
            dst = accT if first else t_sb[2]
            for kvh in range(2):
                lo = 64 * kvh
                P.op("dve", [Y, w_sb], [dst], lambda kvh=kvh, lo=lo: nc.vector.tensor_tensor(
                    out=dst[lo:lo + 64, :], in0=Y[lo:lo + 64, kvh * 32:(kvh + 1) * 32],
                    in1=w_sb[lo:lo + 64, kvh * 32:(kvh + 1) * 32], op=ALU.mult))
            if not first:
                P.op("dve", [accT, t_sb[2]], [accT], lambda: nc.vector.tensor_tensor(
                    out=accT[:], in0=accT[:], in1=t_sb[2][:], op=ALU.add))

        if n_sbatch > 0:
            gather_d(0)
            gather_n(0)
            gather_w(0)
        for b in range(n_sbatch):
            pg = pgd[b % 2]
            if sstop <= 1:
                continue
            for sp_ in range(8):
                pt_, pts_ = tr_bank()
                for s2 in range(2):
                    for h in range(4):
                        i = s2 * 4 + h
                        P.op("pe", [pg, C["ident_b"]], pts_, lambda sp_=sp_, s2=s2, h=h, i=i, pt_=pt_: nc.tensor.transpose(
                            out=pt_[:, i * 128:(i + 1) * 128], in_=pg[:, sp_ * 2 + s2, h * 128:(h + 1) * 128],
                            identity=C["ident_b"][:]), inc=(i == 7))
                evac_copy(pts_, pt_[:, :].rearrange("p (s h t) -> p s h t", s=2, h=4), KdT_b,
                          KdT_b[:, :, sp_ * 256:(sp_ + 1) * 256].rearrange("p h (s t) -> p s h t", s=2))
            if b + 1 < n_sbatch:
                pass
            if sstop <= 2:
                continue
            for kt in range(17):
                bank = B[kt // 8]
                for h in range(4):
                    lhs = KdT_b[:, h, kt * 128:(kt + 1) * 128] if kt < 16 else KdTn[:, h, :]
                    c0 = (kt % 8) * 64 + h * 16
                    P.op("pe", [KdT_b, KdTn, QdTblk], [bank], lambda bank=bank, lhs=lhs, c0=c0, h=h: nc.tensor.matmul(
                        bank[:, c0:c0 + 16], lhsT=lhs, rhs=QdTblk[:, h, b, :, :].rearrange("p a b -> p (a b)"),
                        start=True, stop=True), inc=(h == 3 and (kt % 8 == 7 or kt == 16)))
            for bk, k0, nk in ((0, 0, 8), (1, 8, 8), (2, 16, 1)):
                P.op("act", [B[bk]], [PTd], lambda bk=bk, k0=k0, nk=nk: nc.scalar.activation(
                    out=PTd[:, k0:k0 + nk, :].rearrange("p a b -> p (a b)"), in_=B[bk][:, 0:nk * 64], func=AF.Exp, scale=0.125))
            P.op("dve", [PTd, sc["maskN"]], [PTd], lambda: nc.vector.tensor_tensor(
                out=PTd[:, 16, :].rearrange("p (a q) -> p a q", q=8), in0=PTd[:, 16, :].rearrange("p (a q) -> p a q", q=8),
                in1=sc["maskN"][:, b, :].unsqueeze(1).to_broadcast([128, 8, 8]), op=ALU.mult))
            for h in range(4):
                for kt in range(17):
                    v = pg[:, kt, 512 + h * 128:512 + (h + 1) * 128] if kt < 16 else vall_s[:, h * 128:(h + 1) * 128]
                    P.op("pe", [pg, vall_s, PTd], [B[3]], lambda v=v, kt=kt, h=h: nc.tensor.matmul(
                        B[3][:, h * 16:(h + 1) * 16], lhsT=v, rhs=PTd[:, kt, h * 16:(h + 1) * 16],
                        start=(kt == 0), stop=(kt == 16)), inc=(kt == 16))
            for kt in range(17):
                P.op("pe", [ones_bf, PTd], [B[4]], lambda kt=kt: nc.tensor.matmul(
                    B[4][:, 0:64], lhsT=ones_bf[:, :], rhs=PTd[:, kt, :], start=(kt == 0), stop=(kt == 16)), inc=(kt == 16))
            if b + 1 < n_sbatch:
                gather_d(b + 1)
            P.op("dve", [B[4]], [rl_sb], lambda: nc.vector.reciprocal(out=rl_sb[:], in_=B[4][:, 0:64]))
            Yv = B[3][:, 0:64].rearrange("p (h c q) -> p h c q", h=4, c=2)
            rv = rl_sb[:, :].rearrange("p (h c q) -> p h c q", h=4, c=2)
            t1 = t_sb[0][:, :].rearrange("p (h q) -> p h q", q=8)
            t2 = t_sb[1][:, :].rearrange("p (h q) -> p h q", q=8)
            P.op("dve", [B[3], rl_sb], [t_sb[0]], lambda: nc.vector.tensor_tensor(out=t1, in0=Yv[:, :, 0, :], in1=rv[:, :, 0, :], op=ALU.mult))
            P.op("dve", [B[3], rl_sb], [t_sb[1]], lambda: nc.vector.tensor_tensor(out=t2, in0=Yv[:, :, 1, :], in1=rv[:, :, 1, :], op=ALU.mult))
            P.op("dve", [t_sb[0], t_sb[1], neg_lam], [od_s], lambda: nc.vector.scalar_tensor_tensor(
                out=od_s[:], in0=t_sb[1][:], scalar=neg_lam[:, 0:1], in1=t_sb[0][:], op0=ALU.mult, op1=ALU.add))
            P.op("dve", [od_s], [t_sb[0]], lambda: nc.vector.tensor_tensor(out=t_sb[0][:], in0=od_s[:], in1=od_s[:], op=ALU.mult))
            P.op("pe", [ones_f, t_sb[0]], [B[5]], lambda: nc.tensor.matmul(
                B[5][:, 0:32], lhsT=ones_f[:, :], rhs=t_sb[0][:, :], start=True, stop=True))
            P.op("act", [B[5], eps_t], [t_sb[1]], lambda: nc.scalar.activation(
                out=t_sb[1][:], in_=B[5][:, 0:32], func=AF.Ln, bias=eps_t[:], scale=1.0 / 128))
            P.op("act", [t_sb[1]], [t_sb[1]], lambda: nc.scalar.activation(out=t_sb[1][:], in_=t_sb[1][:], func=AF.Exp, scale=-0.5))
            P.op("dve", [od_s, sgT, t_sb[1]], [omixT_s], lambda: nc.vector.scalar_tensor_tensor(
                out=omixT_s[:, 0:4, b * 8:(b + 1) * 8], in0=od_s[:, :].rearrange("p (h q) -> p h q", q=8),
                scalar=sgT[:, 0:1], in1=t_sb[1][:, :].rearrange("p (h q) -> p h q", q=8), op0=ALU.mult, op1=ALU.mult))

            if sstop <= 3:
                continue
            for sp_ in range(8):
                pt_, pts_ = tr_bank()
                for s2 in range(2):
                    for w in range(3):
                        i = s2 * 3 + w
                        P.op("pe", [pgn, C["ident_b"]], pts_, lambda sp_=sp_, s2=s2, w=w, i=i, pt_=pt_: nc.tensor.transpose(
                            out=pt_[:, i * 128:(i + 1) * 128], in_=pgn[:, sp_ * 2 + s2, w * 128:(w + 1) * 128],
                            identity=C["ident_b"][:]), inc=(i == 5))
                evac_copy(pts_, pt_[:, 0:768].rearrange("p (s w t) -> p s w t", s=2, w=3), CS_b,
                          CS_b[:, :, sp_ * 256:(sp_ + 1) * 256].rearrange("p w (s t) -> p s w t", s=2))
            pt_, pts_ = tr_bank()
            for wt in range(4):
                P.op("pe", [wst, C["ident_b"]], pts_, lambda wt=wt, pt_=pt_: nc.tensor.transpose(
                    out=pt_[:, wt * 128:(wt + 1) * 128], in_=wst[:, wt, 0:128], identity=C["ident_b"][:]), inc=(wt == 3))
            evac_copy(pts_, pt_[:, 0:512], WkT_b, WkT_b[:, :])
            if sstop <= 4:
                continue
            for w in range(2):
                for kvh in range(2):
                    lo = 64 * kvh
                    for l in range(32):
                        P.op("pe", [W1, CS_b], [B[5]], lambda w=w, l=l, lo=lo: nc.tensor.matmul(
                            B[5][:, 0:N_CMP_S], lhsT=W1[lo:lo + 64, w, l, :],
                            rhs=CS_b[lo:lo + 64, w, l:l + 16 * (N_CMP_S - 1) + 1:16],
                            start=(l == 0), stop=(l == 31)), inc=(l == 31))
                    gelu_s(B[5], N_CMP_S, cbias[:, w:w + 1], hcb, gtmp[0], gtmp[1])
                    if w == 0:
                        P.op("pe", [W2d, hcb], [B[6]], lambda: nc.tensor.matmul(
                            B[6][:, 0:128], lhsT=W2d[:, 0, :], rhs=hcb[:, 0:128], start=True, stop=True))
                        P.op("dve", [B[6]], [KCT_b], lambda lo=lo: nc.vector.tensor_copy(
                            out=KCT_b[lo:lo + 64, :], in_=B[6][lo:lo + 64, 0:128]))
                    else:
                        P.op("pe", [W2d, hcb], [B[6]], lambda: nc.tensor.matmul(
                            B[6][:, 0:64], lhsT=hcb[:, 0:128], rhs=W2d[:, 1, 0:64], start=True, stop=True))
                        P.op("dve", [B[6]], [VC_b], lambda lo=lo: nc.vector.tensor_copy(out=VC_b[:, lo:lo + 64], in_=B[6][:, 0:64]))
            if sstop <= 5:
                continue
            P.op("pe", [KCT_b, QnTblk], [B[5]], lambda: nc.tensor.matmul(
                B[5][:, 0:64], lhsT=KCT_b[:, :], rhs=QnTblk[:, b, :, :, :].rearrange("p k g q -> p (k g q)"),
                start=True, stop=True))
            if sstop <= 5.2:
                continue
            P.op("act", [B[5]], [PcT], lambda: nc.scalar.activation(out=PcT[:], in_=B[5][:, 0:64], func=AF.Exp, scale=0.125))
            if sstop <= 5.4:
                continue
            P.op("pe", [VC_b, PcT], [B[3]], lambda: nc.tensor.matmul(B[3][:, 0:64], lhsT=VC_b[:, :], rhs=PcT[:, :], start=True, stop=True))
            P.op("pe", [sc["ones127"], PcT], [B[4]], lambda: nc.tensor.matmul(
                B[4][:, 0:64], lhsT=sc["ones127"][:, :], rhs=PcT[:, :], start=True, stop=True))
            P.op("pe", [sc["overlap_s"], PcT], [B[6]], lambda: nc.tensor.matmul(
                B[6][0:64, 0:64], lhsT=sc["overlap_s"][:, :], rhs=PcT[:, :], start=True, stop=True))
            if sstop <= 5.6:
                continue
            nsa_branch_evac(B[3], B[4], b, 0, True)
            if sstop <= 6:
                continue
            P.op("dve", [B[6], rl_sb], [tmpP], lambda: nc.vector.tensor_tensor(
                out=tmpP[:], in0=B[6][0:64, 0:64], in1=rl_sb[0:64, :], op=ALU.mult))
            tp = tmpP[:, :].rearrange("p (k g q) -> p k g q", k=2, g=4)
            pv = pselT[:, :].rearrange("p (k q) -> p k q", k=2)
            P.op("dve", [tmpP], [pselT], lambda: nc.vector.tensor_tensor(out=pv, in0=tp[:, :, 0, :], in1=tp[:, :, 1, :], op=ALU.add))
            P.op("dve", [tmpP, pselT], [pselT], lambda: nc.vector.tensor_tensor(out=pv, in0=pv, in1=tp[:, :, 2, :], op=ALU.add))
            P.op("dve", [tmpP, pselT], [pselT], lambda: nc.vector.tensor_tensor(out=pv, in0=pv, in1=tp[:, :, 3, :], op=ALU.add))
            P.op("pe", [pselT, C["ident_f"]], [B[6]], lambda: nc.tensor.transpose(
                out=B[6][0:16, 64:128], in_=pselT[0:64, :], identity=C["ident_f"][0:64, 0:64]))
            P.op("dve", [B[6], sc["bonus_s"]], [score], lambda: nc.vector.tensor_tensor(
                out=score[:], in0=B[6][0:16, 64:128], in1=sc["bonus_s"][:], op=ALU.add))
            P.op("dve", [score], [m8[0]], lambda: nc.vector.max(out=m8[0][:], in_=score[:]))
            P.op("dve", [score, m8[0]], [swork], lambda: nc.vector.match_replace(
                out=swork[:], in_to_replace=m8[0][:], in_values=score[:], imm_value=-2e9))
            P.op("dve", [swork], [m8[1]], lambda: nc.vector.max(out=m8[1][:], in_=swork[:]))
            P.op("dve", [score, m8[1]], [bm_s], lambda: nc.vector.tensor_scalar(
                out=bm_s[:], in0=score[:], scalar1=m8[1][:, 7:8], scalar2=None, op0=ALU.is_ge))
            P.op("pe", [bm_s, C["ident_b"]], [psT], lambda: nc.tensor.transpose(
                out=psT[0:64, 0:16], in_=bm_s[0:16, :], identity=C["ident_b"][0:16, 0:16]))
            P.op("dve", [psT], [bmT2], lambda: nc.vector.tensor_copy(out=bmT2[:], in_=psT[0:64, 0:16]))
            if sstop <= 7:
                continue
            for kt in range(17):
                bank = B[kt // 8]
                lhs = CS_b[:, 2, kt * 128:(kt + 1) * 128] if kt < 16 else NTn[:, 0, :]
                c0 = (kt % 8) * 64
                P.op("pe", [CS_b, NTn, QnTblk], [bank], lambda bank=bank, lhs=lhs, c0=c0: nc.tensor.matmul(
                    bank[:, c0:c0 + 64], lhsT=lhs, rhs=QnTblk[:, b, :, :, :].rearrange("p k g q -> p (k g q)"),
                    start=True, stop=True), inc=(kt % 8 == 7 or kt == 16))
            for kt in range(17):
                P.op("pe", [sc["expand_s"], bmT2], [B[5]], lambda kt=kt: nc.tensor.matmul(
                    B[5][:, kt * 16:(kt + 1) * 16], lhsT=sc["expand_s"][:, kt, :], rhs=bmT2[:, :], start=True, stop=True),
                    inc=(kt == 16))
            for bk, k0, nk in ((0, 0, 8), (1, 8, 8), (2, 16, 1)):
                P.op("act", [B[bk]], [PsT], lambda bk=bk, k0=k0, nk=nk: nc.scalar.activation(
                    out=PsT[:, k0:k0 + nk, :].rearrange("p a b -> p (a b)"), in_=B[bk][:, 0:nk * 64], func=AF.Exp, scale=0.125))
            P.op("dve", [B[5]], [msk_sb], lambda: nc.vector.tensor_copy(
                out=msk_sb[:].rearrange("p a b -> p (a b)"), in_=B[5][:, 0:272]))
            P.op("dve", [msk_sb, sc["maskN"]], [msk_sb], lambda: nc.vector.tensor_tensor(
                out=msk_sb[:, 16, :].rearrange("p (k q) -> p k q", k=2), in0=msk_sb[:, 16, :].rearrange("p (k q) -> p k q", k=2),
                in1=sc["maskN"][:, b, :].unsqueeze(1).to_broadcast([128, 2, 8]), op=ALU.mult))
            P.op("dve", [PsT, msk_sb], [PsT], lambda: nc.vector.tensor_tensor(
                out=PsT[:].rearrange("p t (k g q) -> p (t k) g q", k=2, g=4),
                in0=PsT[:].rearrange("p t (k g q) -> p (t k) g q", k=2, g=4),
                in1=msk_sb[:].rearrange("p t (k q) -> p (t k) q", k=2).unsqueeze(2).to_broadcast([128, 34, 4, 8]), op=ALU.mult))
            for kt in range(17):
                v = pgn[:, kt, 384:512] if kt < 16 else vall_s[:, 512:640]
                P.op("pe", [pgn, vall_s, PsT], [B[3]], lambda v=v, kt=kt: nc.tensor.matmul(
                    B[3][:, 0:64], lhsT=v, rhs=PsT[:, kt, :], start=(kt == 0), stop=(kt == 16)), inc=(kt == 16))
            for kt in range(17):
                P.op("pe", [ones_bf, PsT], [B[4]], lambda kt=kt: nc.tensor.matmul(
                    B[4][:, 0:64], lhsT=ones_bf[:, :], rhs=PsT[:, kt, :], start=(kt == 0), stop=(kt == 16)), inc=(kt == 16))
            if b + 1 < n_sbatch:
                gather_n(b + 1)
            nsa_branch_evac(B[3], B[4], b, 1, False)
            if sstop <= 8:
                continue
            for kt in range(5):
                lhs = WkT_b[:, kt * 128:(kt + 1) * 128] if kt < 4 else NTn[:, 1, :]
                c0 = kt * 64
                P.op("pe", [WkT_b, NTn, QnTblk], [B[0]], lambda lhs=lhs, c0=c0: nc.tensor.matmul(
                    B[0][:, c0:c0 + 64], lhsT=lhs, rhs=QnTblk[:, b, :, :, :].rearrange("p k g q -> p (k g q)"),
                    start=True, stop=True), inc=(kt == 4))
            P.op("act", [B[0]], [PsT], lambda: nc.scalar.activation(
                out=PsT[:, 0:5, :].rearrange("p a b -> p (a b)"), in_=B[0][:, 0:320], func=AF.Exp, scale=0.125))
            P.op("dve", [PsT, sc["maskW0"]], [PsT], lambda: nc.vector.tensor_tensor(
                out=PsT[:, 0, :].rearrange("p (a q) -> p a q", q=8), in0=PsT[:, 0, :].rearrange("p (a q) -> p a q", q=8),
                in1=sc["maskW0"][:, :].unsqueeze(1).to_broadcast([128, 8, 8]), op=ALU.mult))
            P.op("dve", [PsT, sc["maskN"]], [PsT], lambda: nc.vector.tensor_tensor(
                out=PsT[:, 4, :].rearrange("p (a q) -> p a q", q=8), in0=PsT[:, 4, :].rearrange("p (a q) -> p a q", q=8),
                in1=sc["maskN"][:, b, :].unsqueeze(1).to_broadcast([128, 8, 8]), op=ALU.mult))
            for kt in range(5):
                v = wst[:, kt, 128:256] if kt < 4 else vall_s[:, 640:768]
                P.op("pe", [wst, vall_s, PsT], [B[3]], lambda v=v, kt=kt: nc.tensor.matmul(
                    B[3][:, 0:64], lhsT=v, rhs=PsT[:, kt, :], start=(kt == 0), stop=(kt == 4)), inc=(kt == 4))
            for kt in range(5):
                P.op("pe", [ones_bf, PsT], [B[4]], lambda kt=kt: nc.tensor.matmul(
                    B[4][:, 0:64], lhsT=ones_bf[:, :], rhs=PsT[:, kt, :], start=(kt == 0), stop=(kt == 4)), inc=(kt == 4))
            if b + 1 < n_sbatch:
                gather_w(b + 1)
            nsa_branch_evac(B[3], B[4], b, 2, False)
            P.op("act", [accT], [omixT_s], lambda: nc.scalar.copy(
                out=omixT_s[:, 4:8, b * 8:(b + 1) * 8], in_=accT[:, :].rearrange("p (g q) -> p g q", q=8)))

        if n_sbatch < NB_S:
            pass
        out_proj_and_delta(None, omixT_s, wo_s, [B[0], B[1]], lambda half: D1s[:, half * 512:(half + 1) * 512], 0,
                           skip_transpose=True, d1_t=D1s)
        P.barrier()
        sst.__exit__(None, None, None)

    D1 = P.sbuf("D1", [128, NT_OWN, D_MODEL], BF16)
    P.op("pool", [], [D1], lambda: nc.gpsimd.memset(D1[:], 0.0))
    g1st = contextlib.ExitStack()
    g1st.__enter__()
    load_gains(("g_pre_mix", "g_post_mix"), g1st)

    if do_prompt:
        pst = contextlib.ExitStack()
        pst.__enter__()
        odiff = P.sbuf("odiff", [128, NT_OWN, 512], BF16, pst)
        P.op("pool", [], [odiff], lambda: nc.gpsimd.memset(odiff[:], 0.0))
        rope_own = P.sbuf("sb_rope_own", cshape["rope_own"], F32, pst)
        P.dma("sp", rope_own[:], cdram["rope_own"], writes=[rope_own])
        mask_c = P.sbuf("sb_mask_c", cshape["mask_c"], BF16, pst)
        P.dma("sp", mask_c[:], cdram["mask_c"], writes=[mask_c])
        xt = [P.sbuf("xt%d" % i, [128, D_MODEL], F32, pst) for i in range(2)]
        hb = [P.sbuf("hb%d" % i, [128, D_MODEL], BF16, pst) for i in range(2)]
        hT = [P.sbuf("hT%d" % i, [128, 8, 128], BF16, pst) for i in range(2)]
        PT = [P.sbuf("PT%d" % i, [128, 4, 128], BF16, pst) for i in range(6)]
        pt_i = [0]

        def load_x(src, g, slot):
            P.dma("sp", xt[slot][:], src[g * 128:(g + 1) * 128, :], writes=[xt[slot]])

        ast = contextlib.ExitStack()
        ast.__enter__()
        KdT = P.sbuf("KdT", [128, 4, SEQ], BF16, ast)
        Vd = P.sbuf("Vd", [128, NT_ALL, 4, 129], BF16, ast)
        P.op("pool", [], [Vd], lambda: nc.gpsimd.memset(Vd[:], 1.0))
        rope_all = P.sbuf("sb_rope_all", cshape["rope_all"], F32, ast)
        P.dma("sp", rope_all[:], cdram["rope_all"], writes=[rope_all])
        wA = P.sbuf("wA", [128, 8, 1536], BF16, ast)
        for kc in range(8):
            P.dma("pool", wA[:, kc, 0:1024], w_in3[:, kc, 512:1536], writes=[wA])
        for kc in range(8):
            P.dma("pool", wA[:, kc, 1024:1536], w_in3[:, kc, 0:512], writes=[wA])
        kvd = [P.sbuf("kvd%d" % i, [128, 1024], F32, ast) for i in range(2)]
        psA, psB, psKT = B[0], B[1], B[2]

        psA2 = [B[0], B[3]]
        psB2 = [B[1], B[4]]

        def a1_post(g):
            s = g % 2
            pA, pB = psA2[s], psB2[s]
            cos = rope_all[:, 0, g, :]
            sin = rope_all[:, 1, g, :]
            rope("dve", pA, pA[:, 0:512].rearrange("p (h x) -> p h x", x=64), kvd[s],
                 kvd[s][:, 0:512].rearrange("p (h x) -> p h x", x=64), cos, sin, 8, rtmp, rope_all)
            P.op("act", [pB], [kvd[s]], lambda: nc.scalar.copy(out=kvd[s][:, 512:1024], in_=pB[:, 0:512]))
            P.dma("sp", o_dkv_p[g * 128:(g + 1) * 128, :], kvd[s][:], reads=[kvd[s]], is_output=True)
            for h in range(4):
                P.op("pe", [kvd[s], C["ident_f"]], [psKT],
                     lambda h=h: nc.tensor.transpose(out=psKT[:, h * 128:(h + 1) * 128],
                                                     in_=kvd[s][:, h * 128:(h + 1) * 128], identity=C["ident_f"][:]),
                     inc=(h == 3))
            P.op("act", [psKT], [KdT], lambda: nc.scalar.copy(
                out=KdT[:, :, g * 128:(g + 1) * 128], in_=psKT[:, :].rearrange("p (h t) -> p h t", h=4)))
            P.op("dve", [kvd[s]], [Vd], lambda: nc.vector.tensor_copy(
                out=Vd[:, g, :, 0:128], in_=kvd[s][:, 512:1024].rearrange("p (h e) -> p h e", h=4)))

        if n_kpass > 0:
            load_x(x_all, 0, 0)
            front(xt[0], hb[0], hT[0], 0, "g_pre_mix")
        for g in range(n_kpass):
            s = g % 2
            if g + 1 < n_kpass:
                load_x(x_all, g + 1, (g + 1) % 2)
            proj(psA2[s], 512, hT[s], wA, 0)
            proj(psB2[s], 512, hT[s], wA, 512)
            if g + 1 < n_kpass:
                front(xt[(g + 1) % 2], hb[(g + 1) % 2], hT[(g + 1) % 2], (g + 1) % 2, "g_pre_mix")
            a1_post(g)

        qd_b = [P.sbuf("qd_b%d" % i, [128, 512], BF16, ast) for i in range(2)]
        QdT = [P.sbuf("QdT%d" % i, [128, 4, 128], BF16, ast) for i in range(2)]
        o1_sb = P.sbuf("o1_sb", [128, 128], F32, ast)
        od_sb = P.sbuf("od_sb", [128, 128], F32, ast)
        psQ = B[0]
        psS = [B[1], B[2]]
        psO = [B[3], B[4]]
        gi = [0]
        oi = [0]

        def a2_pre(j):
            s = j % 2
            load_x(x_own, j, s)
            front(xt[s], hb[s], hT[s], s, "g_pre_mix")
            proj(psQ, 512, hT[s], wA, 1024)
            rope("dve", psQ, psQ[:, 0:512].rearrange("p (h x) -> p h x", x=64), qd_b[s],
                 qd_b[s][:, :].rearrange("p (h x) -> p h x", x=64), rope_own[:, 0, j, :], rope_own[:, 1, j, :], 8, rtmp, rope_own)
            transpose_to(qd_b[s], QdT[s], nchunks=4, evac="dve")

        def a2_tasks(j):
            s = j % 2
            par = j % 2
            nkt = 2 * j + 2
            tasks = []
            for h in range(4):
                acc = {}
                for c in range(2):
                    po = psO[oi[0] % 2]
                    oi[0] += 1
                    acc[c] = po
                    groups = [list(range(a, min(a + 4, nkt))) for a in range(0, nkt, 4)]
                    for gidx, grp in enumerate(groups):
                        ps = psS[gi[0] % 2]
                        pt = PT[pt_i[0] % 6]
                        gi[0] += 1
                        pt_i[0] += 1

                        def s1(h=h, c=c, grp=grp, ps=ps, pt=pt):
                            for i, kt in enumerate(grp):
                                P.op("pe", [KdT, QdT[s]], [ps],
                                     lambda i=i, kt=kt: nc.tensor.matmul(
                                         ps[:, i * 128:(i + 1) * 128],
                                         lhsT=KdT[64 * c:64 * c + 64, h, kt * 128:(kt + 1) * 128],
                                         rhs=QdT[s][64 * c:64 * c + 64, h, :], start=True, stop=True),
                                     inc=(i == len(grp) - 1))
                            n = len(grp) * 128
                            P.op("act", [ps], [pt], lambda: nc.scalar.activation(
                                out=pt[:].rearrange("p a b -> p (a b)")[:, 0:n], in_=ps[:, 0:n], func=AF.Exp, scale=0.125))
                            if grp[-1] == nkt - 1:
                                i0 = len(grp) - 2
                                P.op("dve", [pt, mask_c], [pt], lambda: nc.vector.tensor_tensor(
                                    out=pt[:, i0:i0 + 2, :], in0=pt[:, i0:i0 + 2, :], in1=mask_c[:, par, :, :], op=ALU.mult))

                        def s2(h=h, c=c, grp=grp, pt=pt, po=po, acc=acc):
                            for i, kt in enumerate(grp):
                                P.op("pe", [pt, Vd], [po],
                                     lambda i=i, kt=kt: nc.tensor.matmul(
                                         po[:, 0:129], lhsT=pt[:, i, :], rhs=Vd[:, kt, h, :],
                                         start=(kt == 0), stop=(kt == nkt - 1)),
                                     inc=(kt == nkt - 1))
                            if grp[-1] != nkt - 1:
                                return
                            rl = col()
                            P.op("dve", [po], [rl], lambda: nc.vector.reciprocal(out=rl[:], in_=po[:, 128:129]))
                            if c == 0:
                                P.op("dve", [po, rl], [o1_sb], lambda: nc.vector.tensor_scalar(
                                    out=o1_sb[:], in0=po[:, 0:128], scalar1=rl[:, 0:1], scalar2=None, op0=ALU.mult))
                                return
                            P.op("dve", [rl, neg_lam], [rl], lambda: nc.vector.tensor_tensor(
                                out=rl[:], in0=rl[:], in1=neg_lam[:], op=ALU.mult))
                            P.op("dve", [po, rl, o1_sb], [od_sb], lambda: nc.vector.scalar_tensor_tensor(
                                out=od_sb[:], in0=po[:, 0:128], scalar=rl[:, 0:1], in1=o1_sb[:], op0=ALU.mult, op1=ALU.add))
                            ss, ln, rs = col(), col(), col()
                            P.op("act", [od_sb], [junk, ss], lambda: nc.scalar.activation(
                                out=junk[:, 0:128], in_=od_sb[:], func=AF.Square, accum_out=ss[:]))
                            P.op("act", [ss, eps_t], [ln], lambda: nc.scalar.activation(
                                out=ln[:], in_=ss[:], func=AF.Ln, bias=eps_t[:], scale=1.0 / 128))
                            P.op("act", [ln], [rs], lambda: nc.scalar.activation(out=rs[:], in_=ln[:], func=AF.Exp, scale=-0.5))
                            P.op("dve", [od_sb, rs, sg], [odiff], lambda: nc.vector.scalar_tensor_tensor(
                                out=odiff[:, j, h * 128:(h + 1) * 128], in0=od_sb[:], scalar=rs[:, 0:1], in1=sg[:],
                                op0=ALU.mult, op1=ALU.mult))

                        tasks.append((s1, s2))
            return tasks

        if n_qtiles > 0:
            a2_pre(0)
        for j in range(n_qtiles):
            tasks = a2_tasks(j)
            if j + 1 < n_qtiles:
                tasks.insert(len(tasks) // 2, (lambda j=j: a2_pre(j + 1), noop))
            pipeline(tasks)
        P.barrier()
        ast.__exit__(None, None, None)

        bst = contextlib.ExitStack()
        bst.__enter__()
        ST = P.sbuf("ST", [128, 2, SEQ], BF16, bst)
        SV = P.sbuf("SV", [128, NT_ALL, 2, 65], BF16, bst)
        WV = P.sbuf("WV", [128, NT_ALL, 2, 65], BF16, bst)
        P.op("pool", [], [SV], lambda: nc.gpsimd.memset(SV[:], 1.0))
        P.op("pool", [], [WV], lambda: nc.gpsimd.memset(WV[:], 1.0))
        KCT = P.sbuf("KCT", [128, 256], BF16, bst)
        VCX = P.sbuf("VCX", [128, 2, 2, 129], BF16, bst)
        P.op("pool", [], [VCX], lambda: nc.gpsimd.memset(VCX[:], 0.0))
        expand = P.sbuf("sb_expand", cshape["expand"], BF16, bst)
        P.dma("sp", expand[:], cdram["expand"], writes=[expand])
        mask_w = P.sbuf("sb_mask_w", cshape["mask_w"], BF16, bst)
        P.dma("sp", mask_w[:], cdram["mask_w"], writes=[mask_w])

        cst = contextlib.ExitStack()
        cst.__enter__()
        CT = P.sbuf("CT", [128, 2, SEQ], BF16, cst)
        rope_all = P.sbuf("sb_rope_all2", cshape["rope_all"], F32, cst)
        P.dma("sp", rope_all[:], cdram["rope_all"], writes=[rope_all])
        wkn = P.sbuf("wkn", [128, 8, 768], BF16, cst)
        for kc in range(8):
            P.dma("pool", wkn[:, kc, :], w_in3[:, kc, 2048:2816], writes=[wkn])
        W1 = P.sbuf("W1", [128, 2, 32, 128], BF16, cst)
        W2d = P.sbuf("W2d", [128, 2, 128], BF16, cst)
        for w in range(2):
            for dup in range(2):
                P.dma("pool", W1[64 * dup:64 * dup + 64, w, :, :],
                      cmp_w1[w].rearrange("(l d) f -> d l f", d=64), writes=[W1])
                P.dma("pool", W2d[:, w, 64 * dup:64 * dup + 64], cmp_w2[w], writes=[W2d])
        pos_sb = P.sbuf("pos_sb", [32, 2, 64], F32, cst)
        for w in range(2):
            P.dma("sp", pos_sb[:, w, :], cmp_pos[w], writes=[pos_sb])
        ovl = P.sbuf("sb_overlap", cshape["overlap"], BF16, cst)
        P.dma("sp", ovl[:], cdram["overlap"], writes=[ovl])
        kvn = [P.sbuf("kvn%d" % i, [128, 768], F32, cst) for i in range(2)]
        psC, psD, psNT = B[0], B[1], B[2]

        psC2 = [B[0], B[3]]
        psD2 = [B[1], B[4]]

        def b1_post(g):
            s = g % 2
            psC, psD = psC2[s], psD2[s]
            cos = rope_all[:, 0, g, :]
            sin = rope_all[:, 1, g, :]
            for slot in (0, 2):
                rope("dve", psC, psC[:, slot * 128:(slot + 1) * 128].rearrange("p (h x) -> p h x", x=64), kvn[s],
                     kvn[s][:, slot * 128:(slot + 1) * 128].rearrange("p (h x) -> p h x", x=64), cos, sin, 2, rtmp, rope_all)
            rope("dve", psD, psD[:, 0:128].rearrange("p (h x) -> p h x", x=64), kvn[s],
                 kvn[s][:, 512:640].rearrange("p (h x) -> p h x", x=64), cos, sin, 2, rtmp, rope_all)
            for slot in (1, 3):
                P.op("act", [psC], [kvn[s]], lambda slot=slot: nc.scalar.copy(
                    out=kvn[s][:, slot * 128:(slot + 1) * 128], in_=psC[:, slot * 128:(slot + 1) * 128]))
            P.op("act", [psD], [kvn[s]], lambda: nc.scalar.copy(out=kvn[s][:, 640:768], in_=psD[:, 128:256]))
            P.dma("sp", o_nkv_p[g * 128:(g + 1) * 128, :], kvn[s][:, 0:512], reads=[kvn[s]], is_output=True)
            if g >= NT_ALL - 4:
                gg = g - (NT_ALL - 4)
                P.dma("sp", o_wkv_p[gg * 128:(gg + 1) * 128, :], kvn[s][:, 512:768], reads=[kvn[s]], is_output=True)
            for wi, slot in enumerate((0, 1, 2, 4)):
                P.op("pe", [kvn[s], C["ident_f"]], [psNT],
                     lambda wi=wi, slot=slot: nc.tensor.transpose(
                         out=psNT[:, wi * 128:(wi + 1) * 128], in_=kvn[s][:, slot * 128:(slot + 1) * 128],
                         identity=C["ident_f"][:]),
                     inc=(wi == 3))
            P.op("act", [psNT], [CT], lambda: nc.scalar.copy(
                out=CT[:, :, g * 128:(g + 1) * 128], in_=psNT[:, 0:256].rearrange("p (h t) -> p h t", h=2)))
            P.op("act", [psNT], [ST], lambda: nc.scalar.copy(
                out=ST[:, :, g * 128:(g + 1) * 128], in_=psNT[:, 256:512].rearrange("p (h t) -> p h t", h=2)))
            P.op("dve", [kvn[s]], [SV], lambda: nc.vector.tensor_copy(
                out=SV[:, g, :, 0:64], in_=kvn[s][:, 384:512].rearrange("p (h e) -> p h e", h=2)))
            P.op("dve", [kvn[s]], [WV], lambda: nc.vector.tensor_copy(
                out=WV[:, g, :, 0:64], in_=kvn[s][:, 640:768].rearrange("p (h e) -> p h e", h=2)))

        if n_kpass > 0:
            load_x(x_all, 0, 0)
            front(xt[0], hb[0], hT[0], 0, "g_pre_mix")
        for g in range(n_kpass):
            s = g % 2
            if g + 1 < n_kpass:
                load_x(x_all, g + 1, (g + 1) % 2)
            proj(psC2[s], 512, hT[s], wkn, 0)
            proj(psD2[s], 256, hT[s], wkn, 512)
            if g + 1 < n_kpass:
                front(xt[(g + 1) % 2], hb[(g + 1) % 2], hT[(g + 1) % 2], (g + 1) % 2, "g_pre_mix")
            b1_post(g)

        posT = P.sbuf("posT", [64, 2, 32], BF16, cst)
        cbias = P.sbuf("cbias", [128, 2], F32, cst)
        psX = B[3]
        for w in range(2):
            P.op("pe", [pos_sb, C["ident_f"]], [psX], lambda w=w: nc.tensor.transpose(
                out=psX[0:64, w * 32:(w + 1) * 32], in_=pos_sb[0:32, w, :], identity=C["ident_f"][0:32, 0:32]))
        P.op("dve", [psX], [posT], lambda: nc.vector.tensor_copy(
            out=posT[:].rearrange("p a b -> p (a b)"), in_=psX[0:64, 0:64]))
        psX2 = B[4]
        for w in range(2):
            for l in range(32):
                P.op("pe", [W1, posT], [psX2], lambda w=w, l=l: nc.tensor.matmul(
                    psX2[:, w:w + 1], lhsT=W1[0:64, w, l, :], rhs=posT[0:64, w, l:l + 1],
                    start=(l == 0), stop=(l == 31)), inc=(l == 31))
        P.op("dve", [psX2], [cbias], lambda: nc.vector.tensor_copy(out=cbias[:], in_=psX2[:, 0:2]))

        def gelu_to(src_ps, n, bias_col, dst_bf, tmpa, tmpb):
            P.op("dve", [src_ps, cbias], [tmpa], lambda: nc.vector.tensor_scalar(
                out=tmpa[:, 0:n], in0=src_ps[:, 0:n], scalar1=bias_col, scalar2=None, op0=ALU.add))
            P.op("dve", [tmpa], [tmpb], lambda: nc.vector.tensor_tensor(
                out=tmpb[:, 0:n], in0=tmpa[:, 0:n], in1=tmpa[:, 0:n], op=ALU.mult))
            P.op("dve", [tmpb], [tmpb], lambda: nc.vector.tensor_scalar(
                out=tmpb[:, 0:n], in0=tmpb[:, 0:n], scalar1=0.044715, scalar2=1.0, op0=ALU.mult, op1=ALU.add))
            P.op("dve", [tmpb, tmpa], [tmpb], lambda: nc.vector.tensor_tensor(
                out=tmpb[:, 0:n], in0=tmpb[:, 0:n], in1=tmpa[:, 0:n], op=ALU.mult))
            P.op("act", [tmpb], [tmpb], lambda: nc.scalar.activation(
                out=tmpb[:, 0:n], in_=tmpb[:, 0:n], func=AF.Exp, scale=-1.5957691216057308))
            P.op("dve", [tmpb], [tmpb], lambda: nc.vector.tensor_scalar(
                out=tmpb[:, 0:n], in0=tmpb[:, 0:n], scalar1=1.0, scalar2=None, op0=ALU.add))
            P.op("dve", [tmpb], [tmpb], lambda: nc.vector.reciprocal(out=tmpb[:, 0:n], in_=tmpb[:, 0:n]))
            P.op("dve", [tmpb, tmpa], [dst_bf], lambda: nc.vector.tensor_tensor(
                out=dst_bf[:, 0:n], in0=tmpb[:, 0:n], in1=tmpa[:, 0:n], op=ALU.mult))

        gtmp = [P.sbuf("gtmp%d" % i, [128, 256], F32, cst) for i in range(2)]
        hcb = P.sbuf("hcb", [128, 256], BF16, cst)
        P.op("dve", [], [hcb], lambda: nc.vector.memset(hcb[:], 0.0))
        psH, psK = B[5], B[6]
        for w in range(2):
            for kvh in range(2):
                lo = 64 * kvh
                for l in range(32):
                    P.op("pe", [W1, CT], [psH], lambda w=w, l=l, lo=lo: nc.tensor.matmul(
                        psH[:, 0:N_CMP_P], lhsT=W1[lo:lo + 64, w, l, :],
                        rhs=CT[lo:lo + 64, w, l:l + 16 * (N_CMP_P - 1) + 1:16],
                        start=(l == 0), stop=(l == 31)), inc=(l == 31))
                gelu_to(psH, N_CMP_P, cbias[:, w:w + 1], hcb, gtmp[0], gtmp[1])
                if w == 0:
                    P.op("pe", [W2d, hcb], [psK], lambda: nc.tensor.matmul(
                        psK[:, 0:256], lhsT=W2d[:, 0, :], rhs=hcb[:, 0:256], start=True, stop=True))
                    P.op("dve", [psK], [KCT], lambda lo=lo: nc.vector.tensor_copy(
                        out=KCT[lo:lo + 64, :], in_=psK[lo:lo + 64, 0:256]))
                else:
                    for nt in range(2):
                        P.op("pe", [W2d, hcb], [psK], lambda nt=nt: nc.tensor.matmul(
                            psK[:, nt * 64:(nt + 1) * 64], lhsT=hcb[:, nt * 128:(nt + 1) * 128], rhs=W2d[:, 1, 0:64],
                            start=True, stop=True))
                    P.op("dve", [psK], [VCX], lambda kvh=kvh: nc.vector.tensor_copy(
                        out=VCX[:, :, kvh, 0:64], in_=psK[:, 0:128].rearrange("p (a b) -> p a b", a=2)))
        for kvh in range(2):
            P.op("dve", [ovl], [VCX], lambda kvh=kvh: nc.vector.tensor_copy(out=VCX[:, :, kvh, 65:129], in_=ovl[:, :, :]))
            P.op("dve", [], [VCX], lambda kvh=kvh: nc.vector.memset(VCX[:, :, kvh, 64:65], 1.0))
        P.barrier()
        cst.__exit__(None, None, None)

        wo = P.sbuf("wo", [128, 8, 1024], BF16, bst)
        wqn = P.sbuf("wqn", [128, 8, 536], BF16, bst)
        for kc in range(8):
            P.dma("pool", wqn[:, kc, 0:512], w_in3[:, kc, 1536:2048], writes=[wqn])
            P.dma("pool", wqn[:, kc, 512:536], w_in3[:, kc, 2816:2840], writes=[wqn])
        for kc in range(8):
            P.dma("pool", wo[:, kc, :], w_out3[:, kc, :], writes=[wo])
        qn_b = [P.sbuf("qn_b%d" % i, [128, 512], BF16, bst) for i in range(2)]
        QnT = [P.sbuf("QnT%d" % i, [128, 4, 128], BF16, bst) for i in range(2)]
        gate = [P.sbuf("gate%d" % i, [128, 24], F32, bst) for i in range(2)]
        mcmp = [P.sbuf("mcmp%d" % i, [128, 2, 128], BF16, bst) for i in range(2)]
        bonus = [P.sbuf("bonus%d" % i, [128, 64], F32, bst) for i in range(2)]
        onsa = P.sbuf("onsa", [128, 512], F32, bst)
        psel = P.sbuf("psel", [128, 2, 64], F32, bst)
        score = P.sbuf("score", [128, 64], F32, bst)
        swork = P.sbuf("swork", [128, 64], F32, bst)
        m8 = [P.sbuf("m8_%d" % i, [128, 8], F32, bst) for i in range(2)]
        bm_b = P.sbuf("bm_b", [128, 128], BF16, bst)
        bmT = P.sbuf("bmT", [128, 1, 128], BF16, bst)
        msb = [P.sbuf("msb%d" % i, [128, 128], BF16, bst) for i in range(2)]
        ms_i = [0]
        omix_b = P.sbuf("omix_b", [128, 1024], BF16, bst)
        omixT = P.sbuf("omixT", [128, 8, 128], BF16, bst)
        wcol = P.sbuf("wcol", [128, 8], F32, bst)
        psQn, psG = B[0], B[1]
        psS = [B[2], B[3]]
        psM, psOa, psOb = B[4], B[5], B[6]
        psY = [B[0], B[1]]
        first_branch = {}

        def b3_pre(j):
            s = j % 2
            load_x(x_own, j, s)
            P.dma("sp", mcmp[s][:], cdram["mask_cmp"][:, j, :, :], writes=[mcmp[s]])
            P.dma("sp", bonus[s][:], cdram["bonus"][:, j, :], writes=[bonus[s]])
            front(xt[s], hb[s], hT[s], s, "g_pre_mix")
            proj(psQn, 512, hT[s], wqn, 0)
            for kc in range(8):
                P.op("pe", [hT[s], wqn], [psG], lambda kc=kc: nc.tensor.matmul(
                    psG[:, 0:24], lhsT=hT[s][:, kc, :], rhs=wqn[:, kc, 512:536], start=(kc == 0), stop=(kc == 7)),
                    inc=(kc == 7))
            for kvh in range(2):
                rope("dve", psQn, psQn[:, kvh * 256:(kvh + 1) * 256].rearrange("p (h x) -> p h x", x=64), qn_b[s],
                     qn_b[s][:, :].rearrange("p (g k x) -> p k g x", g=4, k=2)[:, kvh, :, :],
                     rope_own[:, 0, j, :], rope_own[:, 1, j, :], 4, rtmp, rope_own)
            P.op("act", [psG], [gate[s]], lambda: nc.scalar.activation(out=gate[s][:], in_=psG[:, 0:24], func=AF.Exp, scale=-1.0))
            P.op("dve", [gate[s]], [gate[s]], lambda: nc.vector.tensor_scalar(
                out=gate[s][:], in0=gate[s][:], scalar1=1.0, scalar2=None, op0=ALU.add))
            P.op("dve", [gate[s]], [gate[s]], lambda: nc.vector.reciprocal(out=gate[s][:], in_=gate[s][:]))
            transpose_to(qn_b[s], QnT[s], nchunks=4, evac="dve")

        def branch_evac(po_ap_fn, po_t, s, kvh, g, bi, width):
            hn = kvh * 4 + g
            rl = col()
            P.op("dve", [po_t], [rl], lambda: nc.vector.tensor_scalar(
                out=rl[:], in0=po_ap_fn(64, 65), scalar1=1e-30, scalar2=None, op0=ALU.max))
            P.op("dve", [rl], [rl], lambda: nc.vector.reciprocal(out=rl[:], in_=rl[:]))
            wc = col()
            P.op("dve", [rl, gate[s]], [wc], lambda: nc.vector.tensor_tensor(
                out=wc[:], in0=rl[:], in1=gate[s][:, hn * 3 + bi:hn * 3 + bi + 1], op=ALU.mult))
            dst = onsa[:, hn * 64:(hn + 1) * 64]
            if bi == 0:
                P.op("dve", [po_t, wc], [onsa], lambda: nc.vector.tensor_scalar(
                    out=dst, in0=po_ap_fn(0, 64), scalar1=wc[:, 0:1], scalar2=None, op0=ALU.mult))
            else:
                P.op("dve", [po_t, wc, onsa], [onsa], lambda: nc.vector.scalar_tensor_tensor(
                    out=dst, in0=po_ap_fn(0, 64), scalar=wc[:, 0:1], in1=dst, op0=ALU.mult, op1=ALU.add))
            return rl

        def b3_tasks(j):
            s = j % 2
            par = j % 2
            nkt = 2 * j + 2
            tasks = []
            for kvh in range(2):
                lo = 64 * kvh
                pts = []
                for nt in range(2):
                    ps = psS[gi[0] % 2]
                    pt = PT[pt_i[0] % 6]
                    gi[0] += 1
                    pt_i[0] += 1
                    pts.append(pt)

                    def s1(nt=nt, ps=ps, pt=pt, lo=lo):
                        P.op("pe", [KCT, QnT[s]], [ps], lambda: nc.tensor.matmul(
                            ps[:, 0:512], lhsT=KCT[lo:lo + 64, nt * 128:(nt + 1) * 128],
                            rhs=QnT[s][lo:lo + 64, :, :].rearrange("p a b -> p (a b)"), start=True, stop=True))
                        P.op("act", [ps], [pt], lambda: nc.scalar.activation(
                            out=pt[:].rearrange("p a b -> p (a b)"), in_=ps[:, 0:512], func=AF.Exp, scale=0.125))
                        P.op("dve", [pt, mcmp[s]], [pt], lambda: nc.vector.tensor_tensor(
                            out=pt[:], in0=pt[:], in1=mcmp[s][:, nt, :].unsqueeze(1).to_broadcast([128, 4, 128]), op=ALU.mult))
                    tasks.append((s1, noop))

                def s2c(kvh=kvh, pts=pts, lo=lo):
                    for g in range(4):
                        po_t = psOa if g < 2 else psOb
                        c0 = (g % 2) * 129
                        for nt in range(2):
                            P.op("pe", [pts[nt], VCX], [po_t], lambda nt=nt, g=g, po_t=po_t, c0=c0: nc.tensor.matmul(
                                po_t[:, c0:c0 + 129], lhsT=pts[nt][:, g, :], rhs=VCX[:, nt, kvh, :],
                                start=(nt == 0), stop=(nt == 1)), inc=(nt == 1))
                    for g in range(4):
                        po_t = psOa if g < 2 else psOb
                        c0 = (g % 2) * 129
                        rl = branch_evac(lambda a, b, po_t=po_t, c0=c0: po_t[:, c0 + a:c0 + b], po_t, s, kvh, g, 0, 129)
                        if g == 0:
                            P.op("dve", [po_t, rl], [psel], lambda po_t=po_t, c0=c0, rl=rl: nc.vector.tensor_scalar(
                                out=psel[:, kvh, :], in0=po_t[:, c0 + 65:c0 + 129], scalar1=rl[:, 0:1], scalar2=None, op0=ALU.mult))
                        else:
                            P.op("dve", [po_t, rl, psel], [psel], lambda po_t=po_t, c0=c0, rl=rl: nc.vector.scalar_tensor_tensor(
                                out=psel[:, kvh, :], in0=po_t[:, c0 + 65:c0 + 129], scalar=rl[:, 0:1], in1=psel[:, kvh, :],
                                op0=ALU.mult, op1=ALU.add))
                    P.op("dve", [psel, bonus[s]], [score], lambda: nc.vector.tensor_tensor(
                        out=score[:], in0=psel[:, kvh, :], in1=bonus[s][:], op=ALU.add))
                    P.op("dve", [score], [m8[0]], lambda: nc.vector.max(out=m8[0][:], in_=score[:]))
                    P.op("dve", [score, m8[0]], [swork], lambda: nc.vector.match_replace(
                        out=swork[:], in_to_replace=m8[0][:], in_values=score[:], imm_value=-1e9))
                    P.op("dve", [swork], [m8[1]], lambda: nc.vector.max(out=m8[1][:], in_=swork[:]))
                    P.op("dve", [score, m8[1]], [bm_b], lambda: nc.vector.tensor_scalar(
                        out=bm_b[:, lo:lo + 64], in0=score[:], scalar1=m8[1][:, 7:8], scalar2=None, op0=ALU.is_ge))
                    if debug and kvh == 1:
                        P.dma("sp", dbg_nsa[0, j * 128:(j + 1) * 128, :], onsa[:], reads=[onsa], is_output=True)
                        P.dma("sp", dbg_psel[j * 128:(j + 1) * 128, :], psel[:].rearrange("p a b -> p (a b)"), reads=[psel], is_output=True)
                        P.dma("sp", dbg_bm[j * 128:(j + 1) * 128, :], bm_b[:], reads=[bm_b], is_output=True)
                tasks.append((noop, s2c))
                tasks.append(None)

            def s_bmT():
                transpose_to(bm_b, bmT, nchunks=1, evac="dve")
            tasks.append(None)
            tasks.append((s_bmT, noop))

            for bi, (kidx, Vt) in ((1, (0, SV)), (2, (1, WV))):
                for kvh in range(2):
                    lo = 64 * kvh
                    kts = list(range(nkt)) if bi == 1 else [kt for kt in range(2 * j - 4, 2 * j + 2) if kt >= 0]
                    po_t = psOa if kvh == 0 else psOb
                    for kt in kts:
                        ps = psS[gi[0] % 2]
                        pt = PT[pt_i[0] % 6]
                        gi[0] += 1
                        pt_i[0] += 1
                        rr = kt - 2 * j
                        ms = msb[ms_i[0] % 2]
                        if bi == 1:
                            ms_i[0] += 1

                        def s1(bi=bi, kidx=kidx, kt=kt, ps=ps, pt=pt, lo=lo, rr=rr, ms=ms):
                            P.op("pe", [ST, QnT[s]], [ps], lambda: nc.tensor.matmul(
                                ps[:, 0:512], lhsT=ST[lo:lo + 64, kidx, kt * 128:(kt + 1) * 128],
                                rhs=QnT[s][lo:lo + 64, :, :].rearrange("p a b -> p (a b)"), start=True, stop=True))
                            if bi == 1:
                                P.op("pe", [expand, bmT], [psM], lambda: nc.tensor.matmul(
                                    psM[:, 0:128], lhsT=expand[lo:lo + 64, kt, :], rhs=bmT[lo:lo + 64, 0, :],
                                    start=True, stop=True))
                                if rr >= 0:
                                    P.op("dve", [psM, mask_c], [ms], lambda: nc.vector.tensor_tensor(
                                        out=ms[:], in0=psM[:, 0:128], in1=mask_c[:, par, rr, :], op=ALU.mult))
                                else:
                                    P.op("dve", [psM], [ms], lambda: nc.vector.tensor_copy(out=ms[:], in_=psM[:, 0:128]))
                            P.op("act", [ps], [pt], lambda: nc.scalar.activation(
                                out=pt[:].rearrange("p a b -> p (a b)"), in_=ps[:, 0:512], func=AF.Exp, scale=0.125))
                            if bi == 1:
                                P.op("dve", [pt, ms], [pt], lambda: nc.vector.tensor_tensor(
                                    out=pt[:], in0=pt[:], in1=ms[:].unsqueeze(1).to_broadcast([128, 4, 128]), op=ALU.mult))
                            elif rr not in (-2, -1):
                                P.op("dve", [pt, mask_w], [pt], lambda: nc.vector.tensor_tensor(
                                    out=pt[:], in0=pt[:], in1=mask_w[:, par, rr + 4, :].unsqueeze(1).to_broadcast([128, 4, 128]),
                                    op=ALU.mult))

                        def s2(bi=bi, Vt=Vt, kt=kt, kts=kts, pt=pt, kvh=kvh, po_t=po_t):
                            for g in range(4):
                                P.op("pe", [pt, Vt], [po_t], lambda g=g: nc.tensor.matmul(
                                    po_t[:, g * 65:(g + 1) * 65], lhsT=pt[:, g, :], rhs=Vt[:, kt, kvh, :],
                                    start=(kt == kts[0] and g == 0), stop=(kt == kts[-1]), skip_group_check=True),
                                    inc=(g == 3 and kt == kts[-1]))
                            if kt == kts[-1]:
                                for g in range(4):
                                    branch_evac(lambda a, b, g=g: po_t[:, g * 65 + a:g * 65 + b], po_t, s, kvh, g, bi, 65)
                                if debug and kvh == 1:
                                    P.dma("sp", dbg_nsa[bi, j * 128:(j + 1) * 128, :], onsa[:], reads=[onsa], is_output=True)
                        tasks.append((s1, s2))

            def s_fin():
                P.op("act", [odiff], [omix_b], lambda: nc.scalar.copy(out=omix_b[:, 0:512], in_=odiff[:, j, :]))
                P.op("act", [onsa], [omix_b], lambda: nc.scalar.copy(out=omix_b[:, 512:1024], in_=onsa[:]))
                if debug:
                    P.dma("sp", dbg_omix[j * 128:(j + 1) * 128, :], omix_b[:], reads=[omix_b], is_output=True)
                out_proj_and_delta(omix_b, omixT, wo, psY, lambda half: D1[:, j, half * 512:(half + 1) * 512], s, d1_t=D1)
            tasks.append((noop, s_fin))
            return tasks

        if n_qtiles > 0:
            b3_pre(0)
        for j in range(n_qtiles):
            tasks = b3_tasks(j)
            if j + 1 < n_qtiles:
                tasks.insert((2 * len(tasks)) // 3, (lambda j=j: b3_pre(j + 1), noop))
            pipeline(tasks)
        bst.__exit__(None, None, None)
        P.barrier()
        pst.__exit__(None, None, None)

    P.barrier()
    g1st.__exit__(None, None, None)
    if do_mlp:
        mst = contextlib.ExitStack()
        mst.__enter__()
        load_gains(("g_pre_mlp", "g_post_mlp"), mst)
        wup = P.sbuf("wup", [128, 8, D_FF], BF16, mst)
        wdn = P.sbuf("wdn", [128, 32, D_MODEL], BF16, mst)
        for kc in range(8):
            for q4 in range(4):
                P.dma("pool", wup[:, kc, q4 * 1024:(q4 + 1) * 1024], w_up3[:, kc, q4 * 1024:(q4 + 1) * 1024], writes=[wup])
        for fc in range(32):
            P.dma("pool", wdn[:, fc, :], w_dn3[:, fc, :], writes=[wdn])
        xm = [P.sbuf("xm%d" % i, [128, D_MODEL], F32, mst) for i in range(2)]
        hbm = [P.sbuf("hbm%d" % i, [128, D_MODEL], BF16, mst) for i in range(2)]
        hmTc = P.sbuf("hmTc", [128, 8, 2, 128], BF16, mst)
        rl_sb = [P.sbuf("rl_sb%d" % i, [128, 256], F32, mst) for i in range(2)]
        hid = [P.sbuf("hid%d" % i, [128, 256], BF16, mst) for i in range(3)]
        ysb = [P.sbuf("ysb%d" % i, [128, 512], F32, mst) for i in range(2)]
        psU = [B[0], B[1]]
        psYm = [[B[2], B[3]], [B[4], B[5]]]
        tiles = [("p", j) for j in range(n_qtiles if do_prompt else 0)] + ([("s", 0)] if do_sample else [])
        pairs = [tiles[i:i + 2] for i in range(0, len(tiles), 2)]
        xi = [0]
        for pair in pairs:
            npair = len(pair)
            xs_ = []
            for ti, (kind, j) in enumerate(pair):
                x_t = xm[xi[0] % 2]
                xi[0] += 1
                xs_.append(x_t)
                src = x_own[j * 128:(j + 1) * 128, :] if kind == "p" else x_smp[:, :]
                P.dma("sp", x_t[:], src, writes=[x_t])
                d_t = D1 if kind == "p" else D1s
                d_ap = D1[:, j, :] if kind == "p" else D1s[:, :]
                P.op("dve", [x_t, d_t], [x_t], lambda x_t=x_t, d_ap=d_ap: nc.vector.tensor_tensor(
                    out=x_t[:], in0=x_t[:], in1=d_ap, op=ALU.add))
                rs_ = rstd_of(x_t, x_t[:], ti)
                P.op("dve", [x_t, rs_, gb["g_pre_mlp"]], [hbm[ti]], lambda x_t=x_t, rs_=rs_, ti=ti: nc.vector.scalar_tensor_tensor(
                    out=hbm[ti][:], in0=x_t[:], scalar=rs_[:, 0:1], in1=gb["g_pre_mlp"][:], op0=ALU.mult, op1=ALU.mult))
                for kc in range(8):
                    P.op("pe", [hbm[ti], C["ident_b"]], [psT], lambda kc=kc, ti=ti: nc.tensor.transpose(
                        out=psT[:, kc * 128:(kc + 1) * 128], in_=hbm[ti][:, kc * 128:(kc + 1) * 128],
                        identity=C["ident_b"][:]), inc=(kc == 7))
                P.op("act", [psT], [hmTc], lambda ti=ti: nc.scalar.copy(
                    out=hmTc[:, :, ti, :], in_=psT[:, :].rearrange("p (a b) -> p a b", b=128)))
            ntok = 128 * npair
            hi = [0]
            def mlp_up(fc):
                pu = psU[fc % 2]
                for kc in range(8):
                    P.op("pe", [wup, hmTc], [pu], lambda fc=fc, kc=kc, pu=pu: nc.tensor.matmul(
                        pu[:, 0:ntok], lhsT=wup[:, kc, fc * 128:(fc + 1) * 128],
                        rhs=hmTc[:, kc, 0:npair, :].rearrange("p a b -> p (a b)"),
                        start=(kc == 0), stop=(kc == 7)), inc=(kc == 7))
                rl_ = rl_sb[fc % 2]
                hd = hid[fc % 3]
                P.op("act", [pu], [rl_], lambda pu=pu, rl_=rl_: nc.scalar.activation(
                    out=rl_[:, 0:ntok], in_=pu[:, 0:ntok], func=AF.Relu))
                P.op("dve", [rl_], [hd], lambda rl_=rl_, hd=hd: nc.vector.tensor_tensor(
                    out=hd[:, 0:ntok], in0=rl_[:, 0:ntok], in1=rl_[:, 0:ntok], op=ALU.mult))

            def mlp_down(fc):
                hd = hid[fc % 3]
                for ti in range(npair):
                    for half in range(2):
                        py = psYm[ti][half]
                        P.op("pe", [hd, wdn], [py], lambda fc=fc, ti=ti, half=half, py=py, hd=hd: nc.tensor.matmul(
                            py[:, 0:512], lhsT=hd[:, ti * 128:(ti + 1) * 128], rhs=wdn[:, fc, half * 512:(half + 1) * 512],
                            start=(fc == 0), stop=(fc == 31)), inc=(fc == 31))

            mlp_up(0)
            for fc in range(32):
                if fc + 1 < 32:
                    mlp_up(fc + 1)
                mlp_down(fc)
            for ti, (kind, j) in enumerate(pair):
                x_t = xs_[ti]
                py = psYm[ti]
                ssa, ssb, ln, rs = col(), col(), col(), col()
                P.op("act", [py[0]], [junk, ssa], lambda py=py, ssa=ssa: nc.scalar.activation(
                    out=junk[:, 0:512], in_=py[0][:, 0:512], func=AF.Square, accum_out=ssa[:]))
                P.op("act", [py[1]], [junk, ssb], lambda py=py, ssb=ssb: nc.scalar.activation(
                    out=junk[:, 0:512], in_=py[1][:, 0:512], func=AF.Square, accum_out=ssb[:]))
                P.op("dve", [ssa, ssb], [ssa], lambda ssa=ssa, ssb=ssb: nc.vector.tensor_tensor(
                    out=ssa[:], in0=ssa[:], in1=ssb[:], op=ALU.add))
                P.op("act", [ssa, eps_t], [ln], lambda ssa=ssa, ln=ln: nc.scalar.activation(
                    out=ln[:], in_=ssa[:], func=AF.Ln, bias=eps_t[:], scale=1.0 / D_MODEL))
                P.op("act", [ln], [rs], lambda ln=ln, rs=rs: nc.scalar.activation(out=rs[:], in_=ln[:], func=AF.Exp, scale=-0.5))
                for half in range(2):
                    y_t = ysb[half]
                    P.op("dve", [py[half], rs, gb["g_post_mlp"]], [y_t], lambda half=half, py=py, rs=rs, y_t=y_t: nc.vector.scalar_tensor_tensor(
                        out=y_t[:], in0=py[half][:, 0:512], scalar=rs[:, 0:1],
                        in1=gb["g_post_mlp"][:, half * 512:(half + 1) * 512], op0=ALU.mult, op1=ALU.mult))
                    P.op("dve", [y_t, x_t], [x_t], lambda half=half, y_t=y_t, x_t=x_t: nc.vector.tensor_tensor(
                        out=x_t[:, half * 512:(half + 1) * 512], in0=y_t[:], in1=x_t[:, half * 512:(half + 1) * 512], op=ALU.add))
                dst = o_yp[j * 128:(j + 1) * 128, :] if kind == "p" else o_ys[:, :]
                P.dma("sp", dst, x_t[:], reads=[x_t], is_output=True)
        P.barrier()
        mst.__exit__(None, None, None)

    P.finish()


def make_in_maps(inp):
    f32 = lambda a: np.ascontiguousarray(np.asarray(a), dtype=np.float32)
    xp = f32(inp["x_prompt"])
    xs = f32(inp["x_sample"])
    shared = {
        "w_in": f32(inp["w_in"])[0], "w_out": f32(inp["w_out"])[0], "w_up": f32(inp["w_up"])[0],
        "w_down": f32(inp["w_down"])[0],
        "diff_subln": f32(inp["diff_subln"]), "cmp_pos": f32(inp["cmp_pos"])[0], "cmp_w1": f32(inp["cmp_w1"])[0],
        "cmp_w2": f32(inp["cmp_w2"])[0],
        "cache_d": f32(inp["cache_diff_kv"]).reshape(-1, 1024),
        "cache_n": f32(inp["cache_nsa_kv"]).reshape(-1, 512),
    }
    for n in ("g_pre_mix", "g_post_mix", "g_pre_mlp", "g_post_mlp", "lam_q1", "lam_k1", "lam_q2", "lam_k2"):
        shared[n] = f32(inp[n])
    pt = np.ascontiguousarray(np.asarray(inp["page_table"]), dtype=np.int32)
    win = f32(inp["state_nsa_win_kv"])[0].reshape(128, 512, 256)
    consts = [make_consts(0), make_consts(1)]
    maps = []
    for c in range(8):
        b, r = c // 2, c % 2
        G = own_tiles(r)
        m = dict(shared)
        m["x_all"] = xp[b]
        m["x_own"] = np.ascontiguousarray(xp[b].reshape(NT_ALL, 128, D_MODEL)[G].reshape(-1, D_MODEL))
        m["x_smp"] = np.ascontiguousarray(xs[NB_S * c:NB_S * (c + 1)].reshape(128, D_MODEL))
        m["ptab"] = np.ascontiguousarray(pt[NB_S * c:NB_S * (c + 1)].reshape(1, -1))
        m["win_st"] = np.ascontiguousarray(win[NB_S * c:NB_S * (c + 1)])
        for n, a in consts[r].items():
            m["c_" + n] = a
        maps.append(m)
    return maps


def assemble(results):
    yp = np.zeros((4, SEQ, D_MODEL), np.float32)
    ys = np.zeros((128, 8, D_MODEL), np.float32)
    dkv_p = np.zeros((1, 4, SEQ, 2, 4, 128), np.float32)
    nkv_p = np.zeros((1, 4, SEQ, 4, 2, 64), np.float32)
    wkv_p = np.zeros((1, 4, 512, 2, 2, 64), np.float32)
    dkv_s = np.zeros((1, 128, 8, 2, 4, 128), np.float32)
    nkv_s = np.zeros((1, 128, 8, 4, 2, 64), np.float32)
    wkv_s = np.zeros((1, 128, 512, 2, 2, 64), np.float32)
    for c in range(8):
        b, r = c // 2, c % 2
        res = results[c]
        G = own_tiles(r)
        ypv = yp[b].reshape(NT_ALL, 128, D_MODEL)
        ypv[G] = np.asarray(res["o_yp"]).reshape(NT_OWN, 128, D_MODEL)
        ys[NB_S * c:NB_S * (c + 1)] = np.asarray(res["o_ys"]).reshape(NB_S, 8, D_MODEL)
        half = slice(r * (SEQ // 2), (r + 1) * (SEQ // 2))
        dkv_p[0, b, half] = np.asarray(res["o_dkv_p"]).reshape(SEQ, 2, 4, 128)[half]
        nkv_p[0, b, half] = np.asarray(res["o_nkv_p"]).reshape(SEQ, 4, 2, 64)[half]
        if r == 1:
            wkv_p[0, b] = np.asarray(res["o_wkv_p"]).reshape(512, 2, 2, 64)
        dkv_s[0, NB_S * c:NB_S * (c + 1)] = np.asarray(res["o_dkv_s"]).reshape(NB_S, 8, 2, 4, 128)
        nkv_s[0, NB_S * c:NB_S * (c + 1)] = np.asarray(res["o_nkv_s"]).reshape(NB_S, 8, 4, 2, 64)
        wkv_s[0, NB_S * c:NB_S * (c + 1)] = np.asarray(res["o_wkv_s"]).reshape(NB_S, 512, 2, 2, 64)
    return (yp, ys, dkv_p, nkv_p, wkv_p, dkv_s, nkv_s, wkv_s)


_CACHE = {}


def kernel(**inputs):
    if "nc" not in _CACHE:
        _CACHE["nc"] = build_program()[0]
    nc = _CACHE["nc"]
    maps = make_in_maps(inputs)
    res = run_bass_kernel_spmd(nc, maps, core_ids=list(range(8)))
    return assemble(res.results)
```
